# Optimizing a Trainium2 kernel written in Bass

```python
import math
import jax
import jax.numpy as jnp
from jax import lax
import numpy as np

D_MODEL = 2048
BATCH = 4
SEQ = 4096
DEPTH = 4

GRID_W = 64
HEAD_DIM = 128
MIX_WIDTH = D_MODEL
EPS = 1e-6
A_WIDTH = MIX_WIDTH // 2
A_GROUPS = 8
A_GROUP_DIM = A_WIDTH // A_GROUPS
CHUNK = 128
B_WIDTH = MIX_WIDTH - A_WIDTH
HYENA_ORDER = 2
HYENA_EMB = 33
HYENA_BANDS = (HYENA_EMB - 1) // 2
HYENA_FFN = 64
HYENA_TARGET = 1e-2
HYENA_FAST = 0.3
HYENA_SLOW = 1.5
AB_IN = 2 * A_WIDTH + (HYENA_ORDER + 1) * B_WIDTH
C_WIDTH = MIX_WIDTH // 2
C_HEADS = C_WIDTH // HEAD_DIM
NA_KH = 8
NA_KW = 16
D_WIDTH = MIX_WIDTH - C_WIDTH
D_HEADS = 8
D_HALF = D_WIDTH // (2 * D_HEADS)
Q_BLOCK = 128
ROPE_THETA = 10000.0
CD_IN = 3 * C_WIDTH + 3 * D_WIDTH
D_FF = 256 * ((8 * D_MODEL // 3 + 255) // 256)
N_EVEN = (DEPTH + 1) // 2
N_ODD = DEPTH // 2

kernel_name = 'hybrid_gmlp_hyena_natten_diffattn_encoder'


def rms_norm(x, g):
    xf = x.astype(jnp.float32)
    y = xf * lax.rsqrt(jnp.mean(xf * xf, axis=-1, keepdims=True) + EPS)
    return (y * g.astype(jnp.float32)).astype(x.dtype)


def dwconv3(h, w, b):
    hp = jnp.pad(h, ((0, 0), (1, 1), (0, 0)))
    return hp[:, :-2] * w[0] + hp[:, 1:-1] * w[1] + hp[:, 2:] * w[2] + b


def rotary(x):
    L, dh = x.shape[1], x.shape[-1]
    half = dh // 2
    inv = 1.0 / (ROPE_THETA ** (jnp.arange(half, dtype=jnp.float32) * 2.0 / dh))
    ang = jnp.arange(L, dtype=jnp.float32)[:, None] * inv[None, :]
    cos = jnp.cos(ang)[None, :, None, None, :]
    sin = jnp.sin(ang)[None, :, None, None, :]
    xf = x.astype(jnp.float32)
    x1, x2 = xf[..., :half], xf[..., half:]
    return jnp.concatenate([x1 * cos - x2 * sin, x1 * sin + x2 * cos], axis=-1).astype(x.dtype)


def chunked_spatial_gating(p, v_gain, w_s, b_s):
    Bn, L, _ = p.shape
    u, v = jnp.split(jax.nn.gelu(p, approximate=False), 2, axis=-1)
    v = rms_norm(v, v_gain).reshape(Bn, L // CHUNK, CHUNK, A_GROUPS, A_GROUP_DIM)
    s = jnp.einsum('gpq,bcqgd->bcpgd', w_s, v) + b_s.T[None, None, :, :, None]
    return u * s.reshape(Bn, L, A_WIDTH)


def hyena_filter_spectrum(L, w1, b1, w2, b2, w3, freq):
    f32 = jnp.float32
    t = jnp.linspace(0.0, 1.0, L, dtype=f32)[:, None]
    w = 2.0 * math.pi * jnp.arange(L, dtype=f32)[:, None] / L
    bands = jnp.linspace(1e-4, HYENA_BANDS - 1, HYENA_BANDS, dtype=f32)[None, :]
    z = jnp.concatenate([t, jnp.cos(w * bands), -jnp.sin(w * bands)], axis=-1)
    fr = freq.astype(f32)
    h = jnp.sin(fr[0] * (z @ w1.astype(f32) + b1.astype(f32)))
    h = jnp.sin(fr[1] * (h @ w2.astype(f32) + b2.astype(f32)))
    h = (h @ w3.astype(f32)).reshape(L, HYENA_ORDER, 2, B_WIDTH)
    max_decay = math.log(HYENA_TARGET) / HYENA_FAST
    min_decay = math.log(HYENA_TARGET) / HYENA_SLOW
    deltas = jnp.abs(jnp.linspace(min_decay, max_decay, B_WIDTH, dtype=f32))
    h = h * jnp.exp(-t[:, :, None, None] * deltas)
    k = jnp.concatenate([h[:, :, 0], jnp.zeros((1, HYENA_ORDER, B_WIDTH), f32), h[:0:-1, :, 1]], axis=0)
    k = k / jnp.sum(jnp.abs(k), axis=0, keepdims=True)
    return jnp.fft.rfft(k, axis=0)


def long_conv(u, kf, skip):
    L = u.shape[1]
    uf32 = u.astype(jnp.float32)
    uf = jnp.fft.rfft(uf32, n=2 * L, axis=1)
    y = jnp.fft.irfft(uf * kf[None], n=2 * L, axis=1)[:, :L]
    return (y + uf32 * skip.astype(jnp.float32)).astype(u.dtype)


def hyena_mixer(p, conv_w, conv_b, kf, skip):
    p = dwconv3(p, conv_w, conv_b)
    v, x1, x2 = jnp.split(p, 3, axis=-1)
    z = x1 * long_conv(v, kf[:, 0], skip[0])
    return x2 * long_conv(z, kf[:, 1], skip[1])


def neighborhood_attention(q, k, v, rpb):
    Bn, L, H, dh = q.shape
    rows = L // GRID_W
    kh = min(NA_KH, rows)
    kw = NA_KW
    col = jnp.arange(GRID_W)
    col_start = jnp.clip(col - kw // 2, 0, GRID_W - kw)
    key_col = col_start[:, None] + jnp.arange(kw)[None, :]
    rel_col = key_col - col[:, None] + (NA_KW - 1)
    scale = dh ** -0.5

    def one_row(r):
        row_start = jnp.clip(r - kh // 2, 0, rows - kh)
        key_row = row_start + jnp.arange(kh)
        idx = (key_row[None, :, None] * GRID_W + key_col[:, None, :]).reshape(GRID_W, kh * kw)
        rel_row = key_row - r + (NA_KH - 1)
        bias = rpb[:, rel_row[None, :, None], rel_col[:, None, :]].reshape(H, GRID_W, kh * kw)
        qr = lax.dynamic_slice_in_dim(q, r * GRID_W, GRID_W, axis=1)
        kr = jnp.take(k, idx, axis=1)
        vr = jnp.take(v, idx, axis=1)
        s = jnp.einsum('bqhd,bqkhd->bhqk', qr, kr, preferred_element_type=jnp.float32) * scale
        pr = jax.nn.softmax(s + bias.astype(jnp.float32), axis=-1).astype(v.dtype)
        return jnp.einsum('bhqk,bqkhd->bqhd', pr, vr)

    out = lax.map(one_row, jnp.arange(rows))
    return out.transpose(1, 0, 2, 3, 4).reshape(Bn, L, H, dh)


def diff_attention(q, k, v, lam_params, subln, lam_init):
    Bn, L, H, _, dh = q.shape
    lp = lam_params.astype(jnp.float32)
    lam = jnp.exp(jnp.sum(lp[0] * lp[1])) - jnp.exp(jnp.sum(lp[2] * lp[3])) + lam_init
    scale = dh ** -0.5
    nblk = L // Q_BLOCK
    qb = q.reshape(Bn, nblk, Q_BLOCK, H, 2, dh).transpose(1, 0, 2, 3, 4, 5)

    def one_block(qblk):
        s = jnp.einsum('bqhmd,bkhmd->bhmqk', qblk, k, preferred_element_type=jnp.float32) * scale
        pr = jax.nn.softmax(s, axis=-1)
        a = pr[:, :, 0] - lam * pr[:, :, 1]
        return jnp.einsum('bhqk,bkhd->bqhd', a.astype(v.dtype), v)

    o = lax.map(one_block, qb)
    o = o.transpose(1, 0, 2, 3, 4).reshape(Bn, L, H, 2 * dh)
    return rms_norm(o, subln) * (1.0 - lam_init)


def conv_glu_ffn(h, w_up, conv_w, conv_b, w_down):
    a = dwconv3(h @ w_up, conv_w, conv_b)
    g, val = jnp.split(a, 2, axis=-1)
    return (jax.nn.silu(g) * val) @ w_down


def setup_inputs(seed: int = 0) -> dict:
    key = jax.random.key(seed)
    ks = iter(jax.random.split(key, 32))

    def nrm(shape, scale):
        return jax.random.normal(next(ks), shape, jnp.float32) * scale

    def gain(shape):
        return 1.0 + nrm(shape, 0.02)

    F = D_FF
    return {
        'x': nrm((BATCH, SEQ, D_MODEL), 1.0),
        'norm_mix': gain((DEPTH, D_MODEL)),
        'norm_ffn': gain((DEPTH, D_MODEL)),
        'w_out': nrm((DEPTH, MIX_WIDTH, D_MODEL), MIX_WIDTH ** -0.5),
        'ffn_up': nrm((DEPTH, D_MODEL, 2 * F), D_MODEL ** -0.5),
        'ffn_conv_w': nrm((DEPTH, 3, 2 * F), 3 ** -0.5),
        'ffn_conv_b': nrm((DEPTH, 2 * F), 0.02),
        'ffn_down': nrm((DEPTH, F, D_MODEL), F ** -0.5),
        'final_norm': gain((D_MODEL,)),
        'ab_w_in': nrm((N_EVEN, D_MODEL, AB_IN), D_MODEL ** -0.5),
        'a_vnorm': gain((N_EVEN, A_WIDTH)),
        'a_ws': nrm((N_EVEN, A_GROUPS, CHUNK, CHUNK), CHUNK ** -0.5),
        'a_bs': 1.0 + nrm((N_EVEN, A_GROUPS, CHUNK), 0.02),
        'b_conv_w': nrm((N_EVEN, 3, 3 * B_WIDTH), 3 ** -0.5),
        'b_conv_b': nrm((N_EVEN, 3 * B_WIDTH), 0.02),
        'b_filt_w1': nrm((N_EVEN, HYENA_EMB, HYENA_FFN), HYENA_EMB ** -0.5),
        'b_filt_b1': nrm((N_EVEN, HYENA_FFN), 0.02),
        'b_filt_w2': nrm((N_EVEN, HYENA_FFN, HYENA_FFN), HYENA_FFN ** -0.5),
        'b_filt_b2': nrm((N_EVEN, HYENA_FFN), 0.02),
        'b_filt_w3': nrm((N_EVEN, HYENA_FFN, HYENA_ORDER * 2 * B_WIDTH), HYENA_FFN ** -0.5),
        'b_filt_freq': gain((N_EVEN, 2, HYENA_FFN)),
        'b_skip': nrm((N_EVEN, HYENA_ORDER, B_WIDTH), 0.1),
        'cd_w_in': nrm((N_ODD, D_MODEL, CD_IN), D_MODEL ** -0.5),
        'c_rpb': nrm((N_ODD, C_HEADS, 2 * NA_KH - 1, 2 * NA_KW - 1), 0.02),
        'd_lambda': nrm((N_ODD, 4, D_HALF), 0.1),
        'd_subln': gain((N_ODD, 2 * D_HALF)),
    }


def reference(x, norm_mix, norm_ffn, w_out, ffn_up, ffn_conv_w, ffn_conv_b, ffn_down, final_norm,
              ab_w_in, a_vnorm, a_ws, a_bs, b_conv_w, b_conv_b, b_filt_w1, b_filt_b1, b_filt_w2,
              b_filt_b2, b_filt_w3, b_filt_freq, b_skip, cd_w_in, c_rpb, d_lambda, d_subln):
    Bn, L, _ = x.shape
    for l in range(DEPTH):
        i = l // 2
        h = rms_norm(x, norm_mix[l])
        if l % 2 == 0:
            p = h @ ab_w_in[i]
            ya = chunked_spatial_gating(p[..., :2 * A_WIDTH], a_vnorm[i], a_ws[i], a_bs[i])
            kf = hyena_filter_spectrum(L, b_filt_w1[i], b_filt_b1[i], b_filt_w2[i], b_filt_b2[i],
                                       b_filt_w3[i], b_filt_freq[i])
            yb = hyena_mixer(p[..., 2 * A_WIDTH:], b_conv_w[i], b_conv_b[i], kf, b_skip[i])
            y = jnp.concatenate([ya, yb], axis=-1)
        else:
            p = h @ cd_w_in[i]
            qkv_c = p[..., :3 * C_WIDTH].reshape(Bn, L, 3, C_HEADS, HEAD_DIM)
            yc = neighborhood_attention(qkv_c[:, :, 0], qkv_c[:, :, 1], qkv_c[:, :, 2], c_rpb[i])
            pd = p[..., 3 * C_WIDTH:]
            qd = pd[..., :D_WIDTH].reshape(Bn, L, D_HEADS, 2, D_HALF)
            kd = pd[..., D_WIDTH:2 * D_WIDTH].reshape(Bn, L, D_HEADS, 2, D_HALF)
            vd = pd[..., 2 * D_WIDTH:].reshape(Bn, L, D_HEADS, 2 * D_HALF)
            lam_init = 0.8 - 0.6 * math.exp(-0.3 * l)
            yd = diff_attention(rotary(qd), rotary(kd), vd, d_lambda[i], d_subln[i], lam_init)
            y = jnp.concatenate([yc.reshape(Bn, L, C_WIDTH), yd.reshape(Bn, L, D_WIDTH)], axis=-1)
        x = x + y @ w_out[l]
        x = x + conv_glu_ffn(rms_norm(x, norm_ffn[l]), ffn_up[l], ffn_conv_w[l], ffn_conv_b[l], ffn_down[l])
    return rms_norm(x, final_norm)
```

```python
import numpy as np
from contextlib import ExitStack
import ml_dtypes
import concourse.bass as bass
import concourse.mybir as mybir
from concourse.bass_utils import run_bass_kernel_spmd

F32 = mybir.dt.float32
BF16 = mybir.dt.bfloat16
AF = mybir.ActivationFunctionType
ALU = mybir.AluOpType
AX = mybir.AxisListType
NPBF = ml_dtypes.bfloat16

D = 2048
NT = 2048
L = 4096
FF = 5632
EPS = 1e-6
NCORE = 8


_LIVE_TOKS = []


class Tok:
    __slots__ = ("name", "w", "rs", "persist")

    def __init__(self, name="", persist=False, track=True):
        self.name = name
        self.w = None
        self.rs = []
        self.persist = persist
        if track:
            _LIVE_TOKS.append(self)


class _Op:
    __slots__ = ("eng", "emit", "deps", "dma", "sig", "sem", "val", "cc", "idx")


class Prog:
    ENGS = ("pe", "act", "dve", "pool", "sp")
    NDMA_SEM = 8

    def __init__(self, nc, same_engine_sync=True):
        self.nc = nc
        self.ops = []
        self.stack = ExitStack()
        self.same = same_engine_sync
        self.nt = 0

    def sb(self, shape, dt, name=None):
        self.nt += 1
        if getattr(self, "arena", None) is None:
            return self.stack.enter_context(self.nc.sbuf_tensor(f"sb{self.nt}_{name or ''}", list(shape), dt))
        esz = 2 if dt == BF16 else 4
        n = 1
        for d_ in shape[1:]:
            n *= d_
        nbytes = (n * esz + 63) // 64 * 64
        off = self.aoff
        assert off + nbytes <= self.abytes, f"arena overflow: {name} {shape} off={off} need={nbytes} cap={self.abytes}"
        self.aoff += nbytes
        ap = self.arena[:, off // 4: off // 4 + nbytes // 4]
        if dt != F32:
            ap = ap.bitcast(dt)
        ap = ap[:, 0:n]
        fs = list(shape[1:])
        if len(fs) == 2:
            ap = ap.rearrange("p (a b) -> p a b", b=fs[1])
        elif len(fs) == 3:
            ap = ap.rearrange("p (a b c) -> p a b c", b=fs[1], c=fs[2])
        if shape[0] < 128:
            ap = ap[0:shape[0]]
        return ap

    def use_arena(self, nbytes):
        self.arena = None
        self.fscr = self.sb([128, 8], F32, name="fence_scr")
        self.arena = self.sb([128, nbytes // 4], F32, name="arena_main")
        self.abytes = nbytes
        self.aoff = 0

    def stage_end(self):
        global _LIVE_TOKS
        toks = list(_LIVE_TOKS)
        fs = self.fscr
        self.add("dve", lambda e: e.memset(fs[0:1, 0:1], 0.0), toks, toks)
        tf = Tok("fence", track=False)
        self.add("dve", lambda e: e.memset(fs[0:1, 1:2], 0.0), toks, [tf])
        for e_ in ("pe", "act", "pool", "sp"):
            self.add(e_, None, [tf], ())
        _LIVE_TOKS = [t for t in toks if t.persist]
        self.aoff = 0

    def ps(self, shape, dt, name=None):
        self.nt += 1
        return self.stack.enter_context(self.nc.psum_tensor(f"ps{self.nt}_{name or ''}", list(shape), dt))

    def add(self, eng, emit, reads=(), writes=(), dma=False, cc=False):
        op = _Op()
        op.cc = cc
        op.eng = eng
        op.emit = emit
        op.dma = dma
        op.sig = dma
        op.sem = None
        op.val = 0
        deps = []
        for t in reads:
            if t.w is not None:
                deps.append(t.w)
        for t in writes:
            if t.w is not None:
                deps.append(t.w)
            deps.extend(t.rs)
        for t in reads:
            t.rs.append(op)
        for t in writes:
            t.w = op
            t.rs = []
        seen = set()
        d2 = []
        for d in deps:
            if id(d) in seen or d is op:
                continue
            seen.add(id(d))
            d2.append(d)
        op.deps = d2
        self.ops.append(op)
        return op

    def dma(self, q, out, in_, r=(), w=()):
        return self.add(q, lambda e: e.dma_start(out=out, in_=in_), r, w, dma=True)

    def allgather(self, groups, src, dst, r=(), w=()):
        return self.add("pool", lambda e: e.collective_compute("AllGather", ALU.bypass, replica_groups=groups,
                                                               ins=[src.opt()], outs=[dst.opt()]), r, w, dma=True, cc=True)

    def mm(self, out, lhsT, rhs, start, stop, r=(), w=()):
        return self.add("pe", lambda e: e.matmul(out, lhsT, rhs, start=start, stop=stop), r, w)

    def tr(self, out, in_, ident, r=(), w=()):
        return self.add("pe", lambda e: e.transpose(out, in_, ident), r, w)

    def act(self, out, in_, func, r=(), w=(), **kw):
        return self.add("act", lambda e: e.activation(out=out, in_=in_, func=func, **kw), r, w)

    def ts(self, eng, out, in0, s1, s2, op0, op1=None, r=(), w=(), **kw):
        if op1 is None:
            return self.add(eng, lambda e: e.tensor_scalar(out, in0, s1, s2, op0, **kw), r, w)
        return self.add(eng, lambda e: e.tensor_scalar(out, in0, s1, s2, op0, op1, **kw), r, w)

    def tt(self, eng, out, in0, in1, op, r=(), w=()):
        return self.add(eng, lambda e: e.tensor_tensor(out, in0, in1, op), r, w)

    def stt(self, eng, out, in0, scalar, in1, op0, op1, r=(), w=()):
        return self.add(eng, lambda e: e.scalar_tensor_tensor(out, in0, scalar, in1, op0, op1), r, w)

    def cp(self, eng, out, in_, r=(), w=()):
        if eng == "act":
            return self.add("act", lambda e: e.activation(out=out, in_=in_, func=AF.Copy), r, w)
        return self.add(eng, lambda e: e.tensor_copy(out=out, in_=in_), r, w)

    def _need(self, op, d):
        if d.dma:
            return True
        if d.eng != op.eng:
            return True
        if op.dma:
            return True
        if op.eng == "pe" or op.emit is None:
            return False
        return self.same

    def finalize(self, final_reads=()):
        nc = self.nc
        self.add("sp", None, reads=final_reads)
        per = {e: [] for e in self.ENGS}
        for op in self.ops:
            per[op.eng].append(op)
        for e in self.ENGS:
            for idx, op in enumerate(per[e]):
                op.idx = idx
        for op in self.ops:
            best = {}
            keep = []
            for d in op.deps:
                if d.dma:
                    keep.append(d)
                else:
                    b = best.get(d.eng)
                    if b is None or d.idx > b.idx:
                        best[d.eng] = d
            op.deps = keep + list(best.values())
        for op in self.ops:
            for d in op.deps:
                if self._need(op, d) and not d.dma:
                    d.sig = True
        sems = {}
        for e in self.ENGS:
            sems[e] = self.stack.enter_context(nc.semaphore(f"s_{e}"))
        dsem = {}
        for e in self.ENGS:
            if any(o.dma for o in per[e]):
                dsem[e] = [self.stack.enter_context(nc.semaphore(f"d_{e}{i}")) for i in range(self.NDMA_SEM)]
        ccsem = self.stack.enter_context(nc.semaphore("s_cc"))
        ncc = 0
        prevcc = None
        for e in self.ENGS:
            cnt = 0
            k = 0
            prev = [None] * self.NDMA_SEM
            for op in per[e]:
                if op.cc:
                    ncc += 1
                    op.sem = ccsem
                    op.val = ncc
                    if prevcc is not None:
                        op.deps.append(prevcc)
                    prevcc = op
                elif op.dma:
                    j = k % self.NDMA_SEM
                    op.sem = dsem[e][j]
                    op.val = 16 * (k // self.NDMA_SEM + 1)
                    if prev[j] is not None:
                        op.deps.append(prev[j])
                    prev[j] = op
                    k += 1
                elif op.sig:
                    cnt += 1
                    op.sem = sems[e]
                    op.val = cnt
        block = self.stack.enter_context(nc.Block())

        def make(e):
            def body(eng):
                waited = {}
                for op in per[e]:
                    for d in op.deps:
                        if not self._need(op, d):
                            continue
                        key = id(d.sem)
                        if waited.get(key, 0) >= d.val:
                            continue
                        eng.wait_ge(d.sem, d.val)
                        waited[key] = d.val
                    if op.emit is None:
                        continue
                    ins = op.emit(eng)
                    if op.cc:
                        ins.then_inc(op.sem)
                    elif op.sig:
                        ins.then_inc(op.sem, 16 if op.dma else 1)
            return body

        block.tensor(make("pe"))
        block.scalar(make("act"))
        block.vector(make("dve"))
        block.gpsimd(make("pool"))
        block.sync(make("sp"))
        self.stack.close()


class Ctx:
    def __init__(self, P, ident_dram):
        self.P = P
        self.bank = [P.ps([128, 512], F32, name=f"bank{i}") for i in range(8)]
        self.tb = [Tok(f"bank{i}", persist=True) for i in range(8)]
        self.ident = P.sb([128, 128], F32, name="ident_sb")
        self.tident = Tok("ident", persist=True)
        P.dma("sp", self.ident[:], ident_dram[:, :], w=[self.tident])
        self.rr = 0

    def eng2(self):
        self.rr += 1
        return "act" if self.rr % 2 else "dve"


def rms_rows(P, C, xt, rows, ss, rstd, junk, tx, tjunk, tstat):
    P.act(junk[:rows, :], xt[:rows, :], AF.Square, r=[tx], w=[tjunk, tstat], accum_out=ss[:rows, :])
    P.ts("dve", rstd[:rows, :], ss[:rows, :], 1.0 / D, EPS, ALU.mult, ALU.add, r=[tstat], w=[tstat])
    P.act(rstd[:rows, :], rstd[:rows, :], AF.Sqrt, r=[tstat], w=[tstat])
    P.add("dve", lambda e: e.reciprocal(rstd[:rows, :], rstd[:rows, :]), [tstat], [tstat])


def norm_transpose_tile(P, C, src_rows, tsrc, rows, gbc, tgbc, dst, dstcol, tdst, bufs, banks):
    xt, tx, junk, tjunk, ss, rstd, tstat, xs, txs = bufs
    P.dma("sp", xt[:rows, :], src_rows, r=tsrc, w=[tx])
    rms_rows(P, C, xt, rows, ss, rstd, junk, tx, tjunk, tstat)
    P.stt("dve", xs[:rows, :], xt[:rows, :], rstd[:rows, 0:1], gbc[:rows, :], ALU.mult, ALU.mult,
          r=[tx, tstat, tgbc], w=[txs])
    for q in range(4):
        bk = banks[q % len(banks)]
        for i in range(4):
            k = q * 4 + i
            P.tr(C.bank[bk][:, i * 128:i * 128 + rows], xs[:rows, k * 128:(k + 1) * 128], C.ident[:rows, :rows],
                 r=[txs, C.tident], w=[C.tb[bk]])
        src = C.bank[bk][:, :].rearrange("p (i t) -> p i t", t=128)[:, :, 0:rows]
        P.cp(C.eng2(), dst[:, q * 4:(q + 1) * 4, dstcol:dstcol + rows], src, r=[C.tb[bk]], w=[tdst])


def load_bcast(P, dst, vec_dram, n, tok):
    src = bass.AP(tensor=vec_dram.tensor, offset=vec_dram.offset, ap=[[0, 128], [1, n]])
    P.dma("sp", dst, src, w=[tok])


def load_cols(P, C, dst, vec_dram, ntile, tmp, ttmp, tdst, bank):
    P.dma("sp", tmp[:ntile, :], vec_dram.rearrange("(t p) -> t p", p=128), w=[ttmp])
    P.tr(C.bank[bank][:, 0:ntile], tmp[:ntile, :], C.ident[:ntile, :ntile], r=[ttmp, C.tident], w=[C.tb[bank]])
    P.cp("dve", dst, C.bank[bank][:, 0:ntile], r=[C.tb[bank]], w=[tdst])


def stage_ffn(nc, P, C, yT, xext, w_out, g_ffn, w_up, conv_w, conv_b, w_down, xmid, xout,
              tin, tout, hx, hx_out=True, ntb=2):
    KT = D // 128
    NJ = FF // 128
    TB = NT // ntb
    W3 = (TB + 2) // 3
    assert W3 * 3 == TB + 2
    arenaA = P.sb([128, NJ * TB], BF16, name="arenaA")
    tA = Tok("arenaA")
    mT = arenaA[:, :].rearrange("p (j t) -> p j t", t=TB)
    y_sb = arenaA[:, 0:KT * NT].rearrange("p (k t) -> p k t", t=NT)
    tm = ty = tA
    wu = [P.sb([128, KT, 256], BF16, name=f"wu{i}") for i in range(2)]
    twu = [Tok(f"wu{i}") for i in range(2)]
    xr2 = [P.sb([128, 256], F32, name=f"xr2{i}") for i in range(3)]
    txr2 = [Tok() for i in range(3)]
    xo2 = [P.sb([128, 256], F32, name=f"xo2{i}") for i in range(3)]
    txo2 = [Tok() for i in range(3)]
    txmid = Tok("xmid", persist=True)
    yv = yT.rearrange("(k p) t -> p k t", p=128)
    for k in range(KT):
        P.dma("sp", y_sb[:, k, :], yv[:, k, :], r=tin, w=[ty])
    wov = w_out.rearrange("(k p) n -> p k n", p=128)
    tiles = [(1 + i * 128, 128) for i in range(NT // 128)]
    cnt = 0

    def load_wo(dc):
        b = dc % 2
        P.dma("pool", wu[b][:, :, :], wov[:, :, dc * 256:(dc + 1) * 256], w=[twu[b]])
    load_wo(0)
    for dc in range(8):
        if dc + 1 < 8:
            load_wo(dc + 1)
        b = dc % 2
        for (r0, rows) in tiles:
            bk = cnt % 4
            i3 = cnt % 3
            cnt += 1
            P.dma("sp", xr2[i3][:rows, :], xext[r0:r0 + rows, dc * 256:(dc + 1) * 256], r=tin, w=[txr2[i3]])
            for k in range(KT):
                P.mm(C.bank[bk][:rows, 0:256], y_sb[:, k, r0 - 1:r0 - 1 + rows], wu[b][:, k, :], k == 0, k == KT - 1,
                     r=[ty, twu[b]], w=[C.tb[bk]])
            P.tt("dve", xo2[i3][:rows, :], C.bank[bk][:rows, 0:256], xr2[i3][:rows, :], ALU.add,
                 r=[C.tb[bk], txr2[i3]], w=[txo2[i3]])
            P.dma("sp", xmid[r0:r0 + rows, dc * 256:(dc + 1) * 256], xo2[i3][:rows, :], r=[txo2[i3]], w=[txmid])
    hx(xmid, txmid)
    gbc = P.sb([128, D], F32, name="gbc")
    tgbc = Tok("gbc")
    load_bcast(P, gbc[:, :], g_ffn, D, tgbc)
    NF = 2 * NJ
    cw = P.sb([128, 4, NF], F32, name="cw")
    tcw = Tok("cw")
    tmpv = P.sb([NF, 128], F32, name="tmpv")
    ttmpv = Tok("tmpv")
    for i in range(3):
        load_cols(P, C, cw[:, i, :], conv_w[i, :], NF, tmpv, ttmpv, tcw, 7)
    load_cols(P, C, cw[:, 3, :], conv_b, NF, tmpv, ttmpv, tcw, 7)
    h_sb = P.sb([128, KT, TB + 2], BF16, name="h_sb")
    th = Tok("h_sb")
    nxt = P.sb([128, D], F32, name="nxt")
    nxs = P.sb([128, D], F32, name="nxs")
    nss = P.sb([128, 1], F32, name="nss")
    nrs = P.sb([128, 1], F32, name="nrs")
    tnx, tnxs, tnst = Tok(), Tok(), Tok()
    nbuf = (nxt, tnx, nxs, tnxs, nss, nrs, tnst, nxs, tnxs)
    a_sb = [P.sb([128, TB + 2], F32, name=f"a{i}") for i in range(2)]
    ta = [Tok(f"a{i}") for i in range(2)]
    c1 = [P.sb([128, TB], F32, name=f"c1_{i}") for i in range(2)]
    tc1 = [Tok() for i in range(2)]
    sg = P.sb([128, TB], F32, name="sg")
    tsg = Tok("sg")
    wd = [P.sb([128, 4, 256], BF16, name=f"wd{i}") for i in range(3)]
    twd = [Tok(f"wd{i}") for i in range(3)]
    wuv = w_up.rearrange("(k p) n -> p k n", p=128)
    wdv = w_down.rearrange("(j p) n -> p j n", p=128)

    def load_wu(j):
        b = j % 2
        P.dma("pool", wu[b][:, :, 0:128], wuv[:, :, j * 128:(j + 1) * 128], w=[twu[b]])
        P.dma("pool", wu[b][:, :, 128:256], wuv[:, :, FF + j * 128:FF + (j + 1) * 128], w=[twu[b]])

    for blk in range(ntb):
        w0 = blk * TB
        segs = [(w0, 1, 0)] + [(w0 + 1 + i * 128, 128, 1 + i * 128) for i in range(TB // 128)] + [(w0 + TB + 1, 1, TB + 1)]
        for (r0, rows, col) in segs:
            norm_transpose_tile(P, C, xmid[r0:r0 + rows, :], [txmid], rows, gbc, tgbc, h_sb, col, th, nbuf, [6, 7])
        load_wu(0)
        step = 0
        for j in range(NJ):
            if j + 1 < NJ:
                load_wu(j + 1)
            b = j % 2
            for half in range(2):
                ft = j if half == 0 else NJ + j
                s = step % 2
                step += 1
                bks = [3 * s, 3 * s + 1, 3 * s + 2]
                for c3 in range(3):
                    for k in range(KT):
                        P.mm(C.bank[bks[c3]][:, 0:W3], wu[b][:, k, half * 128:(half + 1) * 128],
                             h_sb[:, k, c3 * W3:(c3 + 1) * W3], k == 0, k == KT - 1,
                             r=[twu[b], th], w=[C.tb[bks[c3]]])
                for c3 in range(3):
                    P.cp("act", a_sb[s][:, c3 * W3:(c3 + 1) * W3], C.bank[bks[c3]][:, 0:W3],
                         r=[C.tb[bks[c3]]], w=[ta[s]])
                a = a_sb[s]
                P.ts("dve", c1[s][:, :], a[:, 1:TB + 1], cw[:, 1, ft:ft + 1], cw[:, 3, ft:ft + 1], ALU.mult, ALU.add,
                     r=[ta[s], tcw], w=[tc1[s]])
                P.stt("dve", c1[s][:, :], a[:, 0:TB], cw[:, 0, ft:ft + 1], c1[s][:, :], ALU.mult, ALU.add,
                      r=[ta[s], tcw], w=[tc1[s]])
                P.stt("dve", c1[s][:, :], a[:, 2:TB + 2], cw[:, 2, ft:ft + 1], c1[s][:, :], ALU.mult, ALU.add,
                      r=[ta[s], tcw], w=[tc1[s]])
                if half == 0:
                    P.act(sg[:, :], c1[s][:, :], AF.Silu, r=[tc1[s]], w=[tsg])
                else:
                    P.tt("dve", mT[:, j, :], c1[s][:, :], sg[:, :], ALU.mult, r=[tc1[s], tsg], w=[tm])
        NU = NJ // 4
        ucnt = 0
        units = [(dc, u) for dc in range(8) for u in range(NU)]

        def load_wd(idx):
            dc, u = units[idx]
            b3 = idx % 3
            P.dma("pool", wd[b3][:, :, :], wdv[:, u * 4:(u + 1) * 4, dc * 256:(dc + 1) * 256], w=[twd[b3]])
        load_wd(0)
        load_wd(1)
        for idx, (dc, u) in enumerate(units):
            if idx + 2 < len(units):
                load_wd(idx + 2)
            b3 = idx % 3
            for tt_ in range(TB // 128):
                bk = tt_
                co = 0
                for jj in range(4):
                    j = u * 4 + jj
                    P.mm(C.bank[bk][:, co:co + 256], mT[:, j, tt_ * 128:(tt_ + 1) * 128], wd[b3][:, jj, :],
                         j == 0, j == NJ - 1, r=[tm, twd[b3]], w=[C.tb[bk]])
            if u == NU - 1:
                for tt_ in range(TB // 128):
                    bk = tt_
                    co = 0
                    i3 = ucnt % 3
                    ucnt += 1
                    r0 = w0 + 1 + tt_ * 128
                    P.dma("sp", xr2[i3][:, :], xmid[r0:r0 + 128, dc * 256:(dc + 1) * 256], r=[txmid], w=[txr2[i3]])
                    P.tt("dve", xo2[i3][:, :], C.bank[bk][:, co:co + 256], xr2[i3][:, :], ALU.add,
                         r=[C.tb[bk], txr2[i3]], w=[txo2[i3]])
                    P.dma("sp", xout[r0:r0 + 128, dc * 256:(dc + 1) * 256], xo2[i3][:, :],
                          r=[txo2[i3]], w=[tout])
    if hx_out:
        hx(xout, tout)


class WStream:
    def __init__(self, P, KT, units, nbuf=3, name="ws", width=256):
        self.P = P
        self.buf = [P.sb([128, KT, width], BF16, name=f"{name}{i}") for i in range(nbuf)]
        self.tok = [Tok(f"{name}{i}") for i in range(nbuf)]
        self.units = units
        self.nbuf = nbuf
        self.issued = 0

    def get(self, i):
        while self.issued < len(self.units) and self.issued <= i + self.nbuf - 1:
            j = self.issued
            wview, c0, ncols = self.units[j]
            b = j % self.nbuf
            self.P.dma("pool", self.buf[b][:, :, 0:ncols], wview[:, :, c0:c0 + ncols], w=[self.tok[b]])
            self.issued += 1
        return self.buf[i % self.nbuf], self.tok[i % self.nbuf]


def norm_all(P, C, x, tin, g_vec, h_sb, th, ntok, banks=(6, 7)):
    gbc = P.sb([128, D], F32, name="gbc")
    tgbc = Tok("gbc")
    load_bcast(P, gbc[:, :], g_vec, D, tgbc)
    bufs = []
    for i in range(2):
        xt = P.sb([128, D], F32, name=f"nxt{i}")
        xs = P.sb([128, D], F32, name=f"nxs{i}")
        ss = P.sb([128, 1], F32, name=f"nss{i}")
        rs = P.sb([128, 1], F32, name=f"nrs{i}")
        t1, t2, t3 = Tok(), Tok(), Tok()
        bufs.append((xt, t1, xs, t2, ss, rs, t3, xs, t2))
    for i in range(ntok // 128):
        norm_transpose_tile(P, C, x[i * 128:(i + 1) * 128, :], tin, 128, gbc, tgbc, h_sb, i * 128, th,
                            bufs[i % 2], list(banks))


def stage_cd_in(nc, P, C, x, g_mix, w_in, cosT, sinT, rmat, qcT, kcT, vc, qdT, kdT, vd, tin, tout):
    KT = D // 128
    h_sb = P.sb([128, KT, NT], BF16, name="h_sb")
    th = Tok("h_sb")
    norm_all(P, C, x, tin, g_mix, h_sb, th, NT)
    cs = P.sb([128, NT], F32, name="cos_sb")
    sn = P.sb([128, NT], F32, name="sin_sb")
    rm = P.sb([128, 128], BF16, name="rm_sb")
    tcs = Tok("cs")
    P.dma("sp", cs[:, :], cosT[:, :], w=[tcs])
    P.dma("sp", sn[:, :], sinT[:, :], w=[tcs])
    P.dma("sp", rm[:, :], rmat[:, :], w=[tcs])
    wv = w_in.rearrange("(k p) n -> p k n", p=128)
    fm = [(0, qcT, False), (1024, kcT, False), (3072, qdT, True), (4096, kdT, True)]
    units = [(wv, col0 + u * 256, 256) for (col0, _, _) in fm for u in range(4)]
    units += [(wv, col0 + u * 256, 256) for col0 in (2048, 5120) for u in range(4)]
    ws = WStream(P, KT, units, 3)
    ui = 0
    ob = [P.sb([128, 512], BF16, name=f"ob{i}") for i in range(3)]
    tob = [Tok() for _ in range(3)]
    xb = [P.sb([128, 512], BF16, name=f"xb{i}") for i in range(2)]
    txb = [Tok() for _ in range(2)]
    t1 = [P.sb([128, 512], F32, name=f"rt1{i}") for i in range(2)]
    tt1 = [Tok() for _ in range(2)]
    t2 = [P.sb([128, 512], F32, name=f"rt2{i}") for i in range(2)]
    tt2 = [Tok() for _ in range(2)]
    nb = 0
    no = 0
    nx = 0
    for (col0, dst, rot) in fm:
        for u in range(4):
            wb, tw = ws.get(ui)
            ui += 1
            for m in range(2):
                r0 = u * 256 + m * 128
                for tb in range(NT // 512):
                    bk = nb % 4
                    nb += 1
                    for k in range(KT):
                        P.mm(C.bank[bk][:, :], wb[:, k, m * 128:(m + 1) * 128], h_sb[:, k, tb * 512:(tb + 1) * 512],
                             k == 0, k == KT - 1, r=[tw, th], w=[C.tb[bk]])
                    o = no % 3
                    no += 1
                    if not rot:
                        P.cp(C.eng2(), ob[o][:, :], C.bank[bk][:, :], r=[C.tb[bk]], w=[tob[o]])
                    else:
                        xi = nx % 2
                        nx += 1
                        bk2 = 4 + xi
                        P.cp("act", xb[xi][:, :], C.bank[bk][:, :], r=[C.tb[bk]], w=[txb[xi]])
                        P.mm(C.bank[bk2][:, :], rm[:, :], xb[xi][:, :], True, True, r=[tcs, txb[xi]], w=[C.tb[bk2]])
                        P.tt("dve", t1[xi][:, :], xb[xi][:, :], cs[:, tb * 512:(tb + 1) * 512], ALU.mult,
                             r=[txb[xi], tcs], w=[tt1[xi]])
                        P.tt("dve", t2[xi][:, :], C.bank[bk2][:, :], sn[:, tb * 512:(tb + 1) * 512], ALU.mult,
                             r=[C.tb[bk2], tcs], w=[tt2[xi]])
                        P.tt("pool", ob[o][:, :], t1[xi][:, :], t2[xi][:, :], ALU.add, r=[tt1[xi], tt2[xi]], w=[tob[o]])
                    P.dma("sp", dst[r0:r0 + 128, tb * 512:(tb + 1) * 512], ob[o][:, :], r=[tob[o]], w=[tout])
    obt = [P.sb([128, 256], BF16, name=f"obt{i}") for i in range(3)]
    tobt = [Tok() for _ in range(3)]
    for (col0, dst) in [(2048, vc), (5120, vd)]:
        for u in range(4):
            wb, tw = ws.get(ui)
            ui += 1
            for tt_ in range(NT // 128):
                bk = nb % 4
                nb += 1
                for k in range(KT):
                    P.mm(C.bank[bk][:, 0:256], h_sb[:, k, tt_ * 128:(tt_ + 1) * 128], wb[:, k, :],
                         k == 0, k == KT - 1, r=[tw, th], w=[C.tb[bk]])
                o = no % 3
                no += 1
                P.cp(C.eng2(), obt[o][:, :], C.bank[bk][:, 0:256], r=[C.tb[bk]], w=[tobt[o]])
                P.dma("sp", dst[tt_ * 128:(tt_ + 1) * 128, u * 256:(u + 1) * 256], obt[o][:, :], r=[tobt[o]], w=[tout])


NA_KTS = {0: [0, 1, 2, 3, 4, 5], 1: [1, 2, 3, 4, 5], 14: [14, 15, 16, 17, 18], 15: [14, 15, 16, 17, 18, 19]}
NA_CLS = {0: 0, 1: 1, 14: 3, 15: 4}
NKEXT = 2560


def na_kts(i):
    return NA_KTS.get(i, [i, i + 1, i + 2, i + 3, i + 4])


def stage_attn(nc, P, C, qcT, kcx, vcx, btab, qdT, kdT, vdF, dlam, lamc, subln, yT, tin, tout):
    ones_t = Tok("aug")
    lp = P.sb([128, 256], F32, name="lp")
    tlp = Tok("lp")
    load_bcast(P, lp[:, :], dlam.rearrange("a b -> (a b)"), 256, tlp)
    lc = P.sb([128, 2], F32, name="lc")
    load_bcast(P, lc[:, :], lamc, 2, tlp)
    ltmp = P.sb([128, 128], F32, name="ltmp")
    lsum = P.sb([128, 4], F32, name="lsum")
    P.tt("dve", ltmp[:, 0:64], lp[:, 0:64], lp[:, 64:128], ALU.mult, r=[tlp], w=[tlp])
    P.tt("dve", ltmp[:, 64:128], lp[:, 128:192], lp[:, 192:256], ALU.mult, r=[tlp], w=[tlp])
    P.add("dve", lambda e: e.reduce_sum(lsum[:, 0:1], ltmp[:, 0:64], AX.X), [tlp], [tlp])
    P.add("dve", lambda e: e.reduce_sum(lsum[:, 1:2], ltmp[:, 64:128], AX.X), [tlp], [tlp])
    P.act(lsum[:, 0:2], lsum[:, 0:2], AF.Exp, r=[tlp], w=[tlp])
    P.tt("dve", lsum[:, 2:3], lsum[:, 0:1], lsum[:, 1:2], ALU.subtract, r=[tlp], w=[tlp])
    P.tt("dve", lsum[:, 2:3], lsum[:, 2:3], lc[:, 0:1], ALU.add, r=[tlp], w=[tlp])
    P.ts("dve", lsum[:, 3:4], lsum[:, 2:3], -1.0, None, ALU.mult, r=[tlp], w=[tlp])
    neglam = lsum[:, 3:4]
    slb = P.sb([128, 128], F32, name="slb")
    load_bcast(P, slb[:, :], subln, 128, tlp)
    P.ts("dve", slb[:, :], slb[:, :], lc[:, 1:2], None, ALU.mult, r=[tlp], w=[tlp])

    ysb = [P.sb([128, NT], BF16, name=f"ysb{i}") for i in range(2)]
    tys = [Tok() for _ in range(2)]
    ot = [P.sb([128, 128], F32, name=f"ot{i}") for i in range(2)]
    tot = [Tok() for _ in range(2)]
    st = [P.sb([128, 4], F32, name=f"st{i}") for i in range(2)]
    tst = [Tok() for _ in range(2)]
    nsm = 0
    ntr = 0
    nys = 0
    trbanks = [6, 7]

    def finish_tile(o_ap, h_ysb, col, ttok_src):
        nonlocal ntr
        bk = trbanks[ntr % len(trbanks)]
        ntr += 1
        P.tr(C.bank[bk][:, 0:128], o_ap, C.ident[:, :], r=[ttok_src, C.tident], w=[C.tb[bk]])
        P.cp(C.eng2(), ysb[h_ysb][:, col:col + 128], C.bank[bk][:, 0:128], r=[C.tb[bk]], w=[tys[h_ysb]])

    SC_C = 128.0 ** -0.5
    kc = [P.sb([128, NKEXT], BF16, name=f"kc{i}") for i in range(2)]
    qc = [P.sb([128, NT], BF16, name=f"qc{i}") for i in range(2)]
    vca = [P.sb([128, NKEXT // 128, 129], BF16, name=f"vca{i}") for i in range(2)]
    thd = [Tok() for _ in range(2)]
    for i in range(2):
        P.add("pool", lambda e, i=i: e.memset(vca[i][:, :, 128:129], 1.0), (), [thd[i]])
    bint = [P.sb([128, 6, 128], F32, name=f"bint{i}") for i in range(2)]
    bedge = [P.sb([128, 6, 128], F32, name=f"bedge{i}") for i in range(2)]
    tbe = [Tok() for _ in range(2)]
    ssb = [P.sb([128, 6, 128], F32, name=f"ssb{i}") for i in range(2)]
    tss = [Tok() for _ in range(2)]
    pT = [P.sb([128, 6, 128], BF16, name=f"pT{i}") for i in range(2)]
    tpT = [Tok() for _ in range(2)]
    nedge = 0
    nit = 0
    for h in range(8):
        hb = h % 2
        P.dma("sp", kc[hb][:, :], kcx[h * 128:(h + 1) * 128, :], r=tin, w=[thd[hb]])
        P.dma("sp", qc[hb][:, :], qcT[h * 128:(h + 1) * 128, :], r=tin, w=[thd[hb]])
        P.dma("sp", vca[hb][:, :, 0:128], vcx[:, h * 128:(h + 1) * 128].rearrange("(t p) d -> p t d", p=128),
              r=tin, w=[thd[hb]])
        P.dma("sp", bint[hb][:, :, :], btab[h, 2, :, :, :], r=tin, w=[thd[hb]])
        yb_ = nys % 2
        nys += 1
        def na_stage1(i):
            nonlocal nedge, nit
            kts = na_kts(i)
            nk = len(kts)
            cls = NA_CLS.get(i, 2)
            if cls == 2:
                btile, tbt = bint[hb], thd[hb]
            else:
                eb = nedge % 2
                nedge += 1
                P.dma("sp", bedge[eb][:, :, :], btab[h, cls, :, :, :], r=tin, w=[tbe[eb]])
                btile, tbt = bedge[eb], tbe[eb]
            it = nit % 2
            nit += 1
            sb0 = 2 * it
            for jj, kt in enumerate(kts):
                bk = sb0 + jj // 4
                co = (jj % 4) * 128
                P.mm(C.bank[bk][:, co:co + 128], kc[hb][:, kt * 128:(kt + 1) * 128], qc[hb][:, i * 128:(i + 1) * 128],
                     True, True, r=[thd[hb]], w=[C.tb[bk]])
            n0 = min(nk, 4)
            P.stt("dve", ssb[it][:, 0:n0, :], C.bank[sb0][:, 0:n0 * 128].rearrange("p (j q) -> p j q", q=128), SC_C,
                  btile[:, 0:n0, :], ALU.mult, ALU.add, r=[C.tb[sb0], tbt], w=[tss[it]])
            if nk > 4:
                P.stt("dve", ssb[it][:, 4:nk, :], C.bank[sb0 + 1][:, 0:(nk - 4) * 128].rearrange("p (j q) -> p j q", q=128),
                      SC_C, btile[:, 4:nk, :], ALU.mult, ALU.add, r=[C.tb[sb0 + 1], tbt], w=[tss[it]])
            P.act(pT[it][:, 0:nk, :], ssb[it][:, 0:nk, :], AF.Exp, r=[tss[it]], w=[tpT[it]])
            return (i, it, kts)

        def na_stage2(st_):
            nonlocal nsm
            i, it, kts = st_
            nk = len(kts)
            bo = 4 + it
            for jj, kt in enumerate(kts):
                P.mm(C.bank[bo][:, 0:129], pT[it][:, jj, :], vca[hb][:, kt, :], jj == 0, jj == nk - 1,
                     r=[tpT[it], thd[hb]], w=[C.tb[bo]])
            sm = nsm % 2
            nsm += 1
            P.add("dve", lambda e, sm=sm, bo=bo: e.reciprocal(st[sm][:, 0:1], C.bank[bo][:, 128:129]),
                  [C.tb[bo]], [tst[sm]])
            P.ts("dve", ot[sm][:, :], C.bank[bo][:, 0:128], st[sm][:, 0:1], None, ALU.mult,
                 r=[C.tb[bo], tst[sm]], w=[tot[sm]])
            finish_tile(ot[sm][:, :], yb_, i * 128, tot[sm])
        prev_st = None
        for i in range(16):
            cur_st = na_stage1(i)
            if prev_st is not None:
                na_stage2(prev_st)
            prev_st = cur_st
        na_stage2(prev_st)
        P.dma("sp", yT[h * 128:(h + 1) * 128, :], ysb[yb_][:, :], r=[tys[yb_]], w=[tout])

    SC_D = 64.0 ** -0.5
    kd = [P.sb([128, L], BF16, name=f"kd{i}") for i in range(2)]
    qd = [[P.sb([128, NT], BF16, name=f"qd{m}_{i}") for i in range(2)] for m in range(2)]
    vdh = [P.sb([128, L // 128, 128], BF16, name=f"vdh{i}") for i in range(2)]
    thd2 = [Tok() for _ in range(2)]
    for i in range(2):
        P.add("pool", lambda e, i=i: e.memset(qd[0][i][64:128, :], 0.0), (), [thd2[i]])
        P.add("pool", lambda e, i=i: e.memset(qd[1][i][0:64, :], 0.0), (), [thd2[i]])
    onesb = P.sb([128, 128], BF16, name="onesb")
    onesf = P.sb([128, 128], F32, name="onesf")
    tones = Tok()
    P.add("pool", lambda e: e.memset(onesb[:, :], 1.0), (), [tones])
    P.add("pool", lambda e: e.memset(onesf[:, :], 1.0), (), [tones])
    slc = P.sb([128, 1], F32, name="slc")
    P.dma("sp", slc[:, :], subln.rearrange("(p o) -> p o", o=1), r=tin, w=[tlp])
    P.ts("dve", slc[:, :], slc[:, :], lc[:, 1:2], None, ALU.mult, r=[tlp], w=[tlp])
    NPD = 4
    pD = [P.sb([128, 512], BF16, name=f"pD{i}") for i in range(NPD)]
    tpD = [Tok() for _ in range(NPD)]
    wa = P.sb([128, 512], F32, name="wa")
    wb_ = P.sb([128, 512], F32, name="wb")
    wc = P.sb([128, 512], F32, name="wc")
    twa, twb, twc = Tok(), Tok(), Tok()
    npd = 0
    nsb = 0
    NKT = L // 128
    for h in range(8):
        hb = h % 2
        P.dma("sp", kd[hb][:, :], kdT[h * 128:(h + 1) * 128, :], r=tin, w=[thd2[hb]])
        P.dma("sp", qd[0][hb][0:64, :], qdT[h * 128:h * 128 + 64, :], r=tin, w=[thd2[hb]])
        P.dma("sp", qd[1][hb][64:128, :], qdT[h * 128 + 64:(h + 1) * 128, :], r=tin, w=[thd2[hb]])
        P.dma("sp", vdh[hb][:, :, :], vdF[:, h * 128:(h + 1) * 128].rearrange("(t p) d -> p t d", p=128),
              r=tin, w=[thd2[hb]])
        yb_ = nys % 2
        nys += 1

        def d_stage1(item):
            nonlocal nsb, npd
            qb, m, kt = item
            bs = 4 + nsb % 3
            nsb += 1
            P.mm(C.bank[bs][:, :], kd[hb][:, kt * 128:(kt + 1) * 128],
                 qd[m][hb][:, qb * 512:(qb + 1) * 512], True, True,
                 r=[thd2[hb]], w=[C.tb[bs]])
            pi = npd % NPD
            npd += 1
            P.act(pD[pi][:, :], C.bank[bs][:, :], AF.Exp, r=[C.tb[bs]], w=[tpD[pi]], scale=SC_D)
            return (item, pi)

        def d_stage2(st_):
            (qb, m, kt), pi = st_
            bo, bz = 2 * m, 2 * m + 1
            P.mm(C.bank[bo][:, :], vdh[hb][:, kt, :], pD[pi][:, :], kt == 0, kt == NKT - 1,
                 r=[tpD[pi], thd2[hb]], w=[C.tb[bo]])
            P.mm(C.bank[bz][:, :], onesb[:, :], pD[pi][:, :], kt == 0, kt == NKT - 1,
                 r=[tpD[pi], tones], w=[C.tb[bz]])
            if kt != NKT - 1 or m == 0:
                return
            P.add("dve", lambda e: e.reciprocal(wa[:, :], C.bank[1][:, :]), [C.tb[1]], [twa])
            P.tt("dve", wa[:, :], C.bank[0][:, :], wa[:, :], ALU.mult, r=[C.tb[0], twa], w=[twa])
            P.add("dve", lambda e: e.reciprocal(wb_[:, :], C.bank[3][:, :]), [C.tb[3]], [twb])
            P.tt("dve", wb_[:, :], C.bank[2][:, :], wb_[:, :], ALU.mult, r=[C.tb[2], twb], w=[twb])
            P.stt("dve", wa[:, :], wb_[:, :], neglam, wa[:, :], ALU.mult, ALU.add, r=[twb, twa, tlp], w=[twa])
            P.tt("pool", wc[:, :], wa[:, :], wa[:, :], ALU.mult, r=[twa], w=[twc])
            P.mm(C.bank[7][:, :], onesf[:, :], wc[:, :], True, True, r=[twc, tones], w=[C.tb[7]])
            P.ts("dve", wb_[:, :], C.bank[7][:, :], 1.0 / 128, EPS, ALU.mult, ALU.add, r=[C.tb[7]], w=[twb])
            P.act(wb_[:, :], wb_[:, :], AF.Sqrt, r=[twb], w=[twb])
            P.add("dve", lambda e: e.reciprocal(wb_[:, :], wb_[:, :]), [twb], [twb])
            P.tt("dve", wa[:, :], wa[:, :], wb_[:, :], ALU.mult, r=[twa, twb], w=[twa])
            P.ts("dve", ysb[yb_][:, qb * 512:(qb + 1) * 512], wa[:, :], slc[:, 0:1], None, ALU.mult,
                 r=[twa, tlp], w=[tys[yb_]])
        items = [(qb, m, kt) for qb in range(NT // 512) for m in range(2) for kt in range(NKT)]
        pend = []
        for item in items:
            pend.append(d_stage1(item))
            if len(pend) > 2:
                d_stage2(pend.pop(0))
        while pend:
            d_stage2(pend.pop(0))
        P.dma("sp", yT[1024 + h * 128:1024 + (h + 1) * 128, :], ysb[yb_][:, :], r=[tys[yb_]], w=[tout])


def rope_tables(hf):
    d = np.arange(128) % 64 % 32
    inv = (1.0 / (10000.0 ** (d.astype(np.float32) * 2.0 / 64.0))).astype(np.float32)
    pos = (hf * NT + np.arange(NT)).astype(np.float32)
    ang = pos[None, :] * inv[:, None]
    return np.cos(ang).astype(np.float32), np.sin(ang).astype(np.float32)


def rot_matrix():
    rm = np.zeros((128, 128), np.float32)
    for pp in range(128):
        if pp % 64 < 32:
            rm[pp + 32, pp] = -1.0
        else:
            rm[pp - 32, pp] = 1.0
    return rm.astype(NPBF)


def na_bias_tables(rpb, hf):
    out = np.full((8, 5, 128, 6, 128), -30000.0, np.float32)
    p = np.arange(128)
    for ci, i in enumerate([0, 1, 2, 14, 15]):
        kts = na_kts(i)
        q = np.arange(128)
        r = 32 * hf + 2 * i + q // 64
        c = q % 64
        rs = np.clip(r - 4, 0, 56)
        cs = np.clip(c - 8, 0, 48)
        for jj, kt in enumerate(kts):
            gr = 32 * hf - 4 + 2 * kt + p // 64
            cp = p % 64
            valid = ((gr[:, None] >= 0) & (gr[:, None] < 64) & (gr[:, None] >= rs[None, :]) & (gr[:, None] < rs[None, :] + 8)
                     & (cp[:, None] >= cs[None, :]) & (cp[:, None] < cs[None, :] + 16))
            rr = np.clip(gr[:, None] - r[None, :] + 7, 0, 14)
            rc = np.clip(cp[:, None] - c[None, :] + 15, 0, 30)
            vals = rpb[:, rr, rc]
            out[:, ci, :, jj, :] = np.where(valid[None], vals, -30000.0)
    return out


def ext_keys(full_tm, hf):
    out = np.zeros((NKEXT, full_tm.shape[1]), full_tm.dtype)
    lo = hf * NT - 256
    a, b = max(lo, 0), min(lo + NKEXT, L)
    out[a - lo:b - lo] = full_tm[a:b]
    return out


class Arena:
    def __init__(self, P, nbytes, name="arena"):
        self.t = P.sb([128, nbytes // 4], F32, name=name)
        self.nbytes = nbytes

    def view(self, off, shape, dt):
        esz = 2 if dt == BF16 else 4
        n = 1
        for s in shape:
            n *= s
        assert off % 4 == 0 and (n * esz) % 4 == 0 and off + n * esz <= self.nbytes, (off, shape, self.nbytes)
        ap = self.t[:, off // 4: off // 4 + (n * esz) // 4]
        if dt != F32:
            ap = ap.bitcast(dt)
        if len(shape) == 2:
            ap = ap.rearrange("p (a b) -> p a b", b=shape[1])
        elif len(shape) == 3:
            ap = ap.rearrange("p (a b c) -> p a b c", b=shape[1], c=shape[2])
        return ap


def barrier(P, scratch, toks):
    P.add("dve", lambda e: e.memset(scratch[0:1, 0:1], 0.0), toks, toks)


def stage_ab_in(nc, P, C, xext, g_mix, w_in, vnorm, a_ws, a_bs, bcw, bcb, yaT, v_tm, x1_tm, x2T, tin, tout):
    KT = D // 128
    NW = NT + 2
    W5 = NW // 5
    assert W5 * 5 == NW
    scr = P.sb([128, 1], F32, name="scr")
    h_sb = P.sb([128, KT, NW], BF16, name="h_sb")
    th = Tok("h_sb")
    A = Arena(P, 64 * 1024 + 1024, "arenaE")
    gbc = A.view(0, [D], F32)
    xt = A.view(8192, [D], F32)
    xs = A.view(16384, [D], F32)
    tg, tx, txs, tstat = Tok(), Tok(), Tok(), Tok()
    load_bcast(P, gbc, g_mix, D, tg)
    nss = P.sb([128, 1], F32, name="nss")
    nrs = P.sb([128, 1], F32, name="nrs")
    nbuf = (xt, tx, xs, txs, nss, nrs, tstat, xs, txs)
    segs = [(0, 1, 0)] + [(1 + i * 128, 128, 1 + i * 128) for i in range(NT // 128)] + [(NT + 1, 1, NT + 1)]
    for (r0, rows, col) in segs:
        norm_transpose_tile(P, C, xext[r0:r0 + rows, :], tin, rows, gbc, tg, h_sb, col, th, nbuf, [6, 7])
    cw = P.sb([128, 4, 24], F32, name="cwE")
    tcw = Tok("cwE")
    tmpv = P.sb([24, 128], F32, name="tmpvE")
    ttmpv = Tok()
    for i in range(3):
        load_cols(P, C, cw[:, i, :], bcw[i, :], 24, tmpv, ttmpv, tcw, 7)
    load_cols(P, C, cw[:, 3, :], bcb, 24, tmpv, ttmpv, tcw, 7)
    wv = w_in.rearrange("(k p) n -> p k n", p=128)
    units = [(wv, 2048 + u * 256, 256) for u in range(12)] + [(wv, u * 256, 256) for u in range(4)]
    ws = WStream(P, KT, units, 3)
    a_sb = [A.view(i * 8200, [NW], F32) for i in range(2)]
    cb_ = [A.view(16400 + i * 8192, [NT], F32) for i in range(2)]
    ta = [Tok() for _ in range(2)]
    tc = [Tok() for _ in range(2)]
    barrier(P, scr, [tg, tx, txs] + ta + tc)
    otr = [P.sb([128, 512], F32, name=f"otr{i}") for i in range(2)]
    totr = [Tok() for _ in range(2)]
    otb = [P.sb([128, 512], BF16, name=f"otb{i}") for i in range(2)]
    totb = [Tok() for _ in range(2)]
    ntr = 0
    for u in range(12):
        wb, tw = ws.get(u)
        for m in range(2):
            ch = u * 2 + m
            s = ch % 2
            for c5 in range(5):
                for k in range(KT):
                    P.mm(C.bank[c5][:, 0:W5], wb[:, k, m * 128:(m + 1) * 128], h_sb[:, k, c5 * W5:(c5 + 1) * W5],
                         k == 0, k == KT - 1, r=[tw, th], w=[C.tb[c5]])
            for c5 in range(5):
                P.cp("act", a_sb[s][:, c5 * W5:(c5 + 1) * W5], C.bank[c5][:, 0:W5], r=[C.tb[c5]], w=[ta[s]])
            a = a_sb[s]
            c = cb_[s]
            P.ts("dve", c[:, :], a[:, 1:NT + 1], cw[:, 1, ch:ch + 1], cw[:, 3, ch:ch + 1], ALU.mult, ALU.add,
                 r=[ta[s], tcw], w=[tc[s]])
            P.stt("dve", c[:, :], a[:, 0:NT], cw[:, 0, ch:ch + 1], c[:, :], ALU.mult, ALU.add, r=[ta[s], tcw], w=[tc[s]])
            P.stt("dve", c[:, :], a[:, 2:NT + 2], cw[:, 2, ch:ch + 1], c[:, :], ALU.mult, ALU.add, r=[ta[s], tcw], w=[tc[s]])
            if True:
                dst = v_tm if ch < 8 else (x1_tm if ch < 16 else x2T)
                c0 = (ch % 8) * 128
                for t4 in range(NT // 512):
                    bk = 5 + ntr % 3
                    o = ntr % 2
                    ntr += 1
                    for q in range(4):
                        tt_ = t4 * 4 + q
                        P.tr(C.bank[bk][:, q * 128:(q + 1) * 128], c[:, tt_ * 128:(tt_ + 1) * 128], C.ident[:, :],
                             r=[tc[s], C.tident], w=[C.tb[bk]])
                    src = C.bank[bk][:, :].rearrange("p (q c) -> p q c", c=128)
                    if ch < 8:
                        P.cp("act", otb[o][:, :].rearrange("p (q c) -> p q c", c=128), src, r=[C.tb[bk]], w=[totb[o]])
                        P.dma("sp", dst[t4 * 512:(t4 + 1) * 512, c0:c0 + 128].rearrange("(q p) c -> p q c", p=128),
                              otb[o][:, :].rearrange("p (q c) -> p q c", c=128), r=[totb[o]], w=[tout])
                    else:
                        P.cp("act", otr[o][:, :].rearrange("p (q c) -> p q c", c=128), src, r=[C.tb[bk]], w=[totr[o]])
                        P.dma("sp", dst[t4 * 512:(t4 + 1) * 512, c0:c0 + 128].rearrange("(q p) c -> p q c", p=128),
                              otr[o][:, :].rearrange("p (q c) -> p q c", c=128), r=[totr[o]], w=[tout])
    wvb = A.view(0, [KT, 1024], BF16)
    u_sb = A.view(32768, [8, NT], BF16)
    twv, tu = Tok(), Tok()
    barrier(P, scr, ta + tc + [twv, tu])
    for u in range(4):
        P.dma("pool", wvb[:, :, u * 256:(u + 1) * 256], wv[:, :, 1024 + u * 256:1024 + (u + 1) * 256], w=[twv])
    bsb = P.sb([128, 8, 128], F32, name="bsb")
    tbs = Tok()
    load_bcast(P, bsb[:, :, :].rearrange("p g q -> p (g q)"), a_bs.rearrange("g q -> (g q)"), 1024, tbs)
    vgb = P.sb([128, 1024], F32, name="vgb")
    load_bcast(P, vgb[:, :], vnorm, 1024, tbs)
    wsT = P.sb([128, 8, 128], BF16, name="wsT")
    wtmp = P.sb([128, 128], F32, name="wtmp")
    twt = Tok()
    for g in range(8):
        P.dma("sp", wtmp[:, :], a_ws[g, :, :], w=[twt])
        P.tr(C.bank[7][:, 0:128], wtmp[:, :], C.ident[:, :], r=[twt, C.tident], w=[C.tb[7]])
        P.cp("dve", wsT[:, g, :], C.bank[7][:, 0:128], r=[C.tb[7]], w=[tbs])
    nb = 0
    for u in range(4):
        wb, tw = ws.get(12 + u)
        for m in range(2):
            fc = u * 2 + m
            for tb in range(NT // 512):
                bk = nb % 4
                nb += 1
                for k in range(KT):
                    P.mm(C.bank[bk][:, :], wb[:, k, m * 128:(m + 1) * 128], h_sb[:, k, 1 + tb * 512:1 + (tb + 1) * 512],
                         k == 0, k == KT - 1, r=[tw, th], w=[C.tb[bk]])
                P.act(u_sb[:, fc, tb * 512:(tb + 1) * 512], C.bank[bk][:, :], AF.Gelu, r=[C.tb[bk]], w=[tu])
    vt = [P.sb([128, 1024], F32, name=f"vt{i}") for i in range(2)]
    tvt = [Tok() for _ in range(2)]
    vn = [P.sb([128, 1024], BF16, name=f"vn{i}") for i in range(2)]
    tvn = [Tok() for _ in range(2)]
    vj = P.sb([128, 1024], BF16, name="vj")
    tvj = Tok()
    vst = [P.sb([128, 2], F32, name=f"vst{i}") for i in range(2)]
    tvs = [Tok() for _ in range(2)]
    sg_ = [P.sb([128, 8, 128], F32, name=f"sg{i}") for i in range(2)]
    tsg = [Tok() for _ in range(2)]
    for ck in range(NT // 128):
        s = ck % 2
        for half in range(2):
            bk = (ck % 2) * 2 + half
            for k in range(KT):
                P.mm(C.bank[bk][:, :], h_sb[:, k, 1 + ck * 128:1 + (ck + 1) * 128], wvb[:, k, half * 512:(half + 1) * 512],
                     k == 0, k == KT - 1, r=[twv, th], w=[C.tb[bk]])
            P.act(vt[s][:, half * 512:(half + 1) * 512], C.bank[bk][:, :], AF.Gelu, r=[C.tb[bk]], w=[tvt[s]])
        P.act(vj[:, :], vt[s][:, :], AF.Square, r=[tvt[s]], w=[tvj, tvs[s]], accum_out=vst[s][:, 0:1])
        P.ts("dve", vst[s][:, 1:2], vst[s][:, 0:1], 1.0 / 1024, EPS, ALU.mult, ALU.add, r=[tvs[s]], w=[tvs[s]])
        P.act(vst[s][:, 1:2], vst[s][:, 1:2], AF.Sqrt, r=[tvs[s]], w=[tvs[s]])
        P.add("dve", lambda e, s=s: e.reciprocal(vst[s][:, 1:2], vst[s][:, 1:2]), [tvs[s]], [tvs[s]])
        P.stt("dve", vn[s][:, :], vt[s][:, :], vst[s][:, 1:2], vgb[:, :], ALU.mult, ALU.mult,
              r=[tvt[s], tvs[s], tbs], w=[tvn[s]])
        for g in range(8):
            bk = 4 + (ck % 2) * 2 + g // 4
            P.mm(C.bank[bk][:, (g % 4) * 128:(g % 4 + 1) * 128], vn[s][:, g * 128:(g + 1) * 128], wsT[:, g, :],
                 True, True, r=[tvn[s], tbs], w=[C.tb[bk]])
        for hh in range(2):
            bk = 4 + (ck % 2) * 2 + hh
            P.tt("dve", sg_[s][:, hh * 4:(hh + 1) * 4, :], C.bank[bk][:, :].rearrange("p (g q) -> p g q", q=128),
                 bsb[:, hh * 4:(hh + 1) * 4, :], ALU.add, r=[C.tb[bk], tbs], w=[tsg[s]])
        P.tt("pool", u_sb[:, :, ck * 128:(ck + 1) * 128], u_sb[:, :, ck * 128:(ck + 1) * 128], sg_[s][:, :, :], ALU.mult,
             r=[tsg[s]], w=[tu])
    for g in range(8):
        P.dma("sp", yaT[g * 128:(g + 1) * 128, :], u_sb[:, g, :], r=[tu], w=[tout])


NFT = 33
NFP = NFT * 128
NFFT = 2 * L


def dft_fwd_tables():
    m = (np.arange(32)[None, :, None] * 128 + np.arange(128)[:, None, None]).astype(np.int64)
    outc = np.zeros((NFT, 128, 32, 128), NPBF)
    outs = np.zeros((NFT, 128, 32, 128), NPBF)
    for ft in range(NFT):
        f = (ft * 128 + np.arange(128))[None, None, :].astype(np.int64)
        ph = ((m * f) % NFFT).astype(np.float64) * (2.0 * np.pi / NFFT)
        valid = (f <= L)
        outc[ft] = np.where(valid, np.cos(ph), 0.0).astype(NPBF)
        outs[ft] = np.where(valid, np.sin(ph), 0.0).astype(NPBF)
    return outc, outs


def dft_inv_tables(hf):
    f = (np.arange(NFT)[None, :, None] * 128 + np.arange(128)[:, None, None]).astype(np.int64)
    wf = np.where((f == 0) | (f == L), 1.0, 2.0) / NFFT
    wf = np.where(f <= L, wf, 0.0)
    outc = np.zeros((NT // 128, 128, NFT, 128), NPBF)
    outs = np.zeros((NT // 128, 128, NFT, 128), NPBF)
    for tt in range(NT // 128):
        t = (hf * NT + tt * 128 + np.arange(128))[None, None, :].astype(np.int64)
        ph = ((f * t) % NFFT).astype(np.float64) * (2.0 * np.pi / NFFT)
        outc[tt] = (wf * np.cos(ph)).astype(NPBF)
        outs[tt] = (-wf * np.sin(ph)).astype(NPBF)
    return outc, outs


def hyena_pos_features():
    t = np.linspace(0.0, 1.0, L, dtype=np.float32)[:, None]
    w = (2.0 * np.pi * np.arange(L, dtype=np.float32)[:, None] / L).astype(np.float32)
    bands = np.linspace(1e-4, 15.0, 16, dtype=np.float32)[None, :]
    z = np.concatenate([t, np.cos(w * bands), -np.sin(w * bands)], axis=-1).astype(np.float32)
    return np.ascontiguousarray(z.T)


def hyena_decay_full():
    import math
    t = np.linspace(0.0, 1.0, L, dtype=np.float32)[:, None]
    max_decay = math.log(1e-2) / 0.3
    min_decay = math.log(1e-2) / 1.5
    deltas = np.abs(np.linspace(min_decay, max_decay, 1024, dtype=np.float32))
    return np.ascontiguousarray(np.exp(-t * deltas[None, :]).astype(np.float32))


def hyena_decay(core):
    import math
    t = np.linspace(0.0, 1.0, L, dtype=np.float32)[:, None]
    max_decay = math.log(1e-2) / 0.3
    min_decay = math.log(1e-2) / 1.5
    deltas = np.abs(np.linspace(min_decay, max_decay, 1024, dtype=np.float32))[core * 128:(core + 1) * 128]
    dec = np.exp(-t * deltas[None, :]).astype(np.float32)
    return np.ascontiguousarray(np.tile(dec, (1, 4)))


def stage_filter(nc, P, C, zT, dec, fw1, fb1, fw2, fb2, fw3r, ffreq, fskipr, tabC, tabS, kf_out, tin, tout):
    scr = P.sb([128, 1], F32, name="scrF")
    A_all = P.sb([128, 32, 512], BF16, name="A_all")
    B_all = P.sb([128, 32, 512], BF16, name="B_all")
    tAB = Tok()
    acc = P.sb([128, 512], F32, name="accF")
    tacc = Tok()
    AH = Arena(P, 65536, "arenaH")
    h2 = [AH.view(32768 + i * 16384, [L], F32)[0:64] for i in range(2)]
    th2 = [Tok() for _ in range(2)]
    wsm = P.sb([64, 128], F32, name="wsm")
    w3s = [P.sb([128, 512], BF16, name=f"w3s{i}") for i in range(2)]
    tw3 = [Tok() for _ in range(2)]
    h2b = [P.sb([128, L], BF16, name=f"h2b_{i}") for i in range(2)]
    for i in range(2):
        P.add("pool", lambda e, i=i: e.memset(w3s[i][64:128, :], 0.0), (), [tw3[i]])
        P.add("pool", lambda e, i=i: e.memset(h2b[i][64:128, :], 0.0), (), [th2[i]])
    cols = P.sb([64, 4], F32, name="colsF")
    tws = Tok()
    dsb = P.sb([128, 32, 128], F32, name="dsb")
    tds = Tok()
    tmpa = [P.sb([128, 256], F32, name=f"tmpa{i}") for i in range(2)]
    ttmp = [Tok() for _ in range(2)]
    TWO_PI = 2.0 * np.pi
    qi = P.sb([64, 512], mybir.dt.int32, name="qiF")
    qf = P.sb([64, 512], F32, name="qfF")
    tqi = Tok()
    ones = P.sb([128, 128], F32, name="onesF")
    tones = Tok()
    P.add("dve", lambda e: e.memset(ones[:, :], 1.0), (), [tones])
    rn = P.sb([128, 512], F32, name="rnF")
    trn = Tok()
    skb = P.sb([128, 512], F32, name="skb")
    z_sb = AH.view(0, [L], F32)[0:33]
    h1 = AH.view(16384, [L], F32)[0:64]
    tz, th1 = Tok(), Tok()
    P.dma("sp", z_sb[:, :], zT[:, :], r=tin, w=[tz])
    for li in range(2):
        P.dma("sp", wsm[0:33, 0:64], fw1[li, :, :], r=tin, w=[tws])
        P.dma("sp", wsm[:, 64:128], fw2[li, :, :], r=tin, w=[tws])
        P.dma("sp", cols[:, 0:1], fb1[li, :].rearrange("(p o) -> p o", o=1), r=tin, w=[tws])
        P.dma("sp", cols[:, 1:2], ffreq[li, 0, :].rearrange("(p o) -> p o", o=1), r=tin, w=[tws])
        P.dma("sp", cols[:, 2:3], fb2[li, :].rearrange("(p o) -> p o", o=1), r=tin, w=[tws])
        P.dma("sp", cols[:, 3:4], ffreq[li, 1, :].rearrange("(p o) -> p o", o=1), r=tin, w=[tws])
        for (src, srows, wcol, bcol, dst, tsrc, tdst) in [(z_sb, 33, 0, 0, h1, tz, th1), (h1, 64, 64, 2, h2[li], th1, th2[li])]:
            for cb in range(L // 512):
                bk = cb % 4
                P.mm(C.bank[bk][0:64, :], wsm[0:srows, wcol:wcol + 64], src[0:srows, cb * 512:(cb + 1) * 512], True, True,
                     r=[tws, tsrc], w=[C.tb[bk]])
                dsl = dst[:, cb * 512:(cb + 1) * 512]
                P.ts("dve", dsl, C.bank[bk][0:64, :], cols[:, bcol:bcol + 1],
                     cols[:, bcol + 1:bcol + 2], ALU.add, ALU.mult, r=[C.tb[bk], tws], w=[tdst])
                P.ts("dve", qi[:, :], dsl, 1.0 / TWO_PI, None, ALU.mult, r=[tdst], w=[tqi])
                P.cp("dve", qf[:, :], qi[:, :], r=[tqi], w=[tqi])
                P.stt("dve", dsl, qf[:, :], -TWO_PI, dsl, ALU.mult, ALU.add, r=[tqi], w=[tdst])
                P.ts("dve", dsl, dsl, -3.141592, 3.141592, ALU.max, ALU.min, r=[tdst], w=[tdst])
                P.act(dsl, dsl, AF.Sin, r=[tdst], w=[tdst])
        P.cp("dve", h2b[li][0:64, :], h2[li][:, :], r=[th2[li]], w=[th2[li]])
    hbuf = AH.view(0, [32, 512], F32)
    thb = Tok()
    tC = [AH.view(i * 8192, [32, 128], BF16) for i in range(2)]
    tS = [AH.view(16384 + i * 8192, [32, 128], BF16) for i in range(2)]
    ttab = [Tok() for _ in range(2)]
    ko = [AH.view(32768 + i * 4096, [2, 512], F32) for i in range(2)]
    tko = [Tok() for _ in range(2)]
    barrier(P, scr, [tz, th1, thb] + th2 + ttab + tko)
    for cg in range(8):
        P.add("dve", lambda e: e.memset(acc[:, :], 0.0), (), [tacc])
        load_bcast(P, skb[:, :], fskipr[cg, :], 512, trn)
        P.dma("sp", dsb[:, :, :], dec[:, cg * 128:(cg + 1) * 128].rearrange("(m p) c -> p m c", p=128), r=tin, w=[tds])
        for li in range(2):
            P.dma("pool", w3s[li][0:64, :], fw3r[li, cg, :, :], r=tin, w=[tw3[li]])
            for mt in range(32):
                bk = 4 + mt % 2
                dd = dsb[:, mt, :]
                dbc = bass.AP(tensor=dd.tensor, offset=dd.offset, ap=[list(dd.ap[0]), [0, 4], list(dd.ap[-1])])
                P.mm(C.bank[bk][:, :], h2b[li][:, mt * 128:(mt + 1) * 128], w3s[li][:, :], True, True,
                     r=[th2[li], tw3[li]], w=[C.tb[bk]])
                P.tt("dve", hbuf[:, mt, :].rearrange("p (j c) -> p j c", c=128),
                     C.bank[bk][:, :].rearrange("p (j c) -> p j c", c=128), dbc, ALU.mult,
                     r=[C.tb[bk], tds], w=[thb])
            hv = hbuf.rearrange("p m (o d c) -> p m o d c", o=2, d=2)
            P.add("dve", lambda e, hv=hv: e.memset(hv[0:1, 0, :, 1, :], 0.0), [thb], [thb])
            for mt in range(32):
                ti = mt % 2
                fwd = hv[:, mt, :, 0, :]
                bwd = hv[:, mt, :, 1, :]
                Aout = A_all[:, mt, li * 256:(li + 1) * 256].rearrange("p (o c) -> p o c", o=2)
                Bout = B_all[:, mt, li * 256:(li + 1) * 256].rearrange("p (o c) -> p o c", o=2)
                P.tt("pool", Aout, fwd, bwd, ALU.add, r=[thb], w=[tAB])
                P.tt("pool", Bout, bwd, fwd, ALU.subtract, r=[thb], w=[tAB])
                t3 = tmpa[ti][:, :].rearrange("p (o c) -> p o c", o=2)
                P.act(t3, fwd, AF.Abs, r=[thb], w=[ttmp[ti]])
                P.tt("dve", acc[:, li * 256:(li + 1) * 256], acc[:, li * 256:(li + 1) * 256], tmpa[ti][:, :], ALU.add,
                     r=[ttmp[ti]], w=[tacc])
                P.act(t3, bwd, AF.Abs, r=[thb], w=[ttmp[ti]])
                P.tt("dve", acc[:, li * 256:(li + 1) * 256], acc[:, li * 256:(li + 1) * 256], tmpa[ti][:, :], ALU.add,
                     r=[ttmp[ti]], w=[tacc])
        P.mm(C.bank[6][:, :], ones[:, :], acc[:, :], True, True, r=[tones, tacc], w=[C.tb[6]])
        P.add("dve", lambda e: e.reciprocal(rn[:, :], C.bank[6][:, :]), [C.tb[6]], [trn])
        barrier(P, scr, [thb] + ttab + tko)

        def load_tab(ft):
            b = ft % 2
            P.dma("sp", tC[b][:, :, :], tabC[ft, :, :, :], r=tin, w=[ttab[b]])
            P.dma("sp", tS[b][:, :, :], tabS[ft, :, :, :], r=tin, w=[ttab[b]])
        load_tab(0)
        for ft in range(NFT):
            if ft + 1 < NFT:
                load_tab(ft + 1)
            b = ft % 2
            rows = 128 if ft < NFT - 1 else 1
            br, bi = (ft % 2) * 2, (ft % 2) * 2 + 1
            for mt in range(32):
                P.mm(C.bank[br][0:rows, :], tC[b][:, mt, 0:rows], A_all[:, mt, :], mt == 0, mt == 31, r=[ttab[b], tAB], w=[C.tb[br]])
            for mt in range(32):
                P.mm(C.bank[bi][0:rows, :], tS[b][:, mt, 0:rows], B_all[:, mt, :], mt == 0, mt == 31, r=[ttab[b], tAB], w=[C.tb[bi]])
            P.tt("dve", ko[b][0:rows, 0, :], C.bank[br][0:rows, :], rn[0:rows, :], ALU.mult, r=[C.tb[br], trn], w=[tko[b]])
            P.tt("dve", ko[b][0:rows, 0, :], ko[b][0:rows, 0, :], skb[0:rows, :], ALU.add, r=[trn], w=[tko[b]])
            P.tt("dve", ko[b][0:rows, 1, :], C.bank[bi][0:rows, :], rn[0:rows, :], ALU.mult, r=[C.tb[bi], trn], w=[tko[b]])
            P.dma("sp", kf_out[cg, ft, 0:rows, :, :], ko[b][0:rows, :, :], r=[tko[b]], w=[tout])
        barrier(P, scr, [thb] + ttab + tko)


def stage_conv(nc, P, C, u_full, kf, order, mul_tm, tabC, tabS, invC, invS, out, mode, tin, tout):
    v_sb = P.sb([128, 32, 512], BF16, name="v_sb")
    tv = Tok()
    tC = [P.sb([128, 32, 128], BF16, name=f"tC{i}") for i in range(2)]
    tS = [P.sb([128, 32, 128], BF16, name=f"tS{i}") for i in range(2)]
    ttab = [Tok() for _ in range(2)]
    kb = [P.sb([128, 2, 512], F32, name=f"kb{i}") for i in range(2)]
    tkb = [Tok() for _ in range(2)]
    Y = P.sb([128, NFT, 2, 512], BF16, name="Ysb")
    tY = Tok()
    t1 = [P.sb([128, 512], F32, name=f"cv1_{i}") for i in range(2)]
    t2 = [P.sb([128, 512], F32, name=f"cv2_{i}") for i in range(2)]
    tt1 = [Tok() for _ in range(2)]
    tt2 = [Tok() for _ in range(2)]
    gC = [P.sb([128, NFT, 128], BF16, name=f"gC{i}") for i in range(2)]
    gS = [P.sb([128, NFT, 128], BF16, name=f"gS{i}") for i in range(2)]
    tg = [Tok() for _ in range(2)]
    xm = [P.sb([128, 512], F32, name=f"xm{i}") for i in range(2)]
    txm = [Tok() for _ in range(2)]
    ob = [P.sb([128, 512], BF16, name=f"cob{i}") for i in range(2)]
    tob = [Tok() for _ in range(2)]
    of = [P.sb([128, 512], F32, name=f"cof{i}") for i in range(2)]
    tof = [Tok() for _ in range(2)]
    ntr = 0
    for ch2 in range(2):
        c0 = ch2 * 512
        P.dma("sp", v_sb[:, :, :], u_full[:, c0:c0 + 512].rearrange("(m p) c -> p m c", p=128), r=tin, w=[tv])

        def load_f(ft):
            b = ft % 2
            P.dma("sp", tC[b][:, :, :], tabC[ft, :, :, :], r=tin, w=[ttab[b]])
            P.dma("sp", tS[b][:, :, :], tabS[ft, :, :, :], r=tin, w=[ttab[b]])
            rows = 128 if ft < NFT - 1 else 1
            for ri in range(2):
                P.dma("sp", kb[b][0:rows, ri, :].rearrange("f (k c) -> f k c", c=128),
                      kf[ch2 * 4:(ch2 + 1) * 4, ft, 0:rows, ri, order * 128:(order + 1) * 128].rearrange("k f c -> f k c"),
                      r=tin, w=[tkb[b]])
        load_f(0)
        for ft in range(NFT):
            if ft + 1 < NFT:
                load_f(ft + 1)
            b = ft % 2
            rows = 128 if ft < NFT - 1 else 1
            br, bi = (ft % 2) * 2, (ft % 2) * 2 + 1
            for mt in range(32):
                P.mm(C.bank[br][0:rows, :], tC[b][:, mt, 0:rows], v_sb[:, mt, :], mt == 0, mt == 31, r=[ttab[b], tv], w=[C.tb[br]])
            for mt in range(32):
                P.mm(C.bank[bi][0:rows, :], tS[b][:, mt, 0:rows], v_sb[:, mt, :], mt == 0, mt == 31, r=[ttab[b], tv], w=[C.tb[bi]])
            Vr, Vi = C.bank[br][0:rows, :], C.bank[bi][0:rows, :]
            Kr, Ki = kb[b][0:rows, 0, :], kb[b][0:rows, 1, :]
            P.tt("dve", t1[0][0:rows, :], Vr, Kr, ALU.mult, r=[C.tb[br], tkb[b]], w=[tt1[0]])
            P.tt("dve", t2[0][0:rows, :], Vi, Ki, ALU.mult, r=[C.tb[bi], tkb[b]], w=[tt2[0]])
            P.tt("pool", Y[0:rows, ft, 0, :], t1[0][0:rows, :], t2[0][0:rows, :], ALU.add, r=[tt1[0], tt2[0]], w=[tY])
            P.tt("dve", t1[1][0:rows, :], Vr, Ki, ALU.mult, r=[C.tb[br], tkb[b]], w=[tt1[1]])
            P.tt("dve", t2[1][0:rows, :], Vi, Kr, ALU.mult, r=[C.tb[bi], tkb[b]], w=[tt2[1]])
            P.tt("pool", Y[0:rows, ft, 1, :], t1[1][0:rows, :], t2[1][0:rows, :], ALU.subtract, r=[tt1[1], tt2[1]], w=[tY])

        def load_g(tt_):
            b = tt_ % 2
            P.dma("sp", gC[b][:, :, :], invC[tt_, :, :, :], r=tin, w=[tg[b]])
            P.dma("sp", gS[b][:, :, :], invS[tt_, :, :, :], r=tin, w=[tg[b]])
            P.dma("sp", xm[b][:, :], mul_tm[tt_ * 128:(tt_ + 1) * 128, c0:c0 + 512], r=tin, w=[txm[b]])
        load_g(0)
        for tt_ in range(NT // 128):
            if tt_ + 1 < NT // 128:
                load_g(tt_ + 1)
            b = tt_ % 2
            bo = 4 + tt_ % 2
            for ft in range(NFT):
                rows = 128 if ft < NFT - 1 else 1
                P.mm(C.bank[bo][:, :], gC[b][0:rows, ft, :], Y[0:rows, ft, 0, :], ft == 0, False, r=[tg[b], tY], w=[C.tb[bo]])
                P.mm(C.bank[bo][:, :], gS[b][0:rows, ft, :], Y[0:rows, ft, 1, :], False, ft == NFT - 1, r=[tg[b], tY], w=[C.tb[bo]])
            if mode == "z":
                P.tt("dve", ob[b][:, :], C.bank[bo][:, :], xm[b][:, :], ALU.mult, r=[C.tb[bo], txm[b]], w=[tob[b]])
                P.dma("sp", out[tt_ * 128:(tt_ + 1) * 128, c0:c0 + 512], ob[b][:, :], r=[tob[b]], w=[tout])
            else:
                P.tt("dve", of[b][:, :], C.bank[bo][:, :], xm[b][:, :], ALU.mult, r=[C.tb[bo], txm[b]], w=[tof[b]])
                bk = 6 + ntr % 2
                ntr += 1
                for q in range(4):
                    P.tr(C.bank[bk][:, q * 128:(q + 1) * 128], of[b][:, q * 128:(q + 1) * 128], C.ident[:, :],
                         r=[tof[b], C.tident], w=[C.tb[bk]])
                P.cp("act", ob[b][:, :], C.bank[bk][:, :], r=[C.tb[bk]], w=[tob[b]])
                P.dma("sp", out[c0:c0 + 512, tt_ * 128:(tt_ + 1) * 128].rearrange("(q p) t -> p q t", p=128),
                      ob[b][:, :].rearrange("p (q t) -> p q t", t=128), r=[tob[b]], w=[tout])


def stage_final_norm(nc, P, C, x, g_vec, out, tin, tout):
    gbc = P.sb([128, D], F32, name="gbcN")
    tg = Tok()
    load_bcast(P, gbc[:, :], g_vec, D, tg)
    xt = [P.sb([128, D], F32, name=f"fx{i}") for i in range(2)]
    xs = [P.sb([128, D], F32, name=f"fs{i}") for i in range(2)]
    ss = [P.sb([128, 1], F32, name=f"fss{i}") for i in range(2)]
    rs = [P.sb([128, 1], F32, name=f"frs{i}") for i in range(2)]
    tx = [Tok() for _ in range(2)]
    txs = [Tok() for _ in range(2)]
    tst = [Tok() for _ in range(2)]
    for i in range(NT // 128):
        b = i % 2
        P.dma("sp", xt[b][:, :], x[i * 128:(i + 1) * 128, :], r=tin, w=[tx[b]])
        rms_rows(P, C, xt[b], 128, ss[b], rs[b], xs[b], tx[b], txs[b], tst[b])
        P.stt("dve", xs[b][:, :], xt[b][:, :], rs[b][:, 0:1], gbc[:, :], ALU.mult, ALU.mult, r=[tx[b], tst[b], tg], w=[txs[b]])
        P.dma("sp", out[i * 128:(i + 1) * 128, :], xs[b][:, :], r=[txs[b]], w=[tout])


def _new():
    nc = bass.Bass("TRN2", target_bir_lowering=False)

    def dt(n, s, t, k="ExternalInput"):
        return nc.dram_tensor(n, s, t, kind=k).ap()
    return nc, dt


def build_filter():
    nc, dt = _new()
    identd = dt("ident", [128, 128], F32)
    zT = dt("zT", [33, L], F32)
    dec4 = dt("dec4", [L, 512], F32)
    fw1 = dt("fw1", [2, 33, 64], F32)
    fb1 = dt("fb1", [2, 64], F32)
    fw2 = dt("fw2", [2, 64, 64], F32)
    fb2 = dt("fb2", [2, 64], F32)
    fw3c = dt("fw3c", [2, 64, 512], F32)
    ffreq = dt("ffreq", [2, 2, 64], F32)
    fskip = dt("fskip", [512], F32)
    tabC = dt("tabC", [NFT, 128, 32, 128], BF16)
    tabS = dt("tabS", [NFT, 128, 32, 128], BF16)
    kf_out = dt("kf_out", [2, NFP, 512], F32, "ExternalOutput")
    P = Prog(nc)
    C = Ctx(P, identd)
    tout = Tok()
    stage_filter(nc, P, C, zT, dec4, fw1, fb1, fw2, fb2, fw3c, ffreq, fskip, tabC, tabS, kf_out, [], tout)
    P.finalize(final_reads=[tout])
    return nc


def build_ab_in():
    nc, dt = _new()
    identd = dt("ident", [128, 128], F32)
    xext = dt("xext", [NT + 2, D], F32)
    g = dt("g", [D], F32)
    w = dt("w", [D, 5120], F32)
    vnorm = dt("vnorm", [1024], F32)
    a_ws = dt("a_ws", [8, 128, 128], F32)
    a_bs = dt("a_bs", [8, 128], F32)
    bcw = dt("bcw", [3, 3072], F32)
    bcb = dt("bcb", [3072], F32)
    yaT = dt("yaT", [1024, NT], BF16, "ExternalOutput")
    v_tm = dt("v_tm", [NT, 1024], BF16, "ExternalOutput")
    x1_tm = dt("x1_tm", [NT, 1024], F32, "ExternalOutput")
    x2_tm = dt("x2_tm", [NT, 1024], F32, "ExternalOutput")
    P = Prog(nc)
    C = Ctx(P, identd)
    tout = Tok()
    stage_ab_in(nc, P, C, xext, g, w, vnorm, a_ws, a_bs, bcw, bcb, yaT, v_tm, x1_tm, x2_tm, [], tout)
    P.finalize(final_reads=[tout])
    return nc


def build_conv(order, mode):
    nc, dt = _new()
    identd = dt("ident", [128, 128], F32)
    u = dt("u", [L, 1024], BF16)
    kf = dt("kf", [2, NFP, 2, 1024], F32)
    mul = dt("mul", [NT, 1024], F32)
    tabC = dt("tabC", [NFT, 128, 32, 128], BF16)
    tabS = dt("tabS", [NFT, 128, 32, 128], BF16)
    invC = dt("invC", [16, 128, NFT, 128], BF16)
    invS = dt("invS", [16, 128, NFT, 128], BF16)
    out = dt("out", [NT, 1024] if mode == "z" else [1024, NT], BF16, "ExternalOutput")
    P = Prog(nc)
    C = Ctx(P, identd)
    tout = Tok()
    stage_conv(nc, P, C, u, kf, order, mul, tabC, tabS, invC, invS, out, mode, [], tout)
    P.finalize(final_reads=[tout])
    return nc


def build_cd_in():
    nc, dt = _new()
    identd = dt("ident", [128, 128], F32)
    x = dt("x", [NT, D], F32)
    g = dt("g", [D], F32)
    w = dt("w", [D, 6144], F32)
    cs = dt("cs", [128, NT], F32)
    sn = dt("sn", [128, NT], F32)
    rm = dt("rm", [128, 128], BF16)
    o = {n: dt(n, s, BF16, "ExternalOutput") for n, s in [("qcT", [1024, NT]), ("kcT", [1024, NT]), ("vc", [NT, 1024]),
                                                          ("qdT", [1024, NT]), ("kdT", [1024, NT]), ("vd", [NT, 1024])]}
    P = Prog(nc)
    C = Ctx(P, identd)
    tout = Tok()
    stage_cd_in(nc, P, C, x, g, w, cs, sn, rm, o["qcT"], o["kcT"], o["vc"], o["qdT"], o["kdT"], o["vd"], [], tout)
    P.finalize(final_reads=[tout])
    return nc


def build_attn():
    nc, dt = _new()
    identd = dt("ident", [128, 128], F32)
    qcT = dt("qcT", [1024, NT], BF16)
    kcx = dt("kcx", [1024, NKEXT], BF16)
    vcx = dt("vcx", [NKEXT, 1024], BF16)
    btab = dt("btab", [8, 5, 128, 6, 128], F32)
    qdT = dt("qdT", [1024, NT], BF16)
    kdT = dt("kdTf", [1024, L], BF16)
    vdF = dt("vdF", [L, 1024], BF16)
    dlam = dt("dlam", [4, 64], F32)
    lamc = dt("lamc", [2], F32)
    subln = dt("subln", [128], F32)
    yT = dt("yT", [D, NT], BF16, "ExternalOutput")
    P = Prog(nc)
    C = Ctx(P, identd)
    tout = Tok()
    stage_attn(nc, P, C, qcT, kcx, vcx, btab, qdT, kdT, vdF, dlam, lamc, subln, yT, [], tout)
    P.finalize(final_reads=[tout])
    return nc


def build_ffn():
    nc, dt = _new()
    identd = dt("ident", [128, 128], F32)
    yT = dt("yT", [D, NT + 2], BF16)
    xext = dt("xext", [NT + 2, D], F32)
    w_out = dt("w_out", [D, D], F32)
    g = dt("g", [D], F32)
    w_up = dt("w_up", [D, 2 * FF], F32)
    cw = dt("cw", [3, 2 * FF], F32)
    cb = dt("cb", [2 * FF], F32)
    w_down = dt("w_down", [FF, D], F32)
    xmid = dt("xmid", [NT + 2, D], F32, "Internal")
    xout = dt("xout", [NT, D], F32, "ExternalOutput")
    P = Prog(nc)
    C = Ctx(P, identd)
    tout = Tok()
    stage_ffn(nc, P, C, yT, xext, w_out, g, w_up, cw, cb, w_down, xmid, xout, [], tout)
    P.finalize(final_reads=[tout])
    return nc


def build_final():
    nc, dt = _new()
    identd = dt("ident", [128, 128], F32)
    x = dt("x", [NT, D], F32)
    g = dt("g", [D], F32)
    out = dt("out", [NT, D], F32, "ExternalOutput")
    P = Prog(nc)
    C = Ctx(P, identd)
    tout = Tok()
    stage_final_norm(nc, P, C, x, g, out, [], tout)
    P.finalize(final_reads=[tout])
    return nc


PAIRS = [[0, 1], [2, 3], [4, 5], [6, 7]]
QUADS = [[0, 1, 2, 3], [4, 5, 6, 7]]
CROSS = [[0, 4], [1, 5], [2, 6], [3, 7]]
ARENA_BYTES = 206 * 1024


def build_fused(nlayers=4, final=True, do_filter=True, dbg_kf=False):
    global _LIVE_TOKS
    _LIVE_TOKS = []
    nc, dt = _new()
    it = lambda n, s, t: dt(n, s, t, "Internal")
    used = []

    def ein(n, s, t):
        used.append(n)
        return dt(n, s, t)
    identd = ein("ident", [128, 128], F32)
    x = ein("x", [NT, D], F32)
    LW = lambda nm, s: [ein(f"{nm}_L{l}", s, F32) if l < nlayers else None for l in range(4)]
    EW = lambda nm, s: [ein(f"{nm}_L{i}", s, F32) if 2 * i < nlayers else None for i in range(2)]
    OW = lambda nm, s: [ein(f"{nm}_L{i}", s, F32) if 2 * i + 1 < nlayers else None for i in range(2)]
    norm_mix = LW("norm_mix", [D])
    norm_ffn = LW("norm_ffn", [D])
    w_out = LW("w_out", [D, D])
    ffn_up = LW("ffn_up", [D, 2 * FF])
    ffn_cw = LW("ffn_conv_w", [3, 2 * FF])
    ffn_cb = LW("ffn_conv_b", [2 * FF])
    ffn_down = LW("ffn_down", [FF, D])
    final_norm = ein("final_norm", [D], F32) if final else None
    ab_w_in = EW("ab_w_in", [D, 5120])
    a_vnorm = EW("a_vnorm", [1024])
    a_ws = EW("a_ws", [8, 128, 128])
    a_bs = EW("a_bs", [8, 128])
    b_conv_w = EW("b_conv_w", [3, 3072])
    b_conv_b = EW("b_conv_b", [3072])
    cd_w_in = OW("cd_w_in", [D, 6144])
    d_lambda = OW("d_lambda", [4, 64])
    d_subln = OW("d_subln", [128])
    btab = OW("btab", [8, 5, 128, 6, 128])
    lamc = OW("lamc", [2])
    if do_filter:
        fw1 = ein("b_filt_w1", [2, 33, 64], F32)
        fb1 = ein("b_filt_b1", [2, 64], F32)
        fw2 = ein("b_filt_w2", [2, 64, 64], F32)
        fb2 = ein("b_filt_b2", [2, 64], F32)
        ffreq = ein("b_filt_freq", [2, 2, 64], F32)
        fw3r = ein("fw3r", [2, 8, 64, 512], F32)
        fskipr = ein("fskipr", [8, 512], F32)
        dec = ein("dec", [L, 1024], F32)
        zT = ein("zT", [33, L], F32)
    if do_filter or nlayers > 0:
        tabC = ein("tabC", [NFT, 128, 32, 128], BF16)
        tabS = ein("tabS", [NFT, 128, 32, 128], BF16)
    if nlayers > 0:
        invC = ein("invC", [16, 128, NFT, 128], BF16)
        invS = ein("invS", [16, 128, NFT, 128], BF16)
    if nlayers > 1:
        cs = ein("cs", [128, NT], F32)
        sn = ein("sn", [128, NT], F32)
        rm = ein("rm", [128, 128], BF16)
    mskd = ein("msk", [128, 2], F32)
    out = dt("out", [NT, D], F32, "ExternalOutput")
    nc._used_inputs = used
    XE = [it("XEa", [NT + 2, D], F32), it("XEb", [NT + 2, D], F32)]
    xmid = it("xmid", [NT + 2, D], F32)
    yT = it("yT", [D, NT], BF16)
    v_tm = it("v_tm", [NT, 1024], BF16)
    v_full = it("v_full", [2 * NT, 1024], BF16)
    z_tm = it("z_tm", [NT, 1024], BF16)
    z_full = it("z_full", [2 * NT, 1024], BF16)
    x1_tm = it("x1_tm", [NT, 1024], F32)
    x2_tm = it("x2_tm", [NT, 1024], F32)
    qcT = it("qcT", [1024, NT], BF16)
    kcT = it("kcT", [1024, NT], BF16)
    qdT = it("qdT", [1024, NT], BF16)
    kdT = it("kdT", [1024, NT], BF16)
    vc = it("vc", [NT, 1024], BF16)
    vd = it("vd", [NT, 1024], BF16)
    kcb = it("kcb", [1024, 512], BF16)
    kcbg = it("kcbg", [2048, 512], BF16)
    vcb = it("vcb", [512, 1024], BF16)
    vcbg = it("vcbg", [1024, 1024], BF16)
    kdTg = it("kdTg", [2048, NT], BF16)
    vdg = it("vdg", [2 * NT, 1024], BF16)
    kcx = it("kcx", [1024, NKEXT], BF16)
    vcx = it("vcx", [NKEXT, 1024], BF16)
    kdTf = it("kdTf", [1024, L], BF16)
    xb = it("xb", [2, D], F32)
    xbg = it("xbg", [4, D], F32)
    kfall = it("kfall", [8, NFT, 128, 2, 512], F32)
    gch = [it(f"gch{i}", [2048, 1024], BF16) for i in range(2)]
    gchT = [it(f"gchT{i}", [1024, NT], BF16) for i in range(2)]
    vdF = it("vdF", [L, 1024], BF16)

    P = Prog(nc)
    C = Ctx(P, identd)
    msk = P.sb([128, 2], F32, name="msk")
    tmsk = Tok("msk", persist=True)
    P.dma("sp", msk[:, :], mskd[:, :], w=[tmsk])
    P.use_arena(ARENA_BYTES)
    txb = Tok("xb", persist=True)
    txbg = Tok("xbg", persist=True)

    def hx(X, tX):
        thb = Tok()
        P.dma("sp", xb[0:1, :], X[1:2, :], r=[tX], w=[txb])
        P.dma("sp", xb[1:2, :], X[NT:NT + 1, :], r=[tX], w=[txb])
        P.allgather(PAIRS, xb, xbg, r=[txb], w=[txbg])
        hb = P.sb([128, 2, 16], F32, name="hb")
        spread = lambda row: row.rearrange("o (p f) -> (o p) f", p=128)
        P.dma("sp", hb[:, 0, :], spread(xbg[1:2, :]), r=[txbg], w=[thb])
        P.dma("sp", hb[:, 1, :], spread(xbg[2:3, :]), r=[txbg], w=[thb])
        P.ts("dve", hb[:, 0, :], hb[:, 0, :], msk[:, 0:1], None, ALU.mult, r=[thb, tmsk], w=[thb])
        P.ts("dve", hb[:, 1, :], hb[:, 1, :], msk[:, 1:2], None, ALU.mult, r=[thb, tmsk], w=[thb])
        P.dma("sp", spread(X[0:1, :]), hb[:, 0, :], r=[thb], w=[tX])
        P.dma("sp", spread(X[NT + 1:NT + 2, :]), hb[:, 1, :], r=[thb], w=[tX])

    tX = Tok("XE", persist=True)
    for i in range(16):
        P.dma("sp", XE[0][1 + i * 128:1 + (i + 1) * 128, :], x[i * 128:(i + 1) * 128, :], w=[tX])
    hx(XE[0], tX)
    P.stage_end()
    tkl = Tok("kfl")
    if do_filter:
        stage_filter(nc, P, C, zT, dec, fw1, fb1, fw2, fb2, fw3r, ffreq, fskipr, tabC, tabS, kfall, [], tkl)
        P.stage_end()

    def gather_tm(src, dst):
        t1, t2 = Tok(), Tok()
        for ch in range(2):
            P.allgather(PAIRS, src[ch * 1024:(ch + 1) * 1024, :], gch[ch], r=[], w=[t1])
        for ch in range(2):
            for r_ in range(2):
                P.dma("sp", dst[r_ * NT + ch * 1024:r_ * NT + (ch + 1) * 1024, :], gch[ch][r_ * 1024:(r_ + 1) * 1024, :],
                      r=[t1], w=[t2])
        return t2

    def gather_fm(src, dst):
        t1, t2 = Tok(), Tok()
        for ch in range(2):
            P.allgather(PAIRS, src[ch * 512:(ch + 1) * 512, :], gchT[ch], r=[], w=[t1])
        for ch in range(2):
            for r_ in range(2):
                P.dma("sp", dst[ch * 512:(ch + 1) * 512, r_ * NT:(r_ + 1) * NT], gchT[ch][r_ * 512:(r_ + 1) * 512, :],
                      r=[t1], w=[t2])
        return t2
    cur = 0
    for l in range(nlayers):
        i = l // 2
        Xc, Xn = XE[cur], XE[1 - cur]
        ty = Tok("yT")
        if l % 2 == 0:
            stage_ab_in(nc, P, C, Xc, norm_mix[l], ab_w_in[i], a_vnorm[i], a_ws[i], a_bs[i], b_conv_w[i], b_conv_b[i],
                        yT[0:1024, :], v_tm, x1_tm, x2_tm, [], ty)
            P.stage_end()
            tvf = gather_tm(v_tm, v_full)
            stage_conv(nc, P, C, v_full, kfall, i * 2 + 0, x1_tm, tabC, tabS, invC, invS, z_tm, "z", [tvf], ty)
            P.stage_end()
            tzf = gather_tm(z_tm, z_full)
            stage_conv(nc, P, C, z_full, kfall, i * 2 + 1, x2_tm, tabC, tabS, invC, invS, yT[1024:2048, :], "yb", [tzf], ty)
            P.stage_end()
        else:
            stage_cd_in(nc, P, C, Xc[1:NT + 1, :], norm_mix[l], cd_w_in[i], cs, sn, rm, qcT, kcT, vc, qdT, kdT, vd, [], ty)
            P.stage_end()
            tg_ = Tok("gath")
            P.dma("sp", kcb[:, 0:256], kcT[:, 0:256], w=[tg_])
            P.dma("sp", kcb[:, 256:512], kcT[:, NT - 256:NT], w=[tg_])
            P.dma("sp", vcb[0:256, :], vc[0:256, :], w=[tg_])
            P.dma("sp", vcb[256:512, :], vc[NT - 256:NT, :], w=[tg_])
            tg2 = Tok("gath2")
            P.allgather(PAIRS, kcb, kcbg, r=[tg_], w=[tg2])
            P.allgather(PAIRS, vcb, vcbg, r=[tg_], w=[tg2])
            tk_ = gather_fm(kdT, kdTf)
            tv_ = gather_tm(vd, vdF)
            tg3 = Tok("gath3")
            P.dma("sp", kcx[:, 0:256], kcbg[0:1024, 256:512], r=[tg2], w=[tg3])
            P.dma("sp", kcx[:, 256:256 + NT], kcT[:, :], r=[tg2], w=[tg3])
            P.dma("sp", kcx[:, 256 + NT:NKEXT], kcbg[1024:2048, 0:256], r=[tg2], w=[tg3])
            P.dma("sp", vcx[0:256, :], vcbg[256:512, :], r=[tg2], w=[tg3])
            P.dma("sp", vcx[256:256 + NT, :], vc[:, :], r=[tg2], w=[tg3])
            P.dma("sp", vcx[256 + NT:NKEXT, :], vcbg[512:768, :], r=[tg2], w=[tg3])
            stage_attn(nc, P, C, qcT, kcx, vcx, btab[i], qdT, kdTf, vdF, d_lambda[i], lamc[i], d_subln[i], yT, [tg3, tk_, tv_], ty)
            P.stage_end()
        last = (l == nlayers - 1)
        stage_ffn(nc, P, C, yT, Xc, w_out[l], norm_ffn[l], ffn_up[l], ffn_cw[l], ffn_cb[l], ffn_down[l], xmid, Xn,
                  [ty], tX, hx, hx_out=(not last and (l + 1) % 2 == 0))
        P.stage_end()
        cur = 1 - cur
    tout = Tok("out")
    if final:
        stage_final_norm(nc, P, C, XE[cur][1:NT + 1, :], final_norm, out, [tX], tout)
    else:
        for i in range(16):
            P.dma("sp", out[i * 128:(i + 1) * 128, :], XE[cur][1 + i * 128:1 + (i + 1) * 128, :], r=[tX], w=[tout])
    P.finalize(final_reads=[tout])
    return nc


def host_inputs(inp, used=None):
    import math
    f32 = lambda a: np.ascontiguousarray(np.asarray(a, dtype=np.float32))
    inp = {k: f32(v) for k, v in inp.items()}
    need = lambda n: used is None or n in used
    shared = {"ident": np.eye(128, dtype=np.float32)}
    if need("tabC"):
        shared["tabC"], shared["tabS"] = dft_fwd_tables()
    inv = [dft_inv_tables(hf) for hf in range(2)] if need("invC") else None
    ropes = [rope_tables(hf) for hf in range(2)]
    shared["rm"] = rot_matrix()
    shared["zT"] = hyena_pos_features()
    shared["fw3r"] = np.ascontiguousarray(inp["b_filt_w3"].reshape(2, 64, 2, 2, 8, 128).transpose(0, 4, 1, 2, 3, 5).reshape(2, 8, 64, 512))
    shared["fskipr"] = np.ascontiguousarray(inp["b_skip"].reshape(2, 2, 8, 128).transpose(2, 0, 1, 3).reshape(8, 512))
    if need("dec"):
        shared["dec"] = hyena_decay_full()
    for k in ["final_norm", "b_filt_w1", "b_filt_b1", "b_filt_w2", "b_filt_b2", "b_filt_freq"]:
        shared[k] = inp[k]
    for k in ["norm_mix", "norm_ffn", "w_out", "ffn_up", "ffn_conv_w", "ffn_conv_b", "ffn_down"]:
        for l in range(4):
            if need(f"{k}_L{l}"):
                shared[f"{k}_L{l}"] = np.ascontiguousarray(inp[k][l])
    for k in ["ab_w_in", "a_vnorm", "a_ws", "a_bs", "b_conv_w", "b_conv_b", "cd_w_in", "d_lambda", "d_subln"]:
        for i in range(2):
            if need(f"{k}_L{i}"):
                shared[f"{k}_L{i}"] = np.ascontiguousarray(inp[k][i])
    for i, l in enumerate((1, 3)):
        li = 0.8 - 0.6 * math.exp(-0.3 * l)
        shared[f"lamc_L{i}"] = np.array([li, 1.0 - li], np.float32)
    btabs = [[na_bias_tables(inp["c_rpb"][i], hf) if need(f"btab_L{i}") else None for i in range(2)] for hf in range(2)]
    maps = []
    for c in range(NCORE):
        b, hf = c // 2, c % 2
        m = dict(shared)
        m["x"] = np.ascontiguousarray(inp["x"][b, hf * NT:(hf + 1) * NT])
        if inv is not None:
            m["invC"], m["invS"] = inv[hf]
        for i in range(2):
            if btabs[hf][i] is not None:
                m[f"btab_L{i}"] = btabs[hf][i]
        m["cs"], m["sn"] = ropes[hf]
        mk = np.zeros((128, 2), np.float32)
        mk[:, 0] = 1.0 if hf == 1 else 0.0
        mk[:, 1] = 1.0 if hf == 0 else 0.0
        m["msk"] = mk
        if used is not None:
            m = {k: v for k, v in m.items() if k in used}
        maps.append(m)
    return maps


def kernel(**inp):
    nc = build_fused()
    maps = host_inputs(inp, set(nc._used_inputs))
    res = run_bass_kernel_spmd(nc, maps, core_ids=list(range(NCORE))).results
    out = np.zeros((4, L, D), np.float32)
    for c in range(NCORE):
        out[c // 2, (c % 2) * NT:(c % 2 + 1) * NT] = res[c]["out"]
    return out
```

```python
import numpy as np
from contextlib import ExitStack
import ml_dtypes
import concourse.bass as bass
import concourse.mybir as mybir
from concourse.bass_utils import run_bass_kernel_spmd

F32 = mybir.dt.float32
BF16 = mybir.dt.bfloat16
AF = mybir.ActivationFunctionType
ALU = mybir.AluOpType
AX = mybir.AxisListType
NPBF = ml_dtypes.bfloat16

D = 2048
NT = 2048
L = 4096
FF = 5632
EPS = 1e-6
NCORE = 8


_LIVE_TOKS = []


class Tok:
    __slots__ = ("name", "w", "rs", "persist")

    def __init__(self, name="", persist=False, track=True):
        self.name = name
        self.w = None
        self.rs = []
        self.persist = persist
        if track:
            _LIVE_TOKS.append(self)


class _Op:
    __slots__ = ("eng", "emit", "deps", "dma", "sig", "sem", "val", "cc", "idx", "chain")


class Prog:
    ENGS = ("pe", "act", "dve", "pool", "sp")
    NDMA_SEM = 8

    def __init__(self, nc, same_engine_sync=True):
        self.nc = nc
        self.ops = []
        self.stack = ExitStack()
        self.same = same_engine_sync
        self.nt = 0

    def sb(self, shape, dt, name=None):
        self.nt += 1
        if getattr(self, "arena", None) is None:
            return self.stack.enter_context(self.nc.sbuf_tensor(f"sb{self.nt}_{name or ''}", list(shape), dt))
        esz = 2 if dt == BF16 else 4
        n = 1
        for d_ in shape[1:]:
            n *= d_
        nbytes = (n * esz + 63) // 64 * 64
        off = self.aoff
        assert off + nbytes <= self.abytes, f"arena overflow: {name} {shape} off={off} need={nbytes} cap={self.abytes}"
        self.aoff += nbytes
        ap = self.arena[:, off // 4: off // 4 + nbytes // 4]
        if dt != F32:
            ap = ap.bitcast(dt)
        ap = ap[:, 0:n]
        fs = list(shape[1:])
        if len(fs) == 2:
            ap = ap.rearrange("p (a b) -> p a b", b=fs[1])
        elif len(fs) == 3:
            ap = ap.rearrange("p (a b c) -> p a b c", b=fs[1], c=fs[2])
        if shape[0] < 128:
            ap = ap[0:shape[0]]
        return ap

    def use_arena(self, nbytes):
        self.arena = None
        self.fscr = self.sb([128, 8], F32, name="fence_scr")
        self.arena = self.sb([128, nbytes // 4], F32, name="arena_main")
        self.abytes = nbytes
        self.aoff = 0

    def stage_end(self):
        global _LIVE_TOKS
        toks = list(_LIVE_TOKS)
        fs = self.fscr
        self.add("dve", lambda e: e.memset(fs[0:1, 0:1], 0.0), toks, toks)
        tf = Tok("fence", track=False)
        self.add("dve", lambda e: e.memset(fs[0:1, 1:2], 0.0), toks, [tf])
        for e_ in ("pe", "act", "pool", "sp"):
            self.add(e_, None, [tf], ())
        _LIVE_TOKS = [t for t in toks if t.persist]
        self.aoff = 0

    def ps(self, shape, dt, name=None):
        self.nt += 1
        return self.stack.enter_context(self.nc.psum_tensor(f"ps{self.nt}_{name or ''}", list(shape), dt))

    def add(self, eng, emit, reads=(), writes=(), dma=False, cc=False, chain=True):
        op = _Op()
        op.chain = chain
        op.cc = cc
        op.eng = eng
        op.emit = emit
        op.dma = dma
        op.sig = dma
        op.sem = None
        op.val = 0
        deps = []
        for t in reads:
            if t.w is not None:
                deps.append(t.w)
        for t in writes:
            if t.w is not None:
                deps.append(t.w)
            deps.extend(t.rs)
        for t in reads:
            t.rs.append(op)
        for t in writes:
            t.w = op
            t.rs = []
        seen = set()
        d2 = []
        for d in deps:
            if id(d) in seen or d is op:
                continue
            seen.add(id(d))
            d2.append(d)
        op.deps = d2
        self.ops.append(op)
        return op

    def dma(self, q, out, in_, r=(), w=()):
        return self.add(q, lambda e: e.dma_start(out=out, in_=in_), r, w, dma=True)

    def allgather(self, groups, src, dst, r=(), w=(), chain=True):
        return self.add("pool", lambda e: e.collective_compute("AllGather", ALU.bypass, replica_groups=groups,
                                                               ins=[src.opt()], outs=[dst.opt()]), r, w, dma=True, cc=True,
                        chain=chain)

    def mm(self, out, lhsT, rhs, start, stop, r=(), w=()):
        return self.add("pe", lambda e: e.matmul(out, lhsT, rhs, start=start, stop=stop), r, w)

    def tr(self, out, in_, ident, r=(), w=()):
        return self.add("pe", lambda e: e.transpose(out, in_, ident), r, w)

    def act(self, out, in_, func, r=(), w=(), **kw):
        return self.add("act", lambda e: e.activation(out=out, in_=in_, func=func, **kw), r, w)

    def ts(self, eng, out, in0, s1, s2, op0, op1=None, r=(), w=(), **kw):
        if op1 is None:
            return self.add(eng, lambda e: e.tensor_scalar(out, in0, s1, s2, op0, **kw), r, w)
        return self.add(eng, lambda e: e.tensor_scalar(out, in0, s1, s2, op0, op1, **kw), r, w)

    def tt(self, eng, out, in0, in1, op, r=(), w=()):
        return self.add(eng, lambda e: e.tensor_tensor(out, in0, in1, op), r, w)

    def stt(self, eng, out, in0, scalar, in1, op0, op1, r=(), w=()):
        return self.add(eng, lambda e: e.scalar_tensor_tensor(out, in0, scalar, in1, op0, op1), r, w)

    def cp(self, eng, out, in_, r=(), w=()):
        if eng == "act":
            return self.add("act", lambda e: e.activation(out=out, in_=in_, func=AF.Copy), r, w)
        return self.add(eng, lambda e: e.tensor_copy(out=out, in_=in_), r, w)

    def _need(self, op, d):
        if d.dma:
            return True
        if d.eng != op.eng:
            return True
        if op.dma:
            return True
        if op.eng == "pe" or op.emit is None:
            return False
        return self.same

    def finalize(self, final_reads=()):
        nc = self.nc
        self.add("sp", None, reads=final_reads)
        per = {e: [] for e in self.ENGS}
        for op in self.ops:
            per[op.eng].append(op)
        for e in self.ENGS:
            for idx, op in enumerate(per[e]):
                op.idx = idx
        for op in self.ops:
            best = {}
            keep = []
            for d in op.deps:
                if d.dma:
                    keep.append(d)
                else:
                    b = best.get(d.eng)
                    if b is None or d.idx > b.idx:
                        best[d.eng] = d
            op.deps = keep + list(best.values())
        for op in self.ops:
            for d in op.deps:
                if self._need(op, d) and not d.dma:
                    d.sig = True
        sems = {}
        for e in self.ENGS:
            sems[e] = self.stack.enter_context(nc.semaphore(f"s_{e}"))
        dsem = {}
        for e in self.ENGS:
            if any(o.dma for o in per[e]):
                dsem[e] = [self.stack.enter_context(nc.semaphore(f"d_{e}{i}")) for i in range(self.NDMA_SEM)]
        ccsem = self.stack.enter_context(nc.semaphore("s_cc"))
        ncc = 0
        prevcc = None
        for e in self.ENGS:
            cnt = 0
            k = 0
            prev = [None] * self.NDMA_SEM
            for op in per[e]:
                if op.cc:
                    ncc += 1
                    op.sem = ccsem
                    op.val = ncc
                    if op.chain:
                        if prevcc is not None:
                            op.deps.append(prevcc)
                    prevcc = op
                elif op.dma:
                    j = k % self.NDMA_SEM
                    op.sem = dsem[e][j]
                    op.val = 16 * (k // self.NDMA_SEM + 1)
                    if prev[j] is not None:
                        op.deps.append(prev[j])
                    prev[j] = op
                    k += 1
                elif op.sig:
                    cnt += 1
                    op.sem = sems[e]
                    op.val = cnt
        block = self.stack.enter_context(nc.Block())

        def make(e):
            def body(eng):
                waited = {}
                for op in per[e]:
                    for d in op.deps:
                        if not self._need(op, d):
                            continue
                        key = id(d.sem)
                        if waited.get(key, 0) >= d.val:
                            continue
                        eng.wait_ge(d.sem, d.val)
                        waited[key] = d.val
                    if op.emit is None:
                        continue
                    ins = op.emit(eng)
                    if op.cc:
                        ins.then_inc(op.sem)
                    elif op.sig:
                        ins.then_inc(op.sem, 16 if op.dma else 1)
            return body

        block.tensor(make("pe"))
        block.scalar(make("act"))
        block.vector(make("dve"))
        block.gpsimd(make("pool"))
        block.sync(make("sp"))
        self.stack.close()


class Ctx:
    def __init__(self, P, ident_dram):
        self.P = P
        self.bank = [P.ps([128, 512], F32, name=f"bank{i}") for i in range(8)]
        self.tb = [Tok(f"bank{i}", persist=True) for i in range(8)]
        self.ident = P.sb([128, 128], F32, name="ident_sb")
        self.tident = Tok("ident", persist=True)
        P.dma("sp", self.ident[:], ident_dram[:, :], w=[self.tident])
        self.rr = 0

    def eng2(self):
        self.rr += 1
        return "act" if self.rr % 2 else "dve"


def rms_rows(P, C, xt, rows, ss, rstd, junk, tx, tjunk, tstat):
    P.act(junk[:rows, :], xt[:rows, :], AF.Square, r=[tx], w=[tjunk, tstat], accum_out=ss[:rows, :])
    P.ts("dve", rstd[:rows, :], ss[:rows, :], 1.0 / D, EPS, ALU.mult, ALU.add, r=[tstat], w=[tstat])
    P.act(rstd[:rows, :], rstd[:rows, :], AF.Sqrt, r=[tstat], w=[tstat])
    P.add("dve", lambda e: e.reciprocal(rstd[:rows, :], rstd[:rows, :]), [tstat], [tstat])


def norm_transpose_tile(P, C, src_rows, tsrc, rows, gbc, tgbc, dst, dstcol, tdst, bufs, banks, gcol=None):
    xt, tx, junk, tjunk, ss, rstd, tstat, xs, txs = bufs
    P.dma("sp", xt[:rows, :], src_rows, r=tsrc, w=[tx])
    rms_rows(P, C, xt, rows, ss, rstd, junk, tx, tjunk, tstat)
    if gcol is None:
        P.stt("dve", xs[:rows, :], xt[:rows, :], rstd[:rows, 0:1], gbc[:rows, :], ALU.mult, ALU.mult,
              r=[tx, tstat, tgbc], w=[txs])
    else:
        P.ts("dve", xs[:rows, :], xt[:rows, :], rstd[:rows, 0:1], None, ALU.mult, r=[tx, tstat], w=[txs])
    for q in range(4):
        bk = banks[q % len(banks)]
        for i in range(4):
            k = q * 4 + i
            P.tr(C.bank[bk][:, i * 128:i * 128 + rows], xs[:rows, k * 128:(k + 1) * 128], C.ident[:rows, :rows],
                 r=[txs, C.tident], w=[C.tb[bk]])
        if gcol is None:
            src = C.bank[bk][:, :].rearrange("p (i t) -> p i t", t=128)[:, :, 0:rows]
            P.cp(C.eng2(), dst[:, q * 4:(q + 1) * 4, dstcol:dstcol + rows], src, r=[C.tb[bk]], w=[tdst])
        else:
            for i in range(4):
                k = q * 4 + i
                if C.eng2() == "act":
                    P.act(dst[:, k, dstcol:dstcol + rows], C.bank[bk][:, i * 128:i * 128 + rows], AF.Copy,
                          r=[C.tb[bk], tgbc], w=[tdst], scale=gcol[:, k:k + 1])
                else:
                    P.ts("dve", dst[:, k, dstcol:dstcol + rows], C.bank[bk][:, i * 128:i * 128 + rows], gcol[:, k:k + 1], None,
                         ALU.mult, r=[C.tb[bk], tgbc], w=[tdst])


def load_bcast(P, dst, vec_dram, n, tok):
    src = bass.AP(tensor=vec_dram.tensor, offset=vec_dram.offset, ap=[[0, 128], [1, n]])
    P.dma("sp", dst, src, w=[tok])


def load_cols(P, C, dst, vec_dram, ntile, tmp, ttmp, tdst, bank):
    P.dma("sp", tmp[:ntile, :], vec_dram.rearrange("(t p) -> t p", p=128), w=[ttmp])
    P.tr(C.bank[bank][:, 0:ntile], tmp[:ntile, :], C.ident[:ntile, :ntile], r=[ttmp, C.tident], w=[C.tb[bank]])
    P.cp("dve", dst, C.bank[bank][:, 0:ntile], r=[C.tb[bank]], w=[tdst])


def stage_ffn(nc, P, C, yT, xext, w_out, g_ffn, w_up, conv_w, conv_b, w_down, xmid, xout,
              tin, tout, hx, hx_out=True, ntb=2):
    KT = D // 128
    NJ = FF // 128
    TB = NT // ntb
    W3 = (TB + 2) // 3
    assert W3 * 3 == TB + 2
    arenaA = P.sb([128, NJ * TB], BF16, name="arenaA")
    tA = Tok("arenaA")
    mT = arenaA[:, :].rearrange("p (j t) -> p j t", t=TB)
    y_sb = arenaA[:, 0:KT * NT].rearrange("p (k t) -> p k t", t=NT)
    tm = ty = tA
    wu = [P.sb([128, KT, 256], BF16, name=f"wu{i}") for i in range(2)]
    twu = [Tok(f"wu{i}") for i in range(2)]
    xr2 = [P.sb([128, 256], F32, name=f"xr2{i}") for i in range(3)]
    txr2 = [Tok() for i in range(3)]
    xo2 = [P.sb([128, 256], F32, name=f"xo2{i}") for i in range(3)]
    txo2 = [Tok() for i in range(3)]
    txmid = Tok("xmid", persist=True)
    yv = yT.rearrange("(k p) t -> p k t", p=128)
    for k in range(KT):
        P.dma("sp", y_sb[:, k, :], yv[:, k, :], r=tin, w=[ty])
    wov = w_out.rearrange("(k p) n -> p k n", p=128)
    tiles = [(1 + i * 128, 128) for i in range(NT // 128)]
    cnt = 0

    def load_wo(dc):
        b = dc % 2
        P.dma("pool", wu[b][:, :, :], wov[:, :, dc * 256:(dc + 1) * 256], w=[twu[b]])
    load_wo(0)
    for dc in range(8):
        if dc + 1 < 8:
            load_wo(dc + 1)
        b = dc % 2
        for (r0, rows) in tiles:
            bk = cnt % 4
            i3 = cnt % 3
            cnt += 1
            P.dma("sp", xr2[i3][:rows, :], xext[r0:r0 + rows, dc * 256:(dc + 1) * 256], r=tin, w=[txr2[i3]])
            for k in range(KT):
                P.mm(C.bank[bk][:rows, 0:256], y_sb[:, k, r0 - 1:r0 - 1 + rows], wu[b][:, k, :], k == 0, k == KT - 1,
                     r=[ty, twu[b]], w=[C.tb[bk]])
            P.tt("dve", xo2[i3][:rows, :], C.bank[bk][:rows, 0:256], xr2[i3][:rows, :], ALU.add,
                 r=[C.tb[bk], txr2[i3]], w=[txo2[i3]])
            P.dma("sp", xmid[r0:r0 + rows, dc * 256:(dc + 1) * 256], xo2[i3][:rows, :], r=[txo2[i3]], w=[txmid])
    hx(xmid, txmid)
    NF = 2 * NJ
    cw = P.sb([128, 4, NF], F32, name="cw")
    tcw = Tok("cw")
    tmpv = P.sb([NF, 128], F32, name="tmpv")
    ttmpv = Tok("tmpv")
    for i in range(3):
        load_cols(P, C, cw[:, i, :], conv_w[i, :], NF, tmpv, ttmpv, tcw, 7)
    load_cols(P, C, cw[:, 3, :], conv_b, NF, tmpv, ttmpv, tcw, 7)
    gcol = P.sb([128, KT], F32, name="gcol")
    tgbc = Tok("gcol")
    load_cols(P, C, gcol[:, :], g_ffn, KT, tmpv, ttmpv, tgbc, 7)
    h_sb = P.sb([128, KT, TB + 2], BF16, name="h_sb")
    th = Tok("h_sb")
    nbufs = []
    for i_ in range(2):
        nxt = P.sb([128, D], F32, name=f"nxt{i_}")
        nxs = P.sb([128, D], F32, name=f"nxs{i_}")
        nss = P.sb([128, 1], F32, name=f"nss{i_}")
        nrs = P.sb([128, 1], F32, name=f"nrs{i_}")
        tnx, tnxs, tnst = Tok(), Tok(), Tok()
        nbufs.append((nxt, tnx, nxs, tnxs, nss, nrs, tnst, nxs, tnxs))
    nrot = 0
    a_sb = [P.sb([128, TB + 2], F32, name=f"a{i}") for i in range(2)]
    ta = [Tok(f"a{i}") for i in range(2)]
    c1 = [P.sb([128, TB], F32, name=f"c1_{i}") for i in range(2)]
    tc1 = [Tok() for i in range(2)]
    sg = P.sb([128, TB], F32, name="sg")
    tsg = Tok("sg")
    wd = [P.sb([128, 4, 256], BF16, name=f"wd{i}") for i in range(3)]
    twd = [Tok(f"wd{i}") for i in range(3)]
    wuv = w_up.rearrange("(k p) n -> p k n", p=128)
    wdv = w_down.rearrange("(j p) n -> p j n", p=128)

    def load_wu(j):
        b = j % 2
        P.dma("pool", wu[b][:, :, 0:128], wuv[:, :, j * 128:(j + 1) * 128], w=[twu[b]])
        P.dma("pool", wu[b][:, :, 128:256], wuv[:, :, FF + j * 128:FF + (j + 1) * 128], w=[twu[b]])

    for blk in range(ntb):
        w0 = blk * TB
        segs = [(w0, 1, 0)] + [(w0 + 1 + i * 128, 128, 1 + i * 128) for i in range(TB // 128)] + [(w0 + TB + 1, 1, TB + 1)]
        for (r0, rows, col) in segs:
            norm_transpose_tile(P, C, xmid[r0:r0 + rows, :], [txmid], rows, None, tgbc, h_sb, col, th, nbufs[nrot % 2], [6, 7],
                                gcol=gcol)
            nrot += 1
        load_wu(0)
        step = 0
        for j in range(NJ):
            if j + 1 < NJ:
                load_wu(j + 1)
            b = j % 2
            for half in range(2):
                ft = j if half == 0 else NJ + j
                s = step % 2
                step += 1
                bks = [3 * s, 3 * s + 1, 3 * s + 2]
                for c3 in range(3):
                    for k in range(KT):
                        P.mm(C.bank[bks[c3]][:, 0:W3], wu[b][:, k, half * 128:(half + 1) * 128],
                             h_sb[:, k, c3 * W3:(c3 + 1) * W3], k == 0, k == KT - 1,
                             r=[twu[b], th], w=[C.tb[bks[c3]]])
                for c3 in range(3):
                    P.cp("act", a_sb[s][:, c3 * W3:(c3 + 1) * W3], C.bank[bks[c3]][:, 0:W3],
                         r=[C.tb[bks[c3]]], w=[ta[s]])
                a = a_sb[s]
                P.ts("dve", c1[s][:, :], a[:, 1:TB + 1], cw[:, 1, ft:ft + 1], cw[:, 3, ft:ft + 1], ALU.mult, ALU.add,
                     r=[ta[s], tcw], w=[tc1[s]])
                P.stt("dve", c1[s][:, :], a[:, 0:TB], cw[:, 0, ft:ft + 1], c1[s][:, :], ALU.mult, ALU.add,
                      r=[ta[s], tcw], w=[tc1[s]])
                P.stt("dve", c1[s][:, :], a[:, 2:TB + 2], cw[:, 2, ft:ft + 1], c1[s][:, :], ALU.mult, ALU.add,
                      r=[ta[s], tcw], w=[tc1[s]])
                if half == 0:
                    P.act(sg[:, :], c1[s][:, :], AF.Silu, r=[tc1[s]], w=[tsg])
                else:
                    P.tt("dve", mT[:, j, :], c1[s][:, :], sg[:, :], ALU.mult, r=[tc1[s], tsg], w=[tm])
        NU = NJ // 4
        ucnt = 0
        units = [(dc, u) for dc in range(8) for u in range(NU)]

        def load_wd(idx):
            dc, u = units[idx]
            b3 = idx % 3
            P.dma("pool", wd[b3][:, :, :], wdv[:, u * 4:(u + 1) * 4, dc * 256:(dc + 1) * 256], w=[twd[b3]])
        load_wd(0)
        load_wd(1)
        for idx, (dc, u) in enumerate(units):
            if idx + 2 < len(units):
                load_wd(idx + 2)
            b3 = idx % 3
            for tt_ in range(TB // 128):
                bk = tt_
                co = 0
                for jj in range(4):
                    j = u * 4 + jj
                    P.mm(C.bank[bk][:, co:co + 256], mT[:, j, tt_ * 128:(tt_ + 1) * 128], wd[b3][:, jj, :],
                         j == 0, j == NJ - 1, r=[tm, twd[b3]], w=[C.tb[bk]])
            if u == NU - 1:
                for tt_ in range(TB // 128):
                    bk = tt_
                    co = 0
                    i3 = ucnt % 3
                    ucnt += 1
                    r0 = w0 + 1 + tt_ * 128
                    P.dma("sp", xr2[i3][:, :], xmid[r0:r0 + 128, dc * 256:(dc + 1) * 256], r=[txmid], w=[txr2[i3]])
                    P.tt("dve", xo2[i3][:, :], C.bank[bk][:, co:co + 256], xr2[i3][:, :], ALU.add,
                         r=[C.tb[bk], txr2[i3]], w=[txo2[i3]])
                    P.dma("sp", xout[r0:r0 + 128, dc * 256:(dc + 1) * 256], xo2[i3][:, :],
                          r=[txo2[i3]], w=[tout])
    if hx_out:
        hx(xout, tout)


class WStream:
    def __init__(self, P, KT, units, nbuf=3, name="ws", width=256):
        self.P = P
        self.buf = [P.sb([128, KT, width], BF16, name=f"{name}{i}") for i in range(nbuf)]
        self.tok = [Tok(f"{name}{i}") for i in range(nbuf)]
        self.units = units
        self.nbuf = nbuf
        self.issued = 0

    def get(self, i):
        while self.issued < len(self.units) and self.issued <= i + self.nbuf - 1:
            j = self.issued
            wview, c0, ncols = self.units[j]
            b = j % self.nbuf
            self.P.dma("pool", self.buf[b][:, :, 0:ncols], wview[:, :, c0:c0 + ncols], w=[self.tok[b]])
            self.issued += 1
        return self.buf[i % self.nbuf], self.tok[i % self.nbuf]


def norm_all(P, C, x, tin, g_vec, h_sb, th, ntok, banks=(6, 7)):
    gbc = P.sb([128, D], F32, name="gbc")
    tgbc = Tok("gbc")
    load_bcast(P, gbc[:, :], g_vec, D, tgbc)
    bufs = []
    for i in range(2):
        xt = P.sb([128, D], F32, name=f"nxt{i}")
        xs = P.sb([128, D], F32, name=f"nxs{i}")
        ss = P.sb([128, 1], F32, name=f"nss{i}")
        rs = P.sb([128, 1], F32, name=f"nrs{i}")
        t1, t2, t3 = Tok(), Tok(), Tok()
        bufs.append((xt, t1, xs, t2, ss, rs, t3, xs, t2))
    for i in range(ntok // 128):
        norm_transpose_tile(P, C, x[i * 128:(i + 1) * 128, :], tin, 128, gbc, tgbc, h_sb, i * 128, th,
                            bufs[i % 2], list(banks))


def stage_cd_in(nc, P, C, x, g_mix, w_in, cosT, sinT, rmat, qcT, kcT, vc, qdT, kdT, vd, tin, tout):
    KT = D // 128
    h_sb = P.sb([128, KT, NT], BF16, name="h_sb")
    th = Tok("h_sb")
    norm_all(P, C, x, tin, g_mix, h_sb, th, NT)
    cs = P.sb([128, NT], F32, name="cos_sb")
    sn = P.sb([128, NT], F32, name="sin_sb")
    rm = P.sb([128, 128], BF16, name="rm_sb")
    tcs = Tok("cs")
    P.dma("sp", cs[:, :], cosT[:, :], w=[tcs])
    P.dma("sp", sn[:, :], sinT[:, :], w=[tcs])
    P.dma("sp", rm[:, :], rmat[:, :], w=[tcs])
    wv = w_in.rearrange("(k p) n -> p k n", p=128)
    fm = [(0, qcT, False), (1024, kcT, False), (3072, qdT, True), (4096, kdT, True)]
    units = [(wv, col0 + u * 256, 256) for (col0, _, _) in fm for u in range(4)]
    units += [(wv, col0 + u * 256, 256) for col0 in (2048, 5120) for u in range(4)]
    ws = WStream(P, KT, units, 3)
    ui = 0
    ob = [P.sb([128, 512], BF16, name=f"ob{i}") for i in range(3)]
    tob = [Tok() for _ in range(3)]
    xb = [P.sb([128, 512], BF16, name=f"xb{i}") for i in range(2)]
    txb = [Tok() for _ in range(2)]
    t1 = [P.sb([128, 512], F32, name=f"rt1{i}") for i in range(2)]
    tt1 = [Tok() for _ in range(2)]
    t2 = [P.sb([128, 512], F32, name=f"rt2{i}") for i in range(2)]
    tt2 = [Tok() for _ in range(2)]
    nb = 0
    no = 0
    nx = 0
    for (col0, dst, rot) in fm:
        for u in range(4):
            wb, tw = ws.get(ui)
            ui += 1
            for m in range(2):
                r0 = u * 256 + m * 128
                for tb in range(NT // 512):
                    bk = nb % 4
                    nb += 1
                    for k in range(KT):
                        P.mm(C.bank[bk][:, :], wb[:, k, m * 128:(m + 1) * 128], h_sb[:, k, tb * 512:(tb + 1) * 512],
                             k == 0, k == KT - 1, r=[tw, th], w=[C.tb[bk]])
                    o = no % 3
                    no += 1
                    if not rot:
                        P.cp(C.eng2(), ob[o][:, :], C.bank[bk][:, :], r=[C.tb[bk]], w=[tob[o]])
                    else:
                        xi = nx % 2
                        nx += 1
                        bk2 = 4 + xi
                        P.cp("act", xb[xi][:, :], C.bank[bk][:, :], r=[C.tb[bk]], w=[txb[xi]])
                        P.mm(C.bank[bk2][:, :], rm[:, :], xb[xi][:, :], True, True, r=[tcs, txb[xi]], w=[C.tb[bk2]])
                        P.tt("dve", t1[xi][:, :], xb[xi][:, :], cs[:, tb * 512:(tb + 1) * 512], ALU.mult,
                             r=[txb[xi], tcs], w=[tt1[xi]])
                        P.tt("dve", t2[xi][:, :], C.bank[bk2][:, :], sn[:, tb * 512:(tb + 1) * 512], ALU.mult,
                             r=[C.tb[bk2], tcs], w=[tt2[xi]])
                        P.tt("pool", ob[o][:, :], t1[xi][:, :], t2[xi][:, :], ALU.add, r=[tt1[xi], tt2[xi]], w=[tob[o]])
                    P.dma("sp", dst[r0:r0 + 128, tb * 512:(tb + 1) * 512], ob[o][:, :], r=[tob[o]], w=[tout])
    obt = [P.sb([128, 256], BF16, name=f"obt{i}") for i in range(3)]
    tobt = [Tok() for _ in range(3)]
    for (col0, dst) in [(2048, vc), (5120, vd)]:
        for u in range(4):
            wb, tw = ws.get(ui)
            ui += 1
            for tt_ in range(NT // 128):
                bk = nb % 4
                nb += 1
                for k in range(KT):
                    P.mm(C.bank[bk][:, 0:256], h_sb[:, k, tt_ * 128:(tt_ + 1) * 128], wb[:, k, :],
                         k == 0, k == KT - 1, r=[tw, th], w=[C.tb[bk]])
                o = no % 3
                no += 1
                P.cp(C.eng2(), obt[o][:, :], C.bank[bk][:, 0:256], r=[C.tb[bk]], w=[tobt[o]])
                P.dma("sp", dst[tt_ * 128:(tt_ + 1) * 128, u * 256:(u + 1) * 256], obt[o][:, :], r=[tobt[o]], w=[tout])


NA_KTS = {0: [0, 1, 2, 3, 4, 5], 1: [1, 2, 3, 4, 5], 14: [14, 15, 16, 17, 18], 15: [14, 15, 16, 17, 18, 19]}
NA_CLS = {0: 0, 1: 1, 14: 3, 15: 4}
NKEXT = 2560


def na_kts(i):
    return NA_KTS.get(i, [i, i + 1, i + 2, i + 3, i + 4])


def stage_attn(nc, P, C, qcT, kcx, vcx, btab, qdT, kdT, vdF, dlam, lamc, subln, yT, tin, tout):
    ones_t = Tok("aug")
    lp = P.sb([128, 256], F32, name="lp")
    tlp = Tok("lp")
    load_bcast(P, lp[:, :], dlam.rearrange("a b -> (a b)"), 256, tlp)
    lc = P.sb([128, 2], F32, name="lc")
    load_bcast(P, lc[:, :], lamc, 2, tlp)
    ltmp = P.sb([128, 128], F32, name="ltmp")
    lsum = P.sb([128, 4], F32, name="lsum")
    P.tt("dve", ltmp[:, 0:64], lp[:, 0:64], lp[:, 64:128], ALU.mult, r=[tlp], w=[tlp])
    P.tt("dve", ltmp[:, 64:128], lp[:, 128:192], lp[:, 192:256], ALU.mult, r=[tlp], w=[tlp])
    P.add("dve", lambda e: e.reduce_sum(lsum[:, 0:1], ltmp[:, 0:64], AX.X), [tlp], [tlp])
    P.add("dve", lambda e: e.reduce_sum(lsum[:, 1:2], ltmp[:, 64:128], AX.X), [tlp], [tlp])
    P.act(lsum[:, 0:2], lsum[:, 0:2], AF.Exp, r=[tlp], w=[tlp])
    P.tt("dve", lsum[:, 2:3], lsum[:, 0:1], lsum[:, 1:2], ALU.subtract, r=[tlp], w=[tlp])
    P.tt("dve", lsum[:, 2:3], lsum[:, 2:3], lc[:, 0:1], ALU.add, r=[tlp], w=[tlp])
    P.ts("dve", lsum[:, 3:4], lsum[:, 2:3], -1.0, None, ALU.mult, r=[tlp], w=[tlp])
    neglam = lsum[:, 3:4]
    slb = P.sb([128, 128], F32, name="slb")
    load_bcast(P, slb[:, :], subln, 128, tlp)
    P.ts("dve", slb[:, :], slb[:, :], lc[:, 1:2], None, ALU.mult, r=[tlp], w=[tlp])

    ysb = [P.sb([128, NT], BF16, name=f"ysb{i}") for i in range(2)]
    tys = [Tok() for _ in range(2)]
    ot = [P.sb([128, 128], F32, name=f"ot{i}") for i in range(2)]
    tot = [Tok() for _ in range(2)]
    st = [P.sb([128, 4], F32, name=f"st{i}") for i in range(2)]
    tst = [Tok() for _ in range(2)]
    nsm = 0
    ntr = 0
    nys = 0
    trbanks = [6, 7]

    def finish_tile(o_ap, h_ysb, col, ttok_src):
        nonlocal ntr
        bk = trbanks[ntr % len(trbanks)]
        ntr += 1
        P.tr(C.bank[bk][:, 0:128], o_ap, C.ident[:, :], r=[ttok_src, C.tident], w=[C.tb[bk]])
        P.cp(C.eng2(), ysb[h_ysb][:, col:col + 128], C.bank[bk][:, 0:128], r=[C.tb[bk]], w=[tys[h_ysb]])

    SC_C = 128.0 ** -0.5
    kc = [P.sb([128, NKEXT], BF16, name=f"kc{i}") for i in range(2)]
    qc = [P.sb([128, NT], BF16, name=f"qc{i}") for i in range(2)]
    vca = [P.sb([128, NKEXT // 128, 129], BF16, name=f"vca{i}") for i in range(2)]
    thd = [Tok() for _ in range(2)]
    for i in range(2):
        P.add("pool", lambda e, i=i: e.memset(vca[i][:, :, 128:129], 1.0), (), [thd[i]])
    bint = [P.sb([128, 6, 128], F32, name=f"bint{i}") for i in range(2)]
    bedge = [P.sb([128, 6, 128], F32, name=f"bedge{i}") for i in range(2)]
    tbe = [Tok() for _ in range(2)]
    ssb = [P.sb([128, 6, 128], F32, name=f"ssb{i}") for i in range(2)]
    tss = [Tok() for _ in range(2)]
    pT = [P.sb([128, 6, 128], BF16, name=f"pT{i}") for i in range(2)]
    tpT = [Tok() for _ in range(2)]
    nedge = 0
    nit = 0
    for h in range(8):
        hb = h % 2
        P.dma("sp", kc[hb][:, :], kcx[h * 128:(h + 1) * 128, :], r=tin, w=[thd[hb]])
        P.dma("sp", qc[hb][:, :], qcT[h * 128:(h + 1) * 128, :], r=tin, w=[thd[hb]])
        P.dma("sp", vca[hb][:, :, 0:128], vcx[:, h * 128:(h + 1) * 128].rearrange("(t p) d -> p t d", p=128),
              r=tin, w=[thd[hb]])
        P.dma("sp", bint[hb][:, :, :], btab[h, 2, :, :, :], r=tin, w=[thd[hb]])
        yb_ = nys % 2
        nys += 1
        def na_stage1(i):
            nonlocal nedge, nit
            kts = na_kts(i)
            nk = len(kts)
            cls = NA_CLS.get(i, 2)
            if cls == 2:
                btile, tbt = bint[hb], thd[hb]
            else:
                eb = nedge % 2
                nedge += 1
                P.dma("sp", bedge[eb][:, :, :], btab[h, cls, :, :, :], r=tin, w=[tbe[eb]])
                btile, tbt = bedge[eb], tbe[eb]
            it = nit % 2
            nit += 1
            sb0 = 2 * it
            for jj, kt in enumerate(kts):
                bk = sb0 + jj // 4
                co = (jj % 4) * 128
                P.mm(C.bank[bk][:, co:co + 128], kc[hb][:, kt * 128:(kt + 1) * 128], qc[hb][:, i * 128:(i + 1) * 128],
                     True, True, r=[thd[hb]], w=[C.tb[bk]])
            n0 = min(nk, 4)
            P.stt("dve", ssb[it][:, 0:n0, :], C.bank[sb0][:, 0:n0 * 128].rearrange("p (j q) -> p j q", q=128), SC_C,
                  btile[:, 0:n0, :], ALU.mult, ALU.add, r=[C.tb[sb0], tbt], w=[tss[it]])
            if nk > 4:
                P.stt("dve", ssb[it][:, 4:nk, :], C.bank[sb0 + 1][:, 0:(nk - 4) * 128].rearrange("p (j q) -> p j q", q=128),
                      SC_C, btile[:, 4:nk, :], ALU.mult, ALU.add, r=[C.tb[sb0 + 1], tbt], w=[tss[it]])
            P.act(pT[it][:, 0:nk, :], ssb[it][:, 0:nk, :], AF.Exp, r=[tss[it]], w=[tpT[it]])
            return (i, it, kts)

        def na_stage2(st_):
            nonlocal nsm
            i, it, kts = st_
            nk = len(kts)
            bo = 4 + it
            for jj, kt in enumerate(kts):
                P.mm(C.bank[bo][:, 0:129], pT[it][:, jj, :], vca[hb][:, kt, :], jj == 0, jj == nk - 1,
                     r=[tpT[it], thd[hb]], w=[C.tb[bo]])
            sm = nsm % 2
            nsm += 1
            P.add("dve", lambda e, sm=sm, bo=bo: e.reciprocal(st[sm][:, 0:1], C.bank[bo][:, 128:129]),
                  [C.tb[bo]], [tst[sm]])
            P.ts("dve", ot[sm][:, :], C.bank[bo][:, 0:128], st[sm][:, 0:1], None, ALU.mult,
                 r=[C.tb[bo], tst[sm]], w=[tot[sm]])
            finish_tile(ot[sm][:, :], yb_, i * 128, tot[sm])
        prev_st = None
        for i in range(16):
            cur_st = na_stage1(i)
            if prev_st is not None:
                na_stage2(prev_st)
            prev_st = cur_st
        na_stage2(prev_st)
        P.dma("sp", yT[h * 128:(h + 1) * 128, :], ysb[yb_][:, :], r=[tys[yb_]], w=[tout])

    SC_D = 64.0 ** -0.5
    kd = [P.sb([128, L], BF16, name=f"kd{i}") for i in range(2)]
    qd = [[P.sb([128, NT], BF16, name=f"qd{m}_{i}") for i in range(2)] for m in range(2)]
    vdh = [P.sb([128, L // 128, 128], BF16, name=f"vdh{i}") for i in range(2)]
    thd2 = [Tok() for _ in range(2)]
    for i in range(2):
        P.add("pool", lambda e, i=i: e.memset(qd[0][i][64:128, :], 0.0), (), [thd2[i]])
        P.add("pool", lambda e, i=i: e.memset(qd[1][i][0:64, :], 0.0), (), [thd2[i]])
    onesb = P.sb([128, 128], BF16, name="onesb")
    onesf = P.sb([128, 128], F32, name="onesf")
    tones = Tok()
    P.add("pool", lambda e: e.memset(onesb[:, :], 1.0), (), [tones])
    P.add("pool", lambda e: e.memset(onesf[:, :], 1.0), (), [tones])
    slc = P.sb([128, 1], F32, name="slc")
    P.dma("sp", slc[:, :], subln.rearrange("(p o) -> p o", o=1), r=tin, w=[tlp])
    P.ts("dve", slc[:, :], slc[:, :], lc[:, 1:2], None, ALU.mult, r=[tlp], w=[tlp])
    NPD = 4
    pD = [P.sb([128, 512], BF16, name=f"pD{i}") for i in range(NPD)]
    tpD = [Tok() for _ in range(NPD)]
    wa = P.sb([128, 512], F32, name="wa")
    wb_ = P.sb([128, 512], F32, name="wb")
    wc = P.sb([128, 512], F32, name="wc")
    twa, twb, twc = Tok(), Tok(), Tok()
    npd = 0
    nsb = 0
    NKT = L // 128
    for h in range(8):
        hb = h % 2
        P.dma("sp", kd[hb][:, :], kdT[h * 128:(h + 1) * 128, :], r=tin, w=[thd2[hb]])
        P.dma("sp", qd[0][hb][0:64, :], qdT[h * 128:h * 128 + 64, :], r=tin, w=[thd2[hb]])
        P.dma("sp", qd[1][hb][64:128, :], qdT[h * 128 + 64:(h + 1) * 128, :], r=tin, w=[thd2[hb]])
        P.dma("sp", vdh[hb][:, :, :], vdF[:, h * 128:(h + 1) * 128].rearrange("(t p) d -> p t d", p=128),
              r=tin, w=[thd2[hb]])
        yb_ = nys % 2
        nys += 1

        def d_stage1(item):
            nonlocal nsb, npd
            qb, m, kt = item
            bs = 4 + nsb % 3
            nsb += 1
            P.mm(C.bank[bs][:, :], kd[hb][:, kt * 128:(kt + 1) * 128],
                 qd[m][hb][:, qb * 512:(qb + 1) * 512], True, True,
                 r=[thd2[hb]], w=[C.tb[bs]])
            pi = npd % NPD
            npd += 1
            P.act(pD[pi][:, :], C.bank[bs][:, :], AF.Exp, r=[C.tb[bs]], w=[tpD[pi]], scale=SC_D)
            return (item, pi)

        def d_stage2(st_):
            (qb, m, kt), pi = st_
            bo, bz = 2 * m, 2 * m + 1
            P.mm(C.bank[bo][:, :], vdh[hb][:, kt, :], pD[pi][:, :], kt == 0, kt == NKT - 1,
                 r=[tpD[pi], thd2[hb]], w=[C.tb[bo]])
            P.mm(C.bank[bz][:, :], onesb[:, :], pD[pi][:, :], kt == 0, kt == NKT - 1,
                 r=[tpD[pi], tones], w=[C.tb[bz]])
            if kt != NKT - 1 or m == 0:
                return
            P.add("dve", lambda e: e.reciprocal(wa[:, :], C.bank[1][:, :]), [C.tb[1]], [twa])
            P.tt("dve", wa[:, :], C.bank[0][:, :], wa[:, :], ALU.mult, r=[C.tb[0], twa], w=[twa])
            P.add("dve", lambda e: e.reciprocal(wb_[:, :], C.bank[3][:, :]), [C.tb[3]], [twb])
            P.tt("dve", wb_[:, :], C.bank[2][:, :], wb_[:, :], ALU.mult, r=[C.tb[2], twb], w=[twb])
            P.stt("dve", wa[:, :], wb_[:, :], neglam, wa[:, :], ALU.mult, ALU.add, r=[twb, twa, tlp], w=[twa])
            P.tt("pool", wc[:, :], wa[:, :], wa[:, :], ALU.mult, r=[twa], w=[twc])
            P.mm(C.bank[7][:, :], onesf[:, :], wc[:, :], True, True, r=[twc, tones], w=[C.tb[7]])
            P.ts("dve", wb_[:, :], C.bank[7][:, :], 1.0 / 128, EPS, ALU.mult, ALU.add, r=[C.tb[7]], w=[twb])
            P.act(wb_[:, :], wb_[:, :], AF.Sqrt, r=[twb], w=[twb])
            P.add("dve", lambda e: e.reciprocal(wb_[:, :], wb_[:, :]), [twb], [twb])
            P.tt("dve", wa[:, :], wa[:, :], wb_[:, :], ALU.mult, r=[twa, twb], w=[twa])
            P.ts("dve", ysb[yb_][:, qb * 512:(qb + 1) * 512], wa[:, :], slc[:, 0:1], None, ALU.mult,
                 r=[twa, tlp], w=[tys[yb_]])
        items = [(qb, m, kt) for qb in range(NT // 512) for m in range(2) for kt in range(NKT)]
        pend = []
        for item in items:
            pend.append(d_stage1(item))
            if len(pend) > 2:
                d_stage2(pend.pop(0))
        while pend:
            d_stage2(pend.pop(0))
        P.dma("sp", yT[1024 + h * 128:1024 + (h + 1) * 128, :], ysb[yb_][:, :], r=[tys[yb_]], w=[tout])


def rope_tables(hf):
    d = np.arange(128) % 64 % 32
    inv = (1.0 / (10000.0 ** (d.astype(np.float32) * 2.0 / 64.0))).astype(np.float32)
    pos = (hf * NT + np.arange(NT)).astype(np.float32)
    ang = pos[None, :] * inv[:, None]
    return np.cos(ang).astype(np.float32), np.sin(ang).astype(np.float32)


def rot_matrix():
    rm = np.zeros((128, 128), np.float32)
    for pp in range(128):
        if pp % 64 < 32:
            rm[pp + 32, pp] = -1.0
        else:
            rm[pp - 32, pp] = 1.0
    return rm.astype(NPBF)


def na_bias_tables(rpb, hf):
    out = np.full((8, 5, 128, 6, 128), -30000.0, np.float32)
    p = np.arange(128)
    for ci, i in enumerate([0, 1, 2, 14, 15]):
        kts = na_kts(i)
        q = np.arange(128)
        r = 32 * hf + 2 * i + q // 64
        c = q % 64
        rs = np.clip(r - 4, 0, 56)
        cs = np.clip(c - 8, 0, 48)
        for jj, kt in enumerate(kts):
            gr = 32 * hf - 4 + 2 * kt + p // 64
            cp = p % 64
            valid = ((gr[:, None] >= 0) & (gr[:, None] < 64) & (gr[:, None] >= rs[None, :]) & (gr[:, None] < rs[None, :] + 8)
                     & (cp[:, None] >= cs[None, :]) & (cp[:, None] < cs[None, :] + 16))
            rr = np.clip(gr[:, None] - r[None, :] + 7, 0, 14)
            rc = np.clip(cp[:, None] - c[None, :] + 15, 0, 30)
            vals = rpb[:, rr, rc]
            out[:, ci, :, jj, :] = np.where(valid[None], vals, -30000.0)
    return out


def ext_keys(full_tm, hf):
    out = np.zeros((NKEXT, full_tm.shape[1]), full_tm.dtype)
    lo = hf * NT - 256
    a, b = max(lo, 0), min(lo + NKEXT, L)
    out[a - lo:b - lo] = full_tm[a:b]
    return out


class Arena:
    def __init__(self, P, nbytes, name="arena"):
        self.t = P.sb([128, nbytes // 4], F32, name=name)
        self.nbytes = nbytes

    def view(self, off, shape, dt):
        esz = 2 if dt == BF16 else 4
        n = 1
        for s in shape:
            n *= s
        assert off % 4 == 0 and (n * esz) % 4 == 0 and off + n * esz <= self.nbytes, (off, shape, self.nbytes)
        ap = self.t[:, off // 4: off // 4 + (n * esz) // 4]
        if dt != F32:
            ap = ap.bitcast(dt)
        if len(shape) == 2:
            ap = ap.rearrange("p (a b) -> p a b", b=shape[1])
        elif len(shape) == 3:
            ap = ap.rearrange("p (a b c) -> p a b c", b=shape[1], c=shape[2])
        return ap


def barrier(P, scratch, toks):
    P.add("dve", lambda e: e.memset(scratch[0:1, 0:1], 0.0), toks, toks)


def stage_ab_in(nc, P, C, xext, g_mix, w_in, vnorm, a_ws, a_bs, bcw, bcb, yaT, v_tm, x1_tm, x2T, tin, tout):
    KT = D // 128
    NW = NT + 2
    W5 = NW // 5
    assert W5 * 5 == NW
    scr = P.sb([128, 1], F32, name="scr")
    h_sb = P.sb([128, KT, NW], BF16, name="h_sb")
    th = Tok("h_sb")
    A = Arena(P, 64 * 1024 + 1024, "arenaE")
    gbc = A.view(0, [D], F32)
    xt = A.view(8192, [D], F32)
    xs = A.view(16384, [D], F32)
    tg, tx, txs, tstat = Tok(), Tok(), Tok(), Tok()
    load_bcast(P, gbc, g_mix, D, tg)
    nss = P.sb([128, 1], F32, name="nss")
    nrs = P.sb([128, 1], F32, name="nrs")
    nbuf = (xt, tx, xs, txs, nss, nrs, tstat, xs, txs)
    segs = [(0, 1, 0)] + [(1 + i * 128, 128, 1 + i * 128) for i in range(NT // 128)] + [(NT + 1, 1, NT + 1)]
    for (r0, rows, col) in segs:
        norm_transpose_tile(P, C, xext[r0:r0 + rows, :], tin, rows, gbc, tg, h_sb, col, th, nbuf, [6, 7])
    cw = P.sb([128, 4, 24], F32, name="cwE")
    tcw = Tok("cwE")
    tmpv = P.sb([24, 128], F32, name="tmpvE")
    ttmpv = Tok()
    for i in range(3):
        load_cols(P, C, cw[:, i, :], bcw[i, :], 24, tmpv, ttmpv, tcw, 7)
    load_cols(P, C, cw[:, 3, :], bcb, 24, tmpv, ttmpv, tcw, 7)
    wv = w_in.rearrange("(k p) n -> p k n", p=128)
    units = [(wv, 2048 + u * 256, 256) for u in range(12)] + [(wv, u * 256, 256) for u in range(4)]
    ws = WStream(P, KT, units, 3)
    a_sb = [A.view(i * 8200, [NW], F32) for i in range(2)]
    cb_ = [A.view(16400 + i * 8192, [NT], F32) for i in range(2)]
    ta = [Tok() for _ in range(2)]
    tc = [Tok() for _ in range(2)]
    barrier(P, scr, [tg, tx, txs] + ta + tc)
    otr = [P.sb([128, 512], F32, name=f"otr{i}") for i in range(2)]
    totr = [Tok() for _ in range(2)]
    otb = [P.sb([128, 512], BF16, name=f"otb{i}") for i in range(2)]
    totb = [Tok() for _ in range(2)]
    ntr = 0
    for u in range(12):
        wb, tw = ws.get(u)
        for m in range(2):
            ch = u * 2 + m
            s = ch % 2
            for c5 in range(5):
                for k in range(KT):
                    P.mm(C.bank[c5][:, 0:W5], wb[:, k, m * 128:(m + 1) * 128], h_sb[:, k, c5 * W5:(c5 + 1) * W5],
                         k == 0, k == KT - 1, r=[tw, th], w=[C.tb[c5]])
            for c5 in range(5):
                P.cp("act", a_sb[s][:, c5 * W5:(c5 + 1) * W5], C.bank[c5][:, 0:W5], r=[C.tb[c5]], w=[ta[s]])
            a = a_sb[s]
            c = cb_[s]
            P.ts("dve", c[:, :], a[:, 1:NT + 1], cw[:, 1, ch:ch + 1], cw[:, 3, ch:ch + 1], ALU.mult, ALU.add,
                 r=[ta[s], tcw], w=[tc[s]])
            P.stt("dve", c[:, :], a[:, 0:NT], cw[:, 0, ch:ch + 1], c[:, :], ALU.mult, ALU.add, r=[ta[s], tcw], w=[tc[s]])
            P.stt("dve", c[:, :], a[:, 2:NT + 2], cw[:, 2, ch:ch + 1], c[:, :], ALU.mult, ALU.add, r=[ta[s], tcw], w=[tc[s]])
            if True:
                dst = v_tm if ch < 8 else (x1_tm if ch < 16 else x2T)
                c0 = (ch % 8) * 128
                for t4 in range(NT // 512):
                    bk = 5 + ntr % 3
                    o = ntr % 2
                    ntr += 1
                    for q in range(4):
                        tt_ = t4 * 4 + q
                        P.tr(C.bank[bk][:, q * 128:(q + 1) * 128], c[:, tt_ * 128:(tt_ + 1) * 128], C.ident[:, :],
                             r=[tc[s], C.tident], w=[C.tb[bk]])
                    src = C.bank[bk][:, :].rearrange("p (q c) -> p q c", c=128)
                    if ch < 8:
                        P.cp("act", otb[o][:, :].rearrange("p (q c) -> p q c", c=128), src, r=[C.tb[bk]], w=[totb[o]])
                        P.dma("sp", dst[t4 * 512:(t4 + 1) * 512, c0:c0 + 128].rearrange("(q p) c -> p q c", p=128),
                              otb[o][:, :].rearrange("p (q c) -> p q c", c=128), r=[totb[o]], w=[tout])
                    else:
                        P.cp("act", otr[o][:, :].rearrange("p (q c) -> p q c", c=128), src, r=[C.tb[bk]], w=[totr[o]])
                        P.dma("sp", dst[t4 * 512:(t4 + 1) * 512, c0:c0 + 128].rearrange("(q p) c -> p q c", p=128),
                              otr[o][:, :].rearrange("p (q c) -> p q c", c=128), r=[totr[o]], w=[tout])
    wvb = A.view(0, [KT, 1024], BF16)
    u_sb = A.view(32768, [8, NT], BF16)
    twv, tu = Tok(), Tok()
    barrier(P, scr, ta + tc + [twv, tu])
    for u in range(4):
        P.dma("pool", wvb[:, :, u * 256:(u + 1) * 256], wv[:, :, 1024 + u * 256:1024 + (u + 1) * 256], w=[twv])
    bsb = P.sb([128, 8, 128], F32, name="bsb")
    tbs = Tok()
    load_bcast(P, bsb[:, :, :].rearrange("p g q -> p (g q)"), a_bs.rearrange("g q -> (g q)"), 1024, tbs)
    vgb = P.sb([128, 1024], F32, name="vgb")
    load_bcast(P, vgb[:, :], vnorm, 1024, tbs)
    wsT = P.sb([128, 8, 128], BF16, name="wsT")
    wtmp = P.sb([128, 128], F32, name="wtmp")
    twt = Tok()
    for g in range(8):
        P.dma("sp", wtmp[:, :], a_ws[g, :, :], w=[twt])
        P.tr(C.bank[7][:, 0:128], wtmp[:, :], C.ident[:, :], r=[twt, C.tident], w=[C.tb[7]])
        P.cp("dve", wsT[:, g, :], C.bank[7][:, 0:128], r=[C.tb[7]], w=[tbs])
    nb = 0
    for u in range(4):
        wb, tw = ws.get(12 + u)
        for m in range(2):
            fc = u * 2 + m
            for tb in range(NT // 512):
                bk = nb % 4
                nb += 1
                for k in range(KT):
                    P.mm(C.bank[bk][:, :], wb[:, k, m * 128:(m + 1) * 128], h_sb[:, k, 1 + tb * 512:1 + (tb + 1) * 512],
                         k == 0, k == KT - 1, r=[tw, th], w=[C.tb[bk]])
                P.act(u_sb[:, fc, tb * 512:(tb + 1) * 512], C.bank[bk][:, :], AF.Gelu, r=[C.tb[bk]], w=[tu])
    vt = [P.sb([128, 1024], F32, name=f"vt{i}") for i in range(2)]
    tvt = [Tok() for _ in range(2)]
    vn = [P.sb([128, 1024], BF16, name=f"vn{i}") for i in range(2)]
    tvn = [Tok() for _ in range(2)]
    vj = P.sb([128, 1024], BF16, name="vj")
    tvj = Tok()
    vst = [P.sb([128, 2], F32, name=f"vst{i}") for i in range(2)]
    tvs = [Tok() for _ in range(2)]
    sg_ = [P.sb([128, 8, 128], F32, name=f"sg{i}") for i in range(2)]
    tsg = [Tok() for _ in range(2)]
    for ck in range(NT // 128):
        s = ck % 2
        for half in range(2):
            bk = (ck % 2) * 2 + half
            for k in range(KT):
                P.mm(C.bank[bk][:, :], h_sb[:, k, 1 + ck * 128:1 + (ck + 1) * 128], wvb[:, k, half * 512:(half + 1) * 512],
                     k == 0, k == KT - 1, r=[twv, th], w=[C.tb[bk]])
            P.act(vt[s][:, half * 512:(half + 1) * 512], C.bank[bk][:, :], AF.Gelu, r=[C.tb[bk]], w=[tvt[s]])
        P.act(vj[:, :], vt[s][:, :], AF.Square, r=[tvt[s]], w=[tvj, tvs[s]], accum_out=vst[s][:, 0:1])
        P.ts("dve", vst[s][:, 1:2], vst[s][:, 0:1], 1.0 / 1024, EPS, ALU.mult, ALU.add, r=[tvs[s]], w=[tvs[s]])
        P.act(vst[s][:, 1:2], vst[s][:, 1:2], AF.Sqrt, r=[tvs[s]], w=[tvs[s]])
        P.add("dve", lambda e, s=s: e.reciprocal(vst[s][:, 1:2], vst[s][:, 1:2]), [tvs[s]], [tvs[s]])
        P.stt("dve", vn[s][:, :], vt[s][:, :], vst[s][:, 1:2], vgb[:, :], ALU.mult, ALU.mult,
              r=[tvt[s], tvs[s], tbs], w=[tvn[s]])
        for g in range(8):
            bk = 4 + (ck % 2) * 2 + g // 4
            P.mm(C.bank[bk][:, (g % 4) * 128:(g % 4 + 1) * 128], vn[s][:, g * 128:(g + 1) * 128], wsT[:, g, :],
                 True, True, r=[tvn[s], tbs], w=[C.tb[bk]])
        for hh in range(2):
            bk = 4 + (ck % 2) * 2 + hh
            P.tt("dve", sg_[s][:, hh * 4:(hh + 1) * 4, :], C.bank[bk][:, :].rearrange("p (g q) -> p g q", q=128),
                 bsb[:, hh * 4:(hh + 1) * 4, :], ALU.add, r=[C.tb[bk], tbs], w=[tsg[s]])
        P.tt("pool", u_sb[:, :, ck * 128:(ck + 1) * 128], u_sb[:, :, ck * 128:(ck + 1) * 128], sg_[s][:, :, :], ALU.mult,
             r=[tsg[s]], w=[tu])
    for g in range(8):
        P.dma("sp", yaT[g * 128:(g + 1) * 128, :], u_sb[:, g, :], r=[tu], w=[tout])


NFT = 33
NFP = NFT * 128
NFFT = 2 * L


def dft_fwd_tables():
    m = (np.arange(32)[None, :, None] * 128 + np.arange(128)[:, None, None]).astype(np.int64)
    outc = np.zeros((NFT, 128, 32, 128), NPBF)
    outs = np.zeros((NFT, 128, 32, 128), NPBF)
    for ft in range(NFT):
        f = (ft * 128 + np.arange(128))[None, None, :].astype(np.int64)
        ph = ((m * f) % NFFT).astype(np.float64) * (2.0 * np.pi / NFFT)
        valid = (f <= L)
        outc[ft] = np.where(valid, np.cos(ph), 0.0).astype(NPBF)
        outs[ft] = np.where(valid, np.sin(ph), 0.0).astype(NPBF)
    return outc, outs


def dft_inv_tables(hf):
    f = (np.arange(NFT)[None, :, None] * 128 + np.arange(128)[:, None, None]).astype(np.int64)
    wf = np.where((f == 0) | (f == L), 1.0, 2.0) / NFFT
    wf = np.where(f <= L, wf, 0.0)
    outc = np.zeros((NT // 128, 128, NFT, 128), NPBF)
    outs = np.zeros((NT // 128, 128, NFT, 128), NPBF)
    for tt in range(NT // 128):
        t = (hf * NT + tt * 128 + np.arange(128))[None, None, :].astype(np.int64)
        ph = ((f * t) % NFFT).astype(np.float64) * (2.0 * np.pi / NFFT)
        outc[tt] = (wf * np.cos(ph)).astype(NPBF)
        outs[tt] = (-wf * np.sin(ph)).astype(NPBF)
    return outc, outs


def hyena_pos_features():
    t = np.linspace(0.0, 1.0, L, dtype=np.float32)[:, None]
    w = (2.0 * np.pi * np.arange(L, dtype=np.float32)[:, None] / L).astype(np.float32)
    bands = np.linspace(1e-4, 15.0, 16, dtype=np.float32)[None, :]
    z = np.concatenate([t, np.cos(w * bands), -np.sin(w * bands)], axis=-1).astype(np.float32)
    return np.ascontiguousarray(z.T)


def hyena_decay_full():
    import math
    t = np.linspace(0.0, 1.0, L, dtype=np.float32)[:, None]
    max_decay = math.log(1e-2) / 0.3
    min_decay = math.log(1e-2) / 1.5
    deltas = np.abs(np.linspace(min_decay, max_decay, 1024, dtype=np.float32))
    return np.ascontiguousarray(np.exp(-t * deltas[None, :]).astype(np.float32))


def hyena_decay(core):
    import math
    t = np.linspace(0.0, 1.0, L, dtype=np.float32)[:, None]
    max_decay = math.log(1e-2) / 0.3
    min_decay = math.log(1e-2) / 1.5
    deltas = np.abs(np.linspace(min_decay, max_decay, 1024, dtype=np.float32))[core * 128:(core + 1) * 128]
    dec = np.exp(-t * deltas[None, :]).astype(np.float32)
    return np.ascontiguousarray(np.tile(dec, (1, 4)))


def stage_filter(nc, P, C, zT, dec, fw1, fb1, fw2, fb2, fw3r, ffreq, fskipr, tabC, tabS, kf_out, tin, tout):
    scr = P.sb([128, 1], F32, name="scrF")
    A_all = P.sb([128, 32, 512], BF16, name="A_all")
    B_all = P.sb([128, 32, 512], BF16, name="B_all")
    tAB = Tok()
    acc = P.sb([128, 512], F32, name="accF")
    tacc = Tok()
    AH = Arena(P, 65536, "arenaH")
    h2 = [AH.view(32768 + i * 16384, [L], F32)[0:64] for i in range(2)]
    th2 = [Tok() for _ in range(2)]
    wsm = P.sb([64, 128], F32, name="wsm")
    w3s = [P.sb([128, 512], BF16, name=f"w3s{i}") for i in range(2)]
    tw3 = [Tok() for _ in range(2)]
    h2b = [P.sb([128, L], BF16, name=f"h2b_{i}") for i in range(2)]
    for i in range(2):
        P.add("pool", lambda e, i=i: e.memset(w3s[i][64:128, :], 0.0), (), [tw3[i]])
        P.add("pool", lambda e, i=i: e.memset(h2b[i][64:128, :], 0.0), (), [th2[i]])
    cols = P.sb([64, 4], F32, name="colsF")
    tws = Tok()
    dsb = P.sb([128, 32, 128], F32, name="dsb")
    tds = Tok()
    tmpa = [P.sb([128, 256], F32, name=f"tmpa{i}") for i in range(2)]
    ttmp = [Tok() for _ in range(2)]
    TWO_PI = 2.0 * np.pi
    qi = P.sb([64, 512], mybir.dt.int32, name="qiF")
    qf = P.sb([64, 512], F32, name="qfF")
    tqi = Tok()
    ones = P.sb([128, 128], F32, name="onesF")
    tones = Tok()
    P.add("dve", lambda e: e.memset(ones[:, :], 1.0), (), [tones])
    rn = P.sb([128, 512], F32, name="rnF")
    trn = Tok()
    skb = P.sb([128, 512], F32, name="skb")
    z_sb = AH.view(0, [L], F32)[0:33]
    h1 = AH.view(16384, [L], F32)[0:64]
    tz, th1 = Tok(), Tok()
    P.dma("sp", z_sb[:, :], zT[:, :], r=tin, w=[tz])
    for li in range(2):
        P.dma("sp", wsm[0:33, 0:64], fw1[li, :, :], r=tin, w=[tws])
        P.dma("sp", wsm[:, 64:128], fw2[li, :, :], r=tin, w=[tws])
        P.dma("sp", cols[:, 0:1], fb1[li, :].rearrange("(p o) -> p o", o=1), r=tin, w=[tws])
        P.dma("sp", cols[:, 1:2], ffreq[li, 0, :].rearrange("(p o) -> p o", o=1), r=tin, w=[tws])
        P.dma("sp", cols[:, 2:3], fb2[li, :].rearrange("(p o) -> p o", o=1), r=tin, w=[tws])
        P.dma("sp", cols[:, 3:4], ffreq[li, 1, :].rearrange("(p o) -> p o", o=1), r=tin, w=[tws])
        for (src, srows, wcol, bcol, dst, tsrc, tdst) in [(z_sb, 33, 0, 0, h1, tz, th1), (h1, 64, 64, 2, h2[li], th1, th2[li])]:
            for cb in range(L // 512):
                bk = cb % 4
                P.mm(C.bank[bk][0:64, :], wsm[0:srows, wcol:wcol + 64], src[0:srows, cb * 512:(cb + 1) * 512], True, True,
                     r=[tws, tsrc], w=[C.tb[bk]])
                dsl = dst[:, cb * 512:(cb + 1) * 512]
                P.ts("dve", dsl, C.bank[bk][0:64, :], cols[:, bcol:bcol + 1],
                     cols[:, bcol + 1:bcol + 2], ALU.add, ALU.mult, r=[C.tb[bk], tws], w=[tdst])
                P.ts("dve", qi[:, :], dsl, 1.0 / TWO_PI, None, ALU.mult, r=[tdst], w=[tqi])
                P.cp("dve", qf[:, :], qi[:, :], r=[tqi], w=[tqi])
                P.stt("dve", dsl, qf[:, :], -TWO_PI, dsl, ALU.mult, ALU.add, r=[tqi], w=[tdst])
                P.ts("dve", dsl, dsl, -3.141592, 3.141592, ALU.max, ALU.min, r=[tdst], w=[tdst])
                P.act(dsl, dsl, AF.Sin, r=[tdst], w=[tdst])
        P.cp("dve", h2b[li][0:64, :], h2[li][:, :], r=[th2[li]], w=[th2[li]])
    hbuf = AH.view(0, [32, 512], F32)
    thb = Tok()
    tC = [AH.view(i * 8192, [32, 128], BF16) for i in range(2)]
    tS = [AH.view(16384 + i * 8192, [32, 128], BF16) for i in range(2)]
    ttab = [Tok() for _ in range(2)]
    ko = [AH.view(32768 + i * 4096, [2, 512], F32) for i in range(2)]
    tko = [Tok() for _ in range(2)]
    barrier(P, scr, [tz, th1, thb] + th2 + ttab + tko)
    for cg in range(8):
        P.add("dve", lambda e: e.memset(acc[:, :], 0.0), (), [tacc])
        load_bcast(P, skb[:, :], fskipr[cg, :], 512, trn)
        P.dma("sp", dsb[:, :, :], dec[:, cg * 128:(cg + 1) * 128].rearrange("(m p) c -> p m c", p=128), r=tin, w=[tds])
        for li in range(2):
            P.dma("pool", w3s[li][0:64, :], fw3r[li, cg, :, :], r=tin, w=[tw3[li]])
            for mt in range(32):
                bk = 4 + mt % 2
                dd = dsb[:, mt, :]
                dbc = bass.AP(tensor=dd.tensor, offset=dd.offset, ap=[list(dd.ap[0]), [0, 4], list(dd.ap[-1])])
                P.mm(C.bank[bk][:, :], h2b[li][:, mt * 128:(mt + 1) * 128], w3s[li][:, :], True, True,
                     r=[th2[li], tw3[li]], w=[C.tb[bk]])
                P.tt("dve", hbuf[:, mt, :].rearrange("p (j c) -> p j c", c=128),
                     C.bank[bk][:, :].rearrange("p (j c) -> p j c", c=128), dbc, ALU.mult,
                     r=[C.tb[bk], tds], w=[thb])
            hv = hbuf.rearrange("p m (o d c) -> p m o d c", o=2, d=2)
            P.add("dve", lambda e, hv=hv: e.memset(hv[0:1, 0, :, 1, :], 0.0), [thb], [thb])
            for mt in range(32):
                ti = mt % 2
                fwd = hv[:, mt, :, 0, :]
                bwd = hv[:, mt, :, 1, :]
                Aout = A_all[:, mt, li * 256:(li + 1) * 256].rearrange("p (o c) -> p o c", o=2)
                Bout = B_all[:, mt, li * 256:(li + 1) * 256].rearrange("p (o c) -> p o c", o=2)
                P.tt("pool", Aout, fwd, bwd, ALU.add, r=[thb], w=[tAB])
                P.tt("pool", Bout, bwd, fwd, ALU.subtract, r=[thb], w=[tAB])
                t3 = tmpa[ti][:, :].rearrange("p (o c) -> p o c", o=2)
                P.act(t3, fwd, AF.Abs, r=[thb], w=[ttmp[ti]])
                P.tt("dve", acc[:, li * 256:(li + 1) * 256], acc[:, li * 256:(li + 1) * 256], tmpa[ti][:, :], ALU.add,
                     r=[ttmp[ti]], w=[tacc])
                P.act(t3, bwd, AF.Abs, r=[thb], w=[ttmp[ti]])
                P.tt("dve", acc[:, li * 256:(li + 1) * 256], acc[:, li * 256:(li + 1) * 256], tmpa[ti][:, :], ALU.add,
                     r=[ttmp[ti]], w=[tacc])
        P.mm(C.bank[6][:, :], ones[:, :], acc[:, :], True, True, r=[tones, tacc], w=[C.tb[6]])
        P.add("dve", lambda e: e.reciprocal(rn[:, :], C.bank[6][:, :]), [C.tb[6]], [trn])
        barrier(P, scr, [thb] + ttab + tko)

        def load_tab(ft):
            b = ft % 2
            P.dma("sp", tC[b][:, :, :], tabC[ft, :, :, :], r=tin, w=[ttab[b]])
            P.dma("sp", tS[b][:, :, :], tabS[ft, :, :, :], r=tin, w=[ttab[b]])
        load_tab(0)
        for ft in range(NFT):
            if ft + 1 < NFT:
                load_tab(ft + 1)
            b = ft % 2
            rows = 128 if ft < NFT - 1 else 1
            br, bi = (ft % 2) * 2, (ft % 2) * 2 + 1
            for mt in range(32):
                P.mm(C.bank[br][0:rows, :], tC[b][:, mt, 0:rows], A_all[:, mt, :], mt == 0, mt == 31, r=[ttab[b], tAB], w=[C.tb[br]])
            for mt in range(32):
                P.mm(C.bank[bi][0:rows, :], tS[b][:, mt, 0:rows], B_all[:, mt, :], mt == 0, mt == 31, r=[ttab[b], tAB], w=[C.tb[bi]])
            P.tt("dve", ko[b][0:rows, 0, :], C.bank[br][0:rows, :], rn[0:rows, :], ALU.mult, r=[C.tb[br], trn], w=[tko[b]])
            P.tt("dve", ko[b][0:rows, 0, :], ko[b][0:rows, 0, :], skb[0:rows, :], ALU.add, r=[trn], w=[tko[b]])
            P.tt("dve", ko[b][0:rows, 1, :], C.bank[bi][0:rows, :], rn[0:rows, :], ALU.mult, r=[C.tb[bi], trn], w=[tko[b]])
            P.dma("sp", kf_out[cg, ft, 0:rows, :, :], ko[b][0:rows, :, :], r=[tko[b]], w=[tout])
        barrier(P, scr, [thb] + ttab + tko)


def stage_conv(nc, P, C, u_full, kf, order, mul_tm, tabC, tabS, invC, invS, out, mode, tin, tout):
    v_sb = P.sb([128, 32, 512], BF16, name="v_sb")
    tv = Tok()
    tC = [P.sb([128, 32, 128], BF16, name=f"tC{i}") for i in range(2)]
    tS = [P.sb([128, 32, 128], BF16, name=f"tS{i}") for i in range(2)]
    ttab = [Tok() for _ in range(2)]
    kb = [P.sb([128, 2, 512], F32, name=f"kb{i}") for i in range(2)]
    tkb = [Tok() for _ in range(2)]
    Y = P.sb([128, NFT, 2, 512], BF16, name="Ysb")
    tY = Tok()
    t1 = [P.sb([128, 512], F32, name=f"cv1_{i}") for i in range(2)]
    t2 = [P.sb([128, 512], F32, name=f"cv2_{i}") for i in range(2)]
    tt1 = [Tok() for _ in range(2)]
    tt2 = [Tok() for _ in range(2)]
    gC = [P.sb([128, NFT, 128], BF16, name=f"gC{i}") for i in range(2)]
    gS = [P.sb([128, NFT, 128], BF16, name=f"gS{i}") for i in range(2)]
    tg = [Tok() for _ in range(2)]
    xm = [P.sb([128, 512], F32, name=f"xm{i}") for i in range(2)]
    txm = [Tok() for _ in range(2)]
    ob = [P.sb([128, 512], BF16, name=f"cob{i}") for i in range(2)]
    tob = [Tok() for _ in range(2)]
    of = [P.sb([128, 512], F32, name=f"cof{i}") for i in range(2)]
    tof = [Tok() for _ in range(2)]
    ntr = 0
    for ch2 in range(2):
        c0 = ch2 * 512
        P.dma("sp", v_sb[:, :, :], u_full[:, c0:c0 + 512].rearrange("(m p) c -> p m c", p=128), r=tin, w=[tv])

        def load_f(ft):
            b = ft % 2
            P.dma("sp", tC[b][:, :, :], tabC[ft, :, :, :], r=tin, w=[ttab[b]])
            P.dma("sp", tS[b][:, :, :], tabS[ft, :, :, :], r=tin, w=[ttab[b]])
            rows = 128 if ft < NFT - 1 else 1
            for ri in range(2):
                P.dma("sp", kb[b][0:rows, ri, :].rearrange("f (k c) -> f k c", c=128),
                      kf[ch2 * 4:(ch2 + 1) * 4, ft, 0:rows, ri, order * 128:(order + 1) * 128].rearrange("k f c -> f k c"),
                      r=tin, w=[tkb[b]])
        load_f(0)
        for ft in range(NFT):
            if ft + 1 < NFT:
                load_f(ft + 1)
            b = ft % 2
            rows = 128 if ft < NFT - 1 else 1
            br, bi = (ft % 2) * 2, (ft % 2) * 2 + 1
            for mt in range(32):
                P.mm(C.bank[br][0:rows, :], tC[b][:, mt, 0:rows], v_sb[:, mt, :], mt == 0, mt == 31, r=[ttab[b], tv], w=[C.tb[br]])
            for mt in range(32):
                P.mm(C.bank[bi][0:rows, :], tS[b][:, mt, 0:rows], v_sb[:, mt, :], mt == 0, mt == 31, r=[ttab[b], tv], w=[C.tb[bi]])
            Vr, Vi = C.bank[br][0:rows, :], C.bank[bi][0:rows, :]
            Kr, Ki = kb[b][0:rows, 0, :], kb[b][0:rows, 1, :]
            P.tt("dve", t1[0][0:rows, :], Vr, Kr, ALU.mult, r=[C.tb[br], tkb[b]], w=[tt1[0]])
            P.tt("dve", t2[0][0:rows, :], Vi, Ki, ALU.mult, r=[C.tb[bi], tkb[b]], w=[tt2[0]])
            P.tt("pool", Y[0:rows, ft, 0, :], t1[0][0:rows, :], t2[0][0:rows, :], ALU.add, r=[tt1[0], tt2[0]], w=[tY])
            P.tt("dve", t1[1][0:rows, :], Vr, Ki, ALU.mult, r=[C.tb[br], tkb[b]], w=[tt1[1]])
            P.tt("dve", t2[1][0:rows, :], Vi, Kr, ALU.mult, r=[C.tb[bi], tkb[b]], w=[tt2[1]])
            P.tt("pool", Y[0:rows, ft, 1, :], t1[1][0:rows, :], t2[1][0:rows, :], ALU.subtract, r=[tt1[1], tt2[1]], w=[tY])

        def load_g(tt_):
            b = tt_ % 2
            P.dma("sp", gC[b][:, :, :], invC[tt_, :, :, :], r=tin, w=[tg[b]])
            P.dma("sp", gS[b][:, :, :], invS[tt_, :, :, :], r=tin, w=[tg[b]])
            P.dma("sp", xm[b][:, :], mul_tm[tt_ * 128:(tt_ + 1) * 128, c0:c0 + 512], r=tin, w=[txm[b]])
        load_g(0)
        for tt_ in range(NT // 128):
            if tt_ + 1 < NT // 128:
                load_g(tt_ + 1)
            b = tt_ % 2
            bo = 4 + tt_ % 2
            for ft in range(NFT):
                rows = 128 if ft < NFT - 1 else 1
                P.mm(C.bank[bo][:, :], gC[b][0:rows, ft, :], Y[0:rows, ft, 0, :], ft == 0, False, r=[tg[b], tY], w=[C.tb[bo]])
                P.mm(C.bank[bo][:, :], gS[b][0:rows, ft, :], Y[0:rows, ft, 1, :], False, ft == NFT - 1, r=[tg[b], tY], w=[C.tb[bo]])
            if mode == "z":
                P.tt("dve", ob[b][:, :], C.bank[bo][:, :], xm[b][:, :], ALU.mult, r=[C.tb[bo], txm[b]], w=[tob[b]])
                P.dma("sp", out[tt_ * 128:(tt_ + 1) * 128, c0:c0 + 512], ob[b][:, :], r=[tob[b]], w=[tout])
            else:
                P.tt("dve", of[b][:, :], C.bank[bo][:, :], xm[b][:, :], ALU.mult, r=[C.tb[bo], txm[b]], w=[tof[b]])
                bk = 6 + ntr % 2
                ntr += 1
                for q in range(4):
                    P.tr(C.bank[bk][:, q * 128:(q + 1) * 128], of[b][:, q * 128:(q + 1) * 128], C.ident[:, :],
                         r=[tof[b], C.tident], w=[C.tb[bk]])
                P.cp("act", ob[b][:, :], C.bank[bk][:, :], r=[C.tb[bk]], w=[tob[b]])
                P.dma("sp", out[c0:c0 + 512, tt_ * 128:(tt_ + 1) * 128].rearrange("(q p) t -> p q t", p=128),
                      ob[b][:, :].rearrange("p (q t) -> p q t", t=128), r=[tob[b]], w=[tout])


def stage_final_norm(nc, P, C, x, g_vec, out, tin, tout):
    gbc = P.sb([128, D], F32, name="gbcN")
    tg = Tok()
    load_bcast(P, gbc[:, :], g_vec, D, tg)
    xt = [P.sb([128, D], F32, name=f"fx{i}") for i in range(2)]
    xs = [P.sb([128, D], F32, name=f"fs{i}") for i in range(2)]
    ss = [P.sb([128, 1], F32, name=f"fss{i}") for i in range(2)]
    rs = [P.sb([128, 1], F32, name=f"frs{i}") for i in range(2)]
    tx = [Tok() for _ in range(2)]
    txs = [Tok() for _ in range(2)]
    tst = [Tok() for _ in range(2)]
    for i in range(NT // 128):
        b = i % 2
        P.dma("sp", xt[b][:, :], x[i * 128:(i + 1) * 128, :], r=tin, w=[tx[b]])
        rms_rows(P, C, xt[b], 128, ss[b], rs[b], xs[b], tx[b], txs[b], tst[b])
        P.stt("dve", xs[b][:, :], xt[b][:, :], rs[b][:, 0:1], gbc[:, :], ALU.mult, ALU.mult, r=[tx[b], tst[b], tg], w=[txs[b]])
        P.dma("sp", out[i * 128:(i + 1) * 128, :], xs[b][:, :], r=[txs[b]], w=[tout])


def _new():
    nc = bass.Bass("TRN2", target_bir_lowering=False)

    def dt(n, s, t, k="ExternalInput"):
        return nc.dram_tensor(n, s, t, kind=k).ap()
    return nc, dt


def build_filter():
    nc, dt = _new()
    identd = dt("ident", [128, 128], F32)
    zT = dt("zT", [33, L], F32)
    dec4 = dt("dec4", [L, 512], F32)
    fw1 = dt("fw1", [2, 33, 64], F32)
    fb1 = dt("fb1", [2, 64], F32)
    fw2 = dt("fw2", [2, 64, 64], F32)
    fb2 = dt("fb2", [2, 64], F32)
    fw3c = dt("fw3c", [2, 64, 512], F32)
    ffreq = dt("ffreq", [2, 2, 64], F32)
    fskip = dt("fskip", [512], F32)
    tabC = dt("tabC", [NFT, 128, 32, 128], BF16)
    tabS = dt("tabS", [NFT, 128, 32, 128], BF16)
    kf_out = dt("kf_out", [2, NFP, 512], F32, "ExternalOutput")
    P = Prog(nc)
    C = Ctx(P, identd)
    tout = Tok()
    stage_filter(nc, P, C, zT, dec4, fw1, fb1, fw2, fb2, fw3c, ffreq, fskip, tabC, tabS, kf_out, [], tout)
    P.finalize(final_reads=[tout])
    return nc


def build_ab_in():
    nc, dt = _new()
    identd = dt("ident", [128, 128], F32)
    xext = dt("xext", [NT + 2, D], F32)
    g = dt("g", [D], F32)
    w = dt("w", [D, 5120], F32)
    vnorm = dt("vnorm", [1024], F32)
    a_ws = dt("a_ws", [8, 128, 128], F32)
    a_bs = dt("a_bs", [8, 128], F32)
    bcw = dt("bcw", [3, 3072], F32)
    bcb = dt("bcb", [3072], F32)
    yaT = dt("yaT", [1024, NT], BF16, "ExternalOutput")
    v_tm = dt("v_tm", [NT, 1024], BF16, "ExternalOutput")
    x1_tm = dt("x1_tm", [NT, 1024], F32, "ExternalOutput")
    x2_tm = dt("x2_tm", [NT, 1024], F32, "ExternalOutput")
    P = Prog(nc)
    C = Ctx(P, identd)
    tout = Tok()
    stage_ab_in(nc, P, C, xext, g, w, vnorm, a_ws, a_bs, bcw, bcb, yaT, v_tm, x1_tm, x2_tm, [], tout)
    P.finalize(final_reads=[tout])
    return nc


def build_conv(order, mode):
    nc, dt = _new()
    identd = dt("ident", [128, 128], F32)
    u = dt("u", [L, 1024], BF16)
    kf = dt("kf", [2, NFP, 2, 1024], F32)
    mul = dt("mul", [NT, 1024], F32)
    tabC = dt("tabC", [NFT, 128, 32, 128], BF16)
    tabS = dt("tabS", [NFT, 128, 32, 128], BF16)
    invC = dt("invC", [16, 128, NFT, 128], BF16)
    invS = dt("invS", [16, 128, NFT, 128], BF16)
    out = dt("out", [NT, 1024] if mode == "z" else [1024, NT], BF16, "ExternalOutput")
    P = Prog(nc)
    C = Ctx(P, identd)
    tout = Tok()
    stage_conv(nc, P, C, u, kf, order, mul, tabC, tabS, invC, invS, out, mode, [], tout)
    P.finalize(final_reads=[tout])
    return nc


def build_cd_in():
    nc, dt = _new()
    identd = dt("ident", [128, 128], F32)
    x = dt("x", [NT, D], F32)
    g = dt("g", [D], F32)
    w = dt("w", [D, 6144], F32)
    cs = dt("cs", [128, NT], F32)
    sn = dt("sn", [128, NT], F32)
    rm = dt("rm", [128, 128], BF16)
    o = {n: dt(n, s, BF16, "ExternalOutput") for n, s in [("qcT", [1024, NT]), ("kcT", [1024, NT]), ("vc", [NT, 1024]),
                                                          ("qdT", [1024, NT]), ("kdT", [1024, NT]), ("vd", [NT, 1024])]}
    P = Prog(nc)
    C = Ctx(P, identd)
    tout = Tok()
    stage_cd_in(nc, P, C, x, g, w, cs, sn, rm, o["qcT"], o["kcT"], o["vc"], o["qdT"], o["kdT"], o["vd"], [], tout)
    P.finalize(final_reads=[tout])
    return nc


def build_attn():
    nc, dt = _new()
    identd = dt("ident", [128, 128], F32)
    qcT = dt("qcT", [1024, NT], BF16)
    kcx = dt("kcx", [1024, NKEXT], BF16)
    vcx = dt("vcx", [NKEXT, 1024], BF16)
    btab = dt("btab", [8, 5, 128, 6, 128], F32)
    qdT = dt("qdT", [1024, NT], BF16)
    kdT = dt("kdTf", [1024, L], BF16)
    vdF = dt("vdF", [L, 1024], BF16)
    dlam = dt("dlam", [4, 64], F32)
    lamc = dt("lamc", [2], F32)
    subln = dt("subln", [128], F32)
    yT = dt("yT", [D, NT], BF16, "ExternalOutput")
    P = Prog(nc)
    C = Ctx(P, identd)
    tout = Tok()
    stage_attn(nc, P, C, qcT, kcx, vcx, btab, qdT, kdT, vdF, dlam, lamc, subln, yT, [], tout)
    P.finalize(final_reads=[tout])
    return nc


def build_ffn():
    nc, dt = _new()
    identd = dt("ident", [128, 128], F32)
    yT = dt("yT", [D, NT + 2], BF16)
    xext = dt("xext", [NT + 2, D], F32)
    w_out = dt("w_out", [D, D], F32)
    g = dt("g", [D], F32)
    w_up = dt("w_up", [D, 2 * FF], F32)
    cw = dt("cw", [3, 2 * FF], F32)
    cb = dt("cb", [2 * FF], F32)
    w_down = dt("w_down", [FF, D], F32)
    xmid = dt("xmid", [NT + 2, D], F32, "Internal")
    xout = dt("xout", [NT, D], F32, "ExternalOutput")
    P = Prog(nc)
    C = Ctx(P, identd)
    tout = Tok()
    stage_ffn(nc, P, C, yT, xext, w_out, g, w_up, cw, cb, w_down, xmid, xout, [], tout)
    P.finalize(final_reads=[tout])
    return nc


def build_final():
    nc, dt = _new()
    identd = dt("ident", [128, 128], F32)
    x = dt("x", [NT, D], F32)
    g = dt("g", [D], F32)
    out = dt("out", [NT, D], F32, "ExternalOutput")
    P = Prog(nc)
    C = Ctx(P, identd)
    tout = Tok()
    stage_final_norm(nc, P, C, x, g, out, [], tout)
    P.finalize(final_reads=[tout])
    return nc


PAIRS = [[0, 1], [2, 3], [4, 5], [6, 7]]
QUADS = [[0, 1, 2, 3], [4, 5, 6, 7]]
CROSS = [[0, 4], [1, 5], [2, 6], [3, 7]]
ARENA_BYTES = 206 * 1024


def build_fused(nlayers=4, final=True, do_filter=True, dbg_kf=False):
    global _LIVE_TOKS
    _LIVE_TOKS = []
    nc, dt = _new()
    it = lambda n, s, t: dt(n, s, t, "Internal")
    used = []

    def ein(n, s, t):
        used.append(n)
        return dt(n, s, t)
    identd = ein("ident", [128, 128], F32)
    x = ein("x", [NT, D], F32)
    LW = lambda nm, s: [ein(f"{nm}_L{l}", s, F32) if l < nlayers else None for l in range(4)]
    EW = lambda nm, s: [ein(f"{nm}_L{i}", s, F32) if 2 * i < nlayers else None for i in range(2)]
    OW = lambda nm, s: [ein(f"{nm}_L{i}", s, F32) if 2 * i + 1 < nlayers else None for i in range(2)]
    norm_mix = LW("norm_mix", [D])
    norm_ffn = LW("norm_ffn", [D])
    w_out = LW("w_out", [D, D])
    ffn_up = LW("ffn_up", [D, 2 * FF])
    ffn_cw = LW("ffn_conv_w", [3, 2 * FF])
    ffn_cb = LW("ffn_conv_b", [2 * FF])
    ffn_down = LW("ffn_down", [FF, D])
    final_norm = ein("final_norm", [D], F32) if final else None
    ab_w_in = EW("ab_w_in", [D, 5120])
    a_vnorm = EW("a_vnorm", [1024])
    a_ws = EW("a_ws", [8, 128, 128])
    a_bs = EW("a_bs", [8, 128])
    b_conv_w = EW("b_conv_w", [3, 3072])
    b_conv_b = EW("b_conv_b", [3072])
    cd_w_in = OW("cd_w_in", [D, 6144])
    d_lambda = OW("d_lambda", [4, 64])
    d_subln = OW("d_subln", [128])
    btab = OW("btab", [8, 5, 128, 6, 128])
    lamc = OW("lamc", [2])
    if do_filter:
        fw1 = ein("b_filt_w1", [2, 33, 64], F32)
        fb1 = ein("b_filt_b1", [2, 64], F32)
        fw2 = ein("b_filt_w2", [2, 64, 64], F32)
        fb2 = ein("b_filt_b2", [2, 64], F32)
        ffreq = ein("b_filt_freq", [2, 2, 64], F32)
        fw3r = ein("fw3r", [2, 8, 64, 512], F32)
        fskipr = ein("fskipr", [8, 512], F32)
        dec = ein("dec", [L, 1024], F32)
        zT = ein("zT", [33, L], F32)
    if do_filter or nlayers > 0:
        tabC = ein("tabC", [NFT, 128, 32, 128], BF16)
        tabS = ein("tabS", [NFT, 128, 32, 128], BF16)
    if nlayers > 0:
        invC = ein("invC", [16, 128, NFT, 128], BF16)
        invS = ein("invS", [16, 128, NFT, 128], BF16)
    if nlayers > 1:
        cs = ein("cs", [128, NT], F32)
        sn = ein("sn", [128, NT], F32)
        rm = ein("rm", [128, 128], BF16)
    mskd = ein("msk", [128, 2], F32)
    out = dt("out", [NT, D], F32, "ExternalOutput")
    nc._used_inputs = used
    XE = [it("XEa", [NT + 2, D], F32), it("XEb", [NT + 2, D], F32)]
    xmid = it("xmid", [NT + 2, D], F32)
    yT = it("yT", [D, NT], BF16)
    v_tm = it("v_tm", [NT, 1024], BF16)
    v_full = it("v_full", [2 * NT, 1024], BF16)
    z_tm = it("z_tm", [NT, 1024], BF16)
    z_full = it("z_full", [2 * NT, 1024], BF16)
    x1_tm = it("x1_tm", [NT, 1024], F32)
    x2_tm = it("x2_tm", [NT, 1024], F32)
    qcT = it("qcT", [1024, NT], BF16)
    kcT = it("kcT", [1024, NT], BF16)
    qdT = it("qdT", [1024, NT], BF16)
    kdT = it("kdT", [1024, NT], BF16)
    vc = it("vc", [NT, 1024], BF16)
    vd = it("vd", [NT, 1024], BF16)
    kcb = it("kcb", [1024, 512], BF16)
    kcbg = it("kcbg", [2048, 512], BF16)
    vcb = it("vcb", [512, 1024], BF16)
    vcbg = it("vcbg", [1024, 1024], BF16)
    kdTg = it("kdTg", [2048, NT], BF16)
    vdg = it("vdg", [2 * NT, 1024], BF16)
    kcx = it("kcx", [1024, NKEXT], BF16)
    vcx = it("vcx", [NKEXT, 1024], BF16)
    kdTf = it("kdTf", [1024, L], BF16)
    xb = it("xb", [2, D], F32)
    xbg = it("xbg", [4, D], F32)
    kfall = it("kfall", [8, NFT, 128, 2, 512], F32)
    gch = [it(f"gch{i}", [2048, 1024], BF16) for i in range(2)]
    gchT = [it(f"gchT{i}", [1024, NT], BF16) for i in range(2)]
    vdF = it("vdF", [L, 1024], BF16)

    P = Prog(nc)
    C = Ctx(P, identd)
    msk = P.sb([128, 2], F32, name="msk")
    tmsk = Tok("msk", persist=True)
    P.dma("sp", msk[:, :], mskd[:, :], w=[tmsk])
    P.use_arena(ARENA_BYTES)
    txb = Tok("xb", persist=True)
    txbg = Tok("xbg", persist=True)

    def hx(X, tX):
        thb = Tok()
        P.dma("sp", xb[0:1, :], X[1:2, :], r=[tX], w=[txb])
        P.dma("sp", xb[1:2, :], X[NT:NT + 1, :], r=[tX], w=[txb])
        P.allgather(PAIRS, xb, xbg, r=[txb], w=[txbg])
        hb = P.sb([128, 2, 16], F32, name="hb")
        spread = lambda row: row.rearrange("o (p f) -> (o p) f", p=128)
        P.dma("sp", hb[:, 0, :], spread(xbg[1:2, :]), r=[txbg], w=[thb])
        P.dma("sp", hb[:, 1, :], spread(xbg[2:3, :]), r=[txbg], w=[thb])
        P.ts("dve", hb[:, 0, :], hb[:, 0, :], msk[:, 0:1], None, ALU.mult, r=[thb, tmsk], w=[thb])
        P.ts("dve", hb[:, 1, :], hb[:, 1, :], msk[:, 1:2], None, ALU.mult, r=[thb, tmsk], w=[thb])
        P.dma("sp", spread(X[0:1, :]), hb[:, 0, :], r=[thb], w=[tX])
        P.dma("sp", spread(X[NT + 1:NT + 2, :]), hb[:, 1, :], r=[thb], w=[tX])

    tX = Tok("XE", persist=True)
    for i in range(16):
        P.dma("sp", XE[0][1 + i * 128:1 + (i + 1) * 128, :], x[i * 128:(i + 1) * 128, :], w=[tX])
    hx(XE[0], tX)
    P.stage_end()
    tkl = Tok("kfl")
    if do_filter:
        stage_filter(nc, P, C, zT, dec, fw1, fb1, fw2, fb2, fw3r, ffreq, fskipr, tabC, tabS, kfall, [], tkl)
        P.stage_end()

    def gather_tm(src, dst):
        t1, t2 = [Tok(), Tok()], Tok()
        for ch in range(2):
            P.allgather(PAIRS, src[ch * 1024:(ch + 1) * 1024, :], gch[ch], r=[], w=[t1[ch]], chain=(ch == 0))
        for ch in range(2):
            for r_ in range(2):
                P.dma("sp", dst[r_ * NT + ch * 1024:r_ * NT + (ch + 1) * 1024, :], gch[ch][r_ * 1024:(r_ + 1) * 1024, :],
                      r=t1, w=[t2])
        return t2

    def gather_fm(src, dst):
        t1, t2 = [Tok(), Tok()], Tok()
        for ch in range(2):
            P.allgather(PAIRS, src[ch * 512:(ch + 1) * 512, :], gchT[ch], r=[], w=[t1[ch]], chain=(ch == 0))
        for ch in range(2):
            for r_ in range(2):
                P.dma("sp", dst[ch * 512:(ch + 1) * 512, r_ * NT:(r_ + 1) * NT], gchT[ch][r_ * 512:(r_ + 1) * 512, :],
                      r=t1, w=[t2])
        return t2
    cur = 0
    for l in range(nlayers):
        i = l // 2
        Xc, Xn = XE[cur], XE[1 - cur]
        ty = Tok("yT")
        if l % 2 == 0:
            stage_ab_in(nc, P, C, Xc, norm_mix[l], ab_w_in[i], a_vnorm[i], a_ws[i], a_bs[i], b_conv_w[i], b_conv_b[i],
                        yT[0:1024, :], v_tm, x1_tm, x2_tm, [], ty)
            P.stage_end()
            tvf = gather_tm(v_tm, v_full)
            stage_conv(nc, P, C, v_full, kfall, i * 2 + 0, x1_tm, tabC, tabS, invC, invS, z_tm, "z", [tvf], ty)
            P.stage_end()
            tzf = gather_tm(z_tm, z_full)
            stage_conv(nc, P, C, z_full, kfall, i * 2 + 1, x2_tm, tabC, tabS, invC, invS, yT[1024:2048, :], "yb", [tzf], ty)
            P.stage_end()
        else:
            stage_cd_in(nc, P, C, Xc[1:NT + 1, :], norm_mix[l], cd_w_in[i], cs, sn, rm, qcT, kcT, vc, qdT, kdT, vd, [], ty)
            P.stage_end()
            tg_ = Tok("gath")
            P.dma("sp", kcb[:, 0:256], kcT[:, 0:256], w=[tg_])
            P.dma("sp", kcb[:, 256:512], kcT[:, NT - 256:NT], w=[tg_])
            P.dma("sp", vcb[0:256, :], vc[0:256, :], w=[tg_])
            P.dma("sp", vcb[256:512, :], vc[NT - 256:NT, :], w=[tg_])
            tg2 = Tok("gath2")
            P.allgather(PAIRS, kcb, kcbg, r=[tg_], w=[tg2])
            P.allgather(PAIRS, vcb, vcbg, r=[tg_], w=[tg2])
            tk_ = gather_fm(kdT, kdTf)
            tv_ = gather_tm(vd, vdF)
            tg3 = Tok("gath3")
            P.dma("sp", kcx[:, 0:256], kcbg[0:1024, 256:512], r=[tg2], w=[tg3])
            P.dma("sp", kcx[:, 256:256 + NT], kcT[:, :], r=[tg2], w=[tg3])
            P.dma("sp", kcx[:, 256 + NT:NKEXT], kcbg[1024:2048, 0:256], r=[tg2], w=[tg3])
            P.dma("sp", vcx[0:256, :], vcbg[256:512, :], r=[tg2], w=[tg3])
            P.dma("sp", vcx[256:256 + NT, :], vc[:, :], r=[tg2], w=[tg3])
            P.dma("sp", vcx[256 + NT:NKEXT, :], vcbg[512:768, :], r=[tg2], w=[tg3])
            stage_attn(nc, P, C, qcT, kcx, vcx, btab[i], qdT, kdTf, vdF, d_lambda[i], lamc[i], d_subln[i], yT, [tg3, tk_, tv_], ty)
            P.stage_end()
        last = (l == nlayers - 1)
        stage_ffn(nc, P, C, yT, Xc, w_out[l], norm_ffn[l], ffn_up[l], ffn_cw[l], ffn_cb[l], ffn_down[l], xmid, Xn,
                  [ty], tX, hx, hx_out=(not last and (l + 1) % 2 == 0))
        P.stage_end()
        cur = 1 - cur
    tout = Tok("out")
    if final:
        stage_final_norm(nc, P, C, XE[cur][1:NT + 1, :], final_norm, out, [tX], tout)
    else:
        for i in range(16):
            P.dma("sp", out[i * 128:(i + 1) * 128, :], XE[cur][1 + i * 128:1 + (i + 1) * 128, :], r=[tX], w=[tout])
    P.finalize(final_reads=[tout])
    return nc


def host_inputs(inp, used=None):
    import math
    f32 = lambda a: np.ascontiguousarray(np.asarray(a, dtype=np.float32))
    inp = {k: f32(v) for k, v in inp.items()}
    need = lambda n: used is None or n in used
    shared = {"ident": np.eye(128, dtype=np.float32)}
    if need("tabC"):
        shared["tabC"], shared["tabS"] = dft_fwd_tables()
    inv = [dft_inv_tables(hf) for hf in range(2)] if need("invC") else None
    ropes = [rope_tables(hf) for hf in range(2)]
    shared["rm"] = rot_matrix()
    shared["zT"] = hyena_pos_features()
    shared["fw3r"] = np.ascontiguousarray(inp["b_filt_w3"].reshape(2, 64, 2, 2, 8, 128).transpose(0, 4, 1, 2, 3, 5).reshape(2, 8, 64, 512))
    shared["fskipr"] = np.ascontiguousarray(inp["b_skip"].reshape(2, 2, 8, 128).transpose(2, 0, 1, 3).reshape(8, 512))
    if need("dec"):
        shared["dec"] = hyena_decay_full()
    for k in ["final_norm", "b_filt_w1", "b_filt_b1", "b_filt_w2", "b_filt_b2", "b_filt_freq"]:
        shared[k] = inp[k]
    for k in ["norm_mix", "norm_ffn", "w_out", "ffn_up", "ffn_conv_w", "ffn_conv_b", "ffn_down"]:
        for l in range(4):
            if need(f"{k}_L{l}"):
                shared[f"{k}_L{l}"] = np.ascontiguousarray(inp[k][l])
    for k in ["ab_w_in", "a_vnorm", "a_ws", "a_bs", "b_conv_w", "b_conv_b", "cd_w_in", "d_lambda", "d_subln"]:
        for i in range(2):
            if need(f"{k}_L{i}"):
                shared[f"{k}_L{i}"] = np.ascontiguousarray(inp[k][i])
    for i, l in enumerate((1, 3)):
        li = 0.8 - 0.6 * math.exp(-0.3 * l)
        shared[f"lamc_L{i}"] = np.array([li, 1.0 - li], np.float32)
    btabs = [[na_bias_tables(inp["c_rpb"][i], hf) if need(f"btab_L{i}") else None for i in range(2)] for hf in range(2)]
    maps = []
    for c in range(NCORE):
        b, hf = c // 2, c % 2
        m = dict(shared)
        m["x"] = np.ascontiguousarray(inp["x"][b, hf * NT:(hf + 1) * NT])
        if inv is not None:
            m["invC"], m["invS"] = inv[hf]
        for i in range(2):
            if btabs[hf][i] is not None:
                m[f"btab_L{i}"] = btabs[hf][i]
        m["cs"], m["sn"] = ropes[hf]
        mk = np.zeros((128, 2), np.float32)
        mk[:, 0] = 1.0 if hf == 1 else 0.0
        mk[:, 1] = 1.0 if hf == 0 else 0.0
        m["msk"] = mk
        if used is not None:
            m = {k: v for k, v in m.items() if k in used}
        maps.append(m)
    return maps


def kernel(**inp):
    nc = build_fused()
    maps = host_inputs(inp, set(nc._used_inputs))
    res = run_bass_kernel_spmd(nc, maps, core_ids=list(range(NCORE))).results
    out = np.zeros((4, L, D), np.float32)
    for c in range(NCORE):
        out[c // 2, (c % 2) * NT:(c % 2 + 1) * NT] = res[c]["out"]
    return out
```

```python
import numpy as np
from contextlib import ExitStack
import ml_dtypes
import concourse.bass as bass
import concourse.mybir as mybir
from concourse.bass_utils import run_bass_kernel_spmd

F32 = mybir.dt.float32
BF16 = mybir.dt.bfloat16
AF = mybir.ActivationFunctionType
ALU = mybir.AluOpType
AX = mybir.AxisListType
NPBF = ml_dtypes.bfloat16

D = 2048
NT = 2048
L = 4096
FF = 5632
EPS = 1e-6
NCORE = 8


_LIVE_TOKS = []


class Tok:
    __slots__ = ("name", "w", "rs", "persist")

    def __init__(self, name="", persist=False, track=True):
        self.name = name
        self.w = None
        self.rs = []
        self.persist = persist
        if track:
            _LIVE_TOKS.append(self)


class _Op:
    __slots__ = ("eng", "emit", "deps", "dma", "sig", "sem", "val", "cc", "idx")


class Prog:
    ENGS = ("pe", "act", "dve", "pool", "sp")
    NDMA_SEM = 8

    def __init__(self, nc, same_engine_sync=True):
        self.nc = nc
        self.ops = []
        self.stack = ExitStack()
        self.same = same_engine_sync
        self.nt = 0

    def sb(self, shape, dt, name=None):
        self.nt += 1
        if getattr(self, "arena", None) is None:
            return self.stack.enter_context(self.nc.sbuf_tensor(f"sb{self.nt}_{name or ''}", list(shape), dt))
        esz = 2 if dt == BF16 else 4
        n = 1
        for d_ in shape[1:]:
            n *= d_
        nbytes = (n * esz + 63) // 64 * 64
        off = self.aoff
        assert off + nbytes <= self.abytes, f"arena overflow: {name} {shape} off={off} need={nbytes} cap={self.abytes}"
        self.aoff += nbytes
        ap = self.arena[:, off // 4: off // 4 + nbytes // 4]
        if dt != F32:
            ap = ap.bitcast(dt)
        ap = ap[:, 0:n]
        fs = list(shape[1:])
        if len(fs) == 2:
            ap = ap.rearrange("p (a b) -> p a b", b=fs[1])
        elif len(fs) == 3:
            ap = ap.rearrange("p (a b c) -> p a b c", b=fs[1], c=fs[2])
        if shape[0] < 128:
            ap = ap[0:shape[0]]
        return ap

    def use_arena(self, nbytes):
        self.arena = None
        self.fscr = self.sb([128, 8], F32, name="fence_scr")
        self.arena = self.sb([128, nbytes // 4], F32, name="arena_main")
        self.abytes = nbytes
        self.aoff = 0

    def stage_end(self):
        global _LIVE_TOKS
        toks = list(_LIVE_TOKS)
        fs = self.fscr
        self.add("dve", lambda e: e.memset(fs[0:1, 0:1], 0.0), toks, toks)
        tf = Tok("fence", track=False)
        self.add("dve", lambda e: e.memset(fs[0:1, 1:2], 0.0), toks, [tf])
        for e_ in ("pe", "act", "pool", "sp"):
            self.add(e_, None, [tf], ())
        _LIVE_TOKS = [t for t in toks if t.persist]
        self.aoff = 0

    def ps(self, shape, dt, name=None):
        self.nt += 1
        return self.stack.enter_context(self.nc.psum_tensor(f"ps{self.nt}_{name or ''}", list(shape), dt))

    def add(self, eng, emit, reads=(), writes=(), dma=False, cc=False):
        op = _Op()
        op.cc = cc
        op.eng = eng
        op.emit = emit
        op.dma = dma
        op.sig = dma
        op.sem = None
        op.val = 0
        deps = []
        for t in reads:
            if t.w is not None:
                deps.append(t.w)
        for t in writes:
            if t.w is not None:
                deps.append(t.w)
            deps.extend(t.rs)
        for t in reads:
            t.rs.append(op)
        for t in writes:
            t.w = op
            t.rs = []
        seen = set()
        d2 = []
        for d in deps:
            if id(d) in seen or d is op:
                continue
            seen.add(id(d))
            d2.append(d)
        op.deps = d2
        self.ops.append(op)
        return op

    def dma(self, q, out, in_, r=(), w=()):
        return self.add(q, lambda e: e.dma_start(out=out, in_=in_), r, w, dma=True)

    def allgather(self, groups, src, dst, r=(), w=()):
        return self.add("pool", lambda e: e.collective_compute("AllGather", ALU.bypass, replica_groups=groups,
                                                               ins=[src.opt()], outs=[dst.opt()]), r, w, dma=True, cc=True)

    def mm(self, out, lhsT, rhs, start, stop, r=(), w=()):
        return self.add("pe", lambda e: e.matmul(out, lhsT, rhs, start=start, stop=stop), r, w)

    def tr(self, out, in_, ident, r=(), w=()):
        return self.add("pe", lambda e: e.transpose(out, in_, ident), r, w)

    def act(self, out, in_, func, r=(), w=(), **kw):
        return self.add("act", lambda e: e.activation(out=out, in_=in_, func=func, **kw), r, w)

    def ts(self, eng, out, in0, s1, s2, op0, op1=None, r=(), w=(), **kw):
        if op1 is None:
            return self.add(eng, lambda e: e.tensor_scalar(out, in0, s1, s2, op0, **kw), r, w)
        return self.add(eng, lambda e: e.tensor_scalar(out, in0, s1, s2, op0, op1, **kw), r, w)

    def tt(self, eng, out, in0, in1, op, r=(), w=()):
        return self.add(eng, lambda e: e.tensor_tensor(out, in0, in1, op), r, w)

    def stt(self, eng, out, in0, scalar, in1, op0, op1, r=(), w=()):
        return self.add(eng, lambda e: e.scalar_tensor_tensor(out, in0, scalar, in1, op0, op1), r, w)

    def cp(self, eng, out, in_, r=(), w=()):
        if eng == "act":
            return self.add("act", lambda e: e.activation(out=out, in_=in_, func=AF.Copy), r, w)
        return self.add(eng, lambda e: e.tensor_copy(out=out, in_=in_), r, w)

    def _need(self, op, d):
        if d.dma:
            return True
        if d.eng != op.eng:
            return True
        if op.dma:
            return True
        if op.eng == "pe" or op.emit is None:
            return False
        return self.same

    def finalize(self, final_reads=()):
        nc = self.nc
        self.add("sp", None, reads=final_reads)
        per = {e: [] for e in self.ENGS}
        for op in self.ops:
            per[op.eng].append(op)
        for e in self.ENGS:
            for idx, op in enumerate(per[e]):
                op.idx = idx
        for op in self.ops:
            best = {}
            keep = []
            for d in op.deps:
                if d.dma:
                    keep.append(d)
                else:
                    b = best.get(d.eng)
                    if b is None or d.idx > b.idx:
                        best[d.eng] = d
            op.deps = keep + list(best.values())
        for op in self.ops:
            for d in op.deps:
                if self._need(op, d) and not d.dma:
                    d.sig = True
        sems = {}
        for e in self.ENGS:
            sems[e] = self.stack.enter_context(nc.semaphore(f"s_{e}"))
        dsem = {}
        for e in self.ENGS:
            if any(o.dma for o in per[e]):
                dsem[e] = [self.stack.enter_context(nc.semaphore(f"d_{e}{i}")) for i in range(self.NDMA_SEM)]
        ccsem = self.stack.enter_context(nc.semaphore("s_cc"))
        ncc = 0
        prevcc = None
        for e in self.ENGS:
            cnt = 0
            k = 0
            prev = [None] * self.NDMA_SEM
            for op in per[e]:
                if op.cc:
                    ncc += 1
                    op.sem = ccsem
                    op.val = ncc
                    if prevcc is not None:
                        op.deps.append(prevcc)
                    prevcc = op
                elif op.dma:
                    j = k % self.NDMA_SEM
                    op.sem = dsem[e][j]
                    op.val = 16 * (k // self.NDMA_SEM + 1)
                    if prev[j] is not None:
                        op.deps.append(prev[j])
                    prev[j] = op
                    k += 1
                elif op.sig:
                    cnt += 1
                    op.sem = sems[e]
                    op.val = cnt
        block = self.stack.enter_context(nc.Block())

        def make(e):
            def body(eng):
                waited = {}
                for op in per[e]:
                    for d in op.deps:
                        if not self._need(op, d):
                            continue
                        key = id(d.sem)
                        if waited.get(key, 0) >= d.val:
                            continue
                        eng.wait_ge(d.sem, d.val)
                        waited[key] = d.val
                    if op.emit is None:
                        continue
                    ins = op.emit(eng)
                    if op.cc:
                        ins.then_inc(op.sem)
                    elif op.sig:
                        ins.then_inc(op.sem, 16 if op.dma else 1)
            return body

        block.tensor(make("pe"))
        block.scalar(make("act"))
        block.vector(make("dve"))
        block.gpsimd(make("pool"))
        block.sync(make("sp"))
        self.stack.close()


class Ctx:
    def __init__(self, P, ident_dram):
        self.P = P
        self.bank = [P.ps([128, 512], F32, name=f"bank{i}") for i in range(8)]
        self.tb = [Tok(f"bank{i}", persist=True) for i in range(8)]
        self.ident = P.sb([128, 128], F32, name="ident_sb")
        self.tident = Tok("ident", persist=True)
        P.dma("sp", self.ident[:], ident_dram[:, :], w=[self.tident])
        self.rr = 0

    def eng2(self):
        self.rr += 1
        return "act" if self.rr % 2 else "dve"


def rms_rows(P, C, xt, rows, ss, rstd, junk, tx, tjunk, tstat):
    P.act(junk[:rows, :], xt[:rows, :], AF.Square, r=[tx], w=[tjunk, tstat], accum_out=ss[:rows, :])
    P.ts("dve", rstd[:rows, :], ss[:rows, :], 1.0 / D, EPS, ALU.mult, ALU.add, r=[tstat], w=[tstat])
    P.act(rstd[:rows, :], rstd[:rows, :], AF.Sqrt, r=[tstat], w=[tstat])
    P.add("dve", lambda e: e.reciprocal(rstd[:rows, :], rstd[:rows, :]), [tstat], [tstat])


def norm_transpose_tile(P, C, src_rows, tsrc, rows, gbc, tgbc, dst, dstcol, tdst, bufs, banks):
    xt, tx, junk, tjunk, ss, rstd, tstat, xs, txs = bufs
    P.dma("sp", xt[:rows, :], src_rows, r=tsrc, w=[tx])
    rms_rows(P, C, xt, rows, ss, rstd, junk, tx, tjunk, tstat)
    P.stt("dve", xs[:rows, :], xt[:rows, :], rstd[:rows, 0:1], gbc[:rows, :], ALU.mult, ALU.mult,
          r=[tx, tstat, tgbc], w=[txs])
    for q in range(4):
        bk = banks[q % len(banks)]
        for i in range(4):
            k = q * 4 + i
            P.tr(C.bank[bk][:, i * 128:i * 128 + rows], xs[:rows, k * 128:(k + 1) * 128], C.ident[:rows, :rows],
                 r=[txs, C.tident], w=[C.tb[bk]])
        src = C.bank[bk][:, :].rearrange("p (i t) -> p i t", t=128)[:, :, 0:rows]
        P.cp(C.eng2(), dst[:, q * 4:(q + 1) * 4, dstcol:dstcol + rows], src, r=[C.tb[bk]], w=[tdst])


def load_bcast(P, dst, vec_dram, n, tok):
    src = bass.AP(tensor=vec_dram.tensor, offset=vec_dram.offset, ap=[[0, 128], [1, n]])
    P.dma("sp", dst, src, w=[tok])


def load_cols(P, C, dst, vec_dram, ntile, tmp, ttmp, tdst, bank):
    P.dma("sp", tmp[:ntile, :], vec_dram.rearrange("(t p) -> t p", p=128), w=[ttmp])
    P.tr(C.bank[bank][:, 0:ntile], tmp[:ntile, :], C.ident[:ntile, :ntile], r=[ttmp, C.tident], w=[C.tb[bank]])
    P.cp("dve", dst, C.bank[bank][:, 0:ntile], r=[C.tb[bank]], w=[tdst])


def stage_ffn(nc, P, C, yT, xext, w_out, g_ffn, w_up, conv_w, conv_b, w_down, xmid, xout,
              tin, tout, hx, hx_out=True, ntb=2):
    KT = D // 128
    NJ = FF // 128
    TB = NT // ntb
    W3 = (TB + 2) // 3
    assert W3 * 3 == TB + 2
    arenaA = P.sb([128, NJ * TB], BF16, name="arenaA")
    tA = Tok("arenaA")
    mT = arenaA[:, :].rearrange("p (j t) -> p j t", t=TB)
    y_sb = arenaA[:, 0:KT * NT].rearrange("p (k t) -> p k t", t=NT)
    tm = ty = tA
    wu = [P.sb([128, KT, 256], BF16, name=f"wu{i}") for i in range(2)]
    twu = [Tok(f"wu{i}") for i in range(2)]
    xr2 = [P.sb([128, 256], F32, name=f"xr2{i}") for i in range(3)]
    txr2 = [Tok() for i in range(3)]
    xo2 = [P.sb([128, 256], F32, name=f"xo2{i}") for i in range(3)]
    txo2 = [Tok() for i in range(3)]
    txmid = Tok("xmid", persist=True)
    yv = yT.rearrange("(k p) t -> p k t", p=128)
    for k in range(KT):
        P.dma("sp", y_sb[:, k, :], yv[:, k, :], r=tin, w=[ty])
    wov = w_out.rearrange("(k p) n -> p k n", p=128)
    tiles = [(1 + i * 128, 128) for i in range(NT // 128)]
    cnt = 0

    def load_wo(dc):
        b = dc % 2
        P.dma("pool", wu[b][:, :, :], wov[:, :, dc * 256:(dc + 1) * 256], w=[twu[b]])
    load_wo(0)
    for dc in range(8):
        if dc + 1 < 8:
            load_wo(dc + 1)
        b = dc % 2
        for (r0, rows) in tiles:
            bk = cnt % 4
            i3 = cnt % 3
            cnt += 1
            P.dma("sp", xr2[i3][:rows, :], xext[r0:r0 + rows, dc * 256:(dc + 1) * 256], r=tin, w=[txr2[i3]])
            for k in range(KT):
                P.mm(C.bank[bk][:rows, 0:256], y_sb[:, k, r0 - 1:r0 - 1 + rows], wu[b][:, k, :], k == 0, k == KT - 1,
                     r=[ty, twu[b]], w=[C.tb[bk]])
            P.tt("dve", xo2[i3][:rows, :], C.bank[bk][:rows, 0:256], xr2[i3][:rows, :], ALU.add,
                 r=[C.tb[bk], txr2[i3]], w=[txo2[i3]])
            P.dma("sp", xmid[r0:r0 + rows, dc * 256:(dc + 1) * 256], xo2[i3][:rows, :], r=[txo2[i3]], w=[txmid])
    hx(xmid, txmid)
    gbc = P.sb([128, D], F32, name="gbc")
    tgbc = Tok("gbc")
    load_bcast(P, gbc[:, :], g_ffn, D, tgbc)
    NF = 2 * NJ
    cw = P.sb([128, 4, NF], F32, name="cw")
    tcw = Tok("cw")
    tmpv = P.sb([NF, 128], F32, name="tmpv")
    ttmpv = Tok("tmpv")
    for i in range(3):
        load_cols(P, C, cw[:, i, :], conv_w[i, :], NF, tmpv, ttmpv, tcw, 7)
    load_cols(P, C, cw[:, 3, :], conv_b, NF, tmpv, ttmpv, tcw, 7)
    h_sb = P.sb([128, KT, TB + 2], BF16, name="h_sb")
    th = Tok("h_sb")
    nxt = P.sb([128, D], F32, name="nxt")
    nxs = P.sb([128, D], F32, name="nxs")
    nss = P.sb([128, 1], F32, name="nss")
    nrs = P.sb([128, 1], F32, name="nrs")
    tnx, tnxs, tnst = Tok(), Tok(), Tok()
    nbuf = (nxt, tnx, nxs, tnxs, nss, nrs, tnst, nxs, tnxs)
    a_sb = [P.sb([128, TB + 2], F32, name=f"a{i}") for i in range(2)]
    ta = [Tok(f"a{i}") for i in range(2)]
    c1 = [P.sb([128, TB], F32, name=f"c1_{i}") for i in range(2)]
    tc1 = [Tok() for i in range(2)]
    sg = P.sb([128, TB], F32, name="sg")
    tsg = Tok("sg")
    wd = [P.sb([128, 4, 256], BF16, name=f"wd{i}") for i in range(3)]
    twd = [Tok(f"wd{i}") for i in range(3)]
    wuv = w_up.rearrange("(k p) n -> p k n", p=128)
    wdv = w_down.rearrange("(j p) n -> p j n", p=128)

    def load_wu(j):
        b = j % 2
        P.dma("pool", wu[b][:, :, 0:128], wuv[:, :, j * 128:(j + 1) * 128], w=[twu[b]])
        P.dma("pool", wu[b][:, :, 128:256], wuv[:, :, FF + j * 128:FF + (j + 1) * 128], w=[twu[b]])

    for blk in range(ntb):
        w0 = blk * TB
        segs = [(w0, 1, 0)] + [(w0 + 1 + i * 128, 128, 1 + i * 128) for i in range(TB // 128)] + [(w0 + TB + 1, 1, TB + 1)]
        for (r0, rows, col) in segs:
            norm_transpose_tile(P, C, xmid[r0:r0 + rows, :], [txmid], rows, gbc, tgbc, h_sb, col, th, nbuf, [6, 7])
        load_wu(0)
        step = 0
        for j in range(NJ):
            if j + 1 < NJ:
                load_wu(j + 1)
            b = j % 2
            for half in range(2):
                ft = j if half == 0 else NJ + j
                s = step % 2
                step += 1
                bks = [3 * s, 3 * s + 1, 3 * s + 2]
                for c3 in range(3):
                    for k in range(KT):
                        P.mm(C.bank[bks[c3]][:, 0:W3], wu[b][:, k, half * 128:(half + 1) * 128],
                             h_sb[:, k, c3 * W3:(c3 + 1) * W3], k == 0, k == KT - 1,
                             r=[twu[b], th], w=[C.tb[bks[c3]]])
                for c3 in range(3):
                    P.cp("act", a_sb[s][:, c3 * W3:(c3 + 1) * W3], C.bank[bks[c3]][:, 0:W3],
                         r=[C.tb[bks[c3]]], w=[ta[s]])
                a = a_sb[s]
                P.ts("dve", c1[s][:, :], a[:, 1:TB + 1], cw[:, 1, ft:ft + 1], cw[:, 3, ft:ft + 1], ALU.mult, ALU.add,
                     r=[ta[s], tcw], w=[tc1[s]])
                P.stt("dve", c1[s][:, :], a[:, 0:TB], cw[:, 0, ft:ft + 1], c1[s][:, :], ALU.mult, ALU.add,
                      r=[ta[s], tcw], w=[tc1[s]])
                P.stt("dve", c1[s][:, :], a[:, 2:TB + 2], cw[:, 2, ft:ft + 1], c1[s][:, :], ALU.mult, ALU.add,
                      r=[ta[s], tcw], w=[tc1[s]])
                if half == 0:
                    P.act(sg[:, :], c1[s][:, :], AF.Silu, r=[tc1[s]], w=[tsg])
                else:
                    P.tt("dve", mT[:, j, :], c1[s][:, :], sg[:, :], ALU.mult, r=[tc1[s], tsg], w=[tm])
        NU = NJ // 4
        ucnt = 0
        units = [(dc, u) for dc in range(8) for u in range(NU)]

        def load_wd(idx):
            dc, u = units[idx]
            b3 = idx % 3
            P.dma("pool", wd[b3][:, :, :], wdv[:, u * 4:(u + 1) * 4, dc * 256:(dc + 1) * 256], w=[twd[b3]])
        load_wd(0)
        load_wd(1)
        for idx, (dc, u) in enumerate(units):
            if idx + 2 < len(units):
                load_wd(idx + 2)
            b3 = idx % 3
            for tt_ in range(TB // 128):
                bk = tt_
                co = 0
                for jj in range(4):
                    j = u * 4 + jj
                    P.mm(C.bank[bk][:, co:co + 256], mT[:, j, tt_ * 128:(tt_ + 1) * 128], wd[b3][:, jj, :],
                         j == 0, j == NJ - 1, r=[tm, twd[b3]], w=[C.tb[bk]])
            if u == NU - 1:
                for tt_ in range(TB // 128):
                    bk = tt_
                    co = 0
                    i3 = ucnt % 3
                    ucnt += 1
                    r0 = w0 + 1 + tt_ * 128
                    P.dma("sp", xr2[i3][:, :], xmid[r0:r0 + 128, dc * 256:(dc + 1) * 256], r=[txmid], w=[txr2[i3]])
                    P.tt("dve", xo2[i3][:, :], C.bank[bk][:, co:co + 256], xr2[i3][:, :], ALU.add,
                         r=[C.tb[bk], txr2[i3]], w=[txo2[i3]])
                    P.dma("sp", xout[r0:r0 + 128, dc * 256:(dc + 1) * 256], xo2[i3][:, :],
                          r=[txo2[i3]], w=[tout])
    if hx_out:
        hx(xout, tout)


class WStream:
    def __init__(self, P, KT, units, nbuf=3, name="ws", width=256):
        self.P = P
        self.buf = [P.sb([128, KT, width], BF16, name=f"{name}{i}") for i in range(nbuf)]
        self.tok = [Tok(f"{name}{i}") for i in range(nbuf)]
        self.units = units
        self.nbuf = nbuf
        self.issued = 0

    def get(self, i):
        while self.issued < len(self.units) and self.issued <= i + self.nbuf - 1:
            j = self.issued
            wview, c0, ncols = self.units[j]
            b = j % self.nbuf
            self.P.dma("pool", self.buf[b][:, :, 0:ncols], wview[:, :, c0:c0 + ncols], w=[self.tok[b]])
            self.issued += 1
        return self.buf[i % self.nbuf], self.tok[i % self.nbuf]


def norm_all(P, C, x, tin, g_vec, h_sb, th, ntok, banks=(6, 7)):
    gbc = P.sb([128, D], F32, name="gbc")
    tgbc = Tok("gbc")
    load_bcast(P, gbc[:, :], g_vec, D, tgbc)
    bufs = []
    for i in range(2):
        xt = P.sb([128, D], F32, name=f"nxt{i}")
        xs = P.sb([128, D], F32, name=f"nxs{i}")
        ss = P.sb([128, 1], F32, name=f"nss{i}")
        rs = P.sb([128, 1], F32, name=f"nrs{i}")
        t1, t2, t3 = Tok(), Tok(), Tok()
        bufs.append((xt, t1, xs, t2, ss, rs, t3, xs, t2))
    for i in range(ntok // 128):
        norm_transpose_tile(P, C, x[i * 128:(i + 1) * 128, :], tin, 128, gbc, tgbc, h_sb, i * 128, th,
                            bufs[i % 2], list(banks))


def stage_cd_in(nc, P, C, x, g_mix, w_in, cosT, sinT, rmat, qcT, kcT, vc, qdT, kdT, vd, tin, tout):
    KT = D // 128
    h_sb = P.sb([128, KT, NT], BF16, name="h_sb")
    th = Tok("h_sb")
    norm_all(P, C, x, tin, g_mix, h_sb, th, NT)
    cs = P.sb([128, NT], F32, name="cos_sb")
    sn = P.sb([128, NT], F32, name="sin_sb")
    rm = P.sb([128, 128], BF16, name="rm_sb")
    tcs = Tok("cs")
    P.dma("sp", cs[:, :], cosT[:, :], w=[tcs])
    P.dma("sp", sn[:, :], sinT[:, :], w=[tcs])
    P.dma("sp", rm[:, :], rmat[:, :], w=[tcs])
    wv = w_in.rearrange("(k p) n -> p k n", p=128)
    fm = [(0, qcT, False), (1024, kcT, False), (3072, qdT, True), (4096, kdT, True)]
    units = [(wv, col0 + u * 256, 256) for (col0, _, _) in fm for u in range(4)]
    units += [(wv, col0 + u * 256, 256) for col0 in (2048, 5120) for u in range(4)]
    ws = WStream(P, KT, units, 3)
    ui = 0
    ob = [P.sb([128, 512], BF16, name=f"ob{i}") for i in range(3)]
    tob = [Tok() for _ in range(3)]
    xb = [P.sb([128, 512], BF16, name=f"xb{i}") for i in range(2)]
    txb = [Tok() for _ in range(2)]
    t1 = [P.sb([128, 512], F32, name=f"rt1{i}") for i in range(2)]
    tt1 = [Tok() for _ in range(2)]
    t2 = [P.sb([128, 512], F32, name=f"rt2{i}") for i in range(2)]
    tt2 = [Tok() for _ in range(2)]
    nb = 0
    no = 0
    nx = 0
    for (col0, dst, rot) in fm:
        for u in range(4):
            wb, tw = ws.get(ui)
            ui += 1
            for m in range(2):
                r0 = u * 256 + m * 128
                for tb in range(NT // 512):
                    bk = nb % 4
                    nb += 1
                    for k in range(KT):
                        P.mm(C.bank[bk][:, :], wb[:, k, m * 128:(m + 1) * 128], h_sb[:, k, tb * 512:(tb + 1) * 512],
                             k == 0, k == KT - 1, r=[tw, th], w=[C.tb[bk]])
                    o = no % 3
                    no += 1
                    if not rot:
                        P.cp(C.eng2(), ob[o][:, :], C.bank[bk][:, :], r=[C.tb[bk]], w=[tob[o]])
                    else:
                        xi = nx % 2
                        nx += 1
                        bk2 = 4 + xi
                        P.cp("act", xb[xi][:, :], C.bank[bk][:, :], r=[C.tb[bk]], w=[txb[xi]])
                        P.mm(C.bank[bk2][:, :], rm[:, :], xb[xi][:, :], True, True, r=[tcs, txb[xi]], w=[C.tb[bk2]])
                        P.tt("dve", t1[xi][:, :], xb[xi][:, :], cs[:, tb * 512:(tb + 1) * 512], ALU.mult,
                             r=[txb[xi], tcs], w=[tt1[xi]])
                        P.tt("dve", t2[xi][:, :], C.bank[bk2][:, :], sn[:, tb * 512:(tb + 1) * 512], ALU.mult,
                             r=[C.tb[bk2], tcs], w=[tt2[xi]])
                        P.tt("pool", ob[o][:, :], t1[xi][:, :], t2[xi][:, :], ALU.add, r=[tt1[xi], tt2[xi]], w=[tob[o]])
                    P.dma("sp", dst[r0:r0 + 128, tb * 512:(tb + 1) * 512], ob[o][:, :], r=[tob[o]], w=[tout])
    obt = [P.sb([128, 256], BF16, name=f"obt{i}") for i in range(3)]
    tobt = [Tok() for _ in range(3)]
    for (col0, dst) in [(2048, vc), (5120, vd)]:
        for u in range(4):
            wb, tw = ws.get(ui)
            ui += 1
            for tt_ in range(NT // 128):
                bk = nb % 4
                nb += 1
                for k in range(KT):
                    P.mm(C.bank[bk][:, 0:256], h_sb[:, k, tt_ * 128:(tt_ + 1) * 128], wb[:, k, :],
                         k == 0, k == KT - 1, r=[tw, th], w=[C.tb[bk]])
                o = no % 3
                no += 1
                P.cp(C.eng2(), obt[o][:, :], C.bank[bk][:, 0:256], r=[C.tb[bk]], w=[tobt[o]])
                P.dma("sp", dst[tt_ * 128:(tt_ + 1) * 128, u * 256:(u + 1) * 256], obt[o][:, :], r=[tobt[o]], w=[tout])


NA_KTS = {0: [0, 1, 2, 3, 4, 5], 1: [1, 2, 3, 4, 5], 14: [14, 15, 16, 17, 18], 15: [14, 15, 16, 17, 18, 19]}
NA_CLS = {0: 0, 1: 1, 14: 3, 15: 4}
NKEXT = 2560


def na_kts(i):
    return NA_KTS.get(i, [i, i + 1, i + 2, i + 3, i + 4])


def stage_attn(nc, P, C, qcT, kcx, vcx, btab, qdT, kdT, vdF, dlam, lamc, subln, yT, tin, tout, mid_hook=None):
    tin_d = tin
    ones_t = Tok("aug")
    lp = P.sb([128, 256], F32, name="lp")
    tlp = Tok("lp")
    load_bcast(P, lp[:, :], dlam.rearrange("a b -> (a b)"), 256, tlp)
    lc = P.sb([128, 2], F32, name="lc")
    load_bcast(P, lc[:, :], lamc, 2, tlp)
    ltmp = P.sb([128, 128], F32, name="ltmp")
    lsum = P.sb([128, 4], F32, name="lsum")
    P.tt("dve", ltmp[:, 0:64], lp[:, 0:64], lp[:, 64:128], ALU.mult, r=[tlp], w=[tlp])
    P.tt("dve", ltmp[:, 64:128], lp[:, 128:192], lp[:, 192:256], ALU.mult, r=[tlp], w=[tlp])
    P.add("dve", lambda e: e.reduce_sum(lsum[:, 0:1], ltmp[:, 0:64], AX.X), [tlp], [tlp])
    P.add("dve", lambda e: e.reduce_sum(lsum[:, 1:2], ltmp[:, 64:128], AX.X), [tlp], [tlp])
    P.act(lsum[:, 0:2], lsum[:, 0:2], AF.Exp, r=[tlp], w=[tlp])
    P.tt("dve", lsum[:, 2:3], lsum[:, 0:1], lsum[:, 1:2], ALU.subtract, r=[tlp], w=[tlp])
    P.tt("dve", lsum[:, 2:3], lsum[:, 2:3], lc[:, 0:1], ALU.add, r=[tlp], w=[tlp])
    P.ts("dve", lsum[:, 3:4], lsum[:, 2:3], -1.0, None, ALU.mult, r=[tlp], w=[tlp])
    neglam = lsum[:, 3:4]
    slb = P.sb([128, 128], F32, name="slb")
    load_bcast(P, slb[:, :], subln, 128, tlp)
    P.ts("dve", slb[:, :], slb[:, :], lc[:, 1:2], None, ALU.mult, r=[tlp], w=[tlp])

    ysb = [P.sb([128, NT], BF16, name=f"ysb{i}") for i in range(2)]
    tys = [Tok() for _ in range(2)]
    ot = [P.sb([128, 128], F32, name=f"ot{i}") for i in range(2)]
    tot = [Tok() for _ in range(2)]
    st = [P.sb([128, 4], F32, name=f"st{i}") for i in range(2)]
    tst = [Tok() for _ in range(2)]
    nsm = 0
    ntr = 0
    nys = 0
    trbanks = [6, 7]

    def finish_tile(o_ap, h_ysb, col, ttok_src):
        nonlocal ntr
        bk = trbanks[ntr % len(trbanks)]
        ntr += 1
        P.tr(C.bank[bk][:, 0:128], o_ap, C.ident[:, :], r=[ttok_src, C.tident], w=[C.tb[bk]])
        P.cp(C.eng2(), ysb[h_ysb][:, col:col + 128], C.bank[bk][:, 0:128], r=[C.tb[bk]], w=[tys[h_ysb]])

    SC_C = 128.0 ** -0.5
    kc = [P.sb([128, NKEXT], BF16, name=f"kc{i}") for i in range(2)]
    qc = [P.sb([128, NT], BF16, name=f"qc{i}") for i in range(2)]
    vca = [P.sb([128, NKEXT // 128, 129], BF16, name=f"vca{i}") for i in range(2)]
    thd = [Tok() for _ in range(2)]
    for i in range(2):
        P.add("pool", lambda e, i=i: e.memset(vca[i][:, :, 128:129], 1.0), (), [thd[i]])
    bint = [P.sb([128, 6, 128], F32, name=f"bint{i}") for i in range(2)]
    bedge = [P.sb([128, 6, 128], F32, name=f"bedge{i}") for i in range(2)]
    tbe = [Tok() for _ in range(2)]
    ssb = [P.sb([128, 6, 128], F32, name=f"ssb{i}") for i in range(2)]
    tss = [Tok() for _ in range(2)]
    pT = [P.sb([128, 6, 128], BF16, name=f"pT{i}") for i in range(2)]
    tpT = [Tok() for _ in range(2)]
    nedge = 0
    nit = 0
    for h in range(8):
        hb = h % 2
        P.dma("sp", kc[hb][:, :], kcx[h * 128:(h + 1) * 128, :], r=tin, w=[thd[hb]])
        P.dma("sp", qc[hb][:, :], qcT[h * 128:(h + 1) * 128, :], r=tin, w=[thd[hb]])
        P.dma("sp", vca[hb][:, :, 0:128], vcx[:, h * 128:(h + 1) * 128].rearrange("(t p) d -> p t d", p=128),
              r=tin, w=[thd[hb]])
        P.dma("sp", bint[hb][:, :, :], btab[h, 2, :, :, :], r=tin, w=[thd[hb]])
        yb_ = nys % 2
        nys += 1
        def na_stage1(i):
            nonlocal nedge, nit
            kts = na_kts(i)
            nk = len(kts)
            cls = NA_CLS.get(i, 2)
            if cls == 2:
                btile, tbt = bint[hb], thd[hb]
            else:
                eb = nedge % 2
                nedge += 1
                P.dma("sp", bedge[eb][:, :, :], btab[h, cls, :, :, :], r=tin, w=[tbe[eb]])
                btile, tbt = bedge[eb], tbe[eb]
            it = nit % 2
            nit += 1
            sb0 = 2 * it
            for jj, kt in enumerate(kts):
                bk = sb0 + jj // 4
                co = (jj % 4) * 128
                P.mm(C.bank[bk][:, co:co + 128], kc[hb][:, kt * 128:(kt + 1) * 128], qc[hb][:, i * 128:(i + 1) * 128],
                     True, True, r=[thd[hb]], w=[C.tb[bk]])
            n0 = min(nk, 4)
            P.stt("dve", ssb[it][:, 0:n0, :], C.bank[sb0][:, 0:n0 * 128].rearrange("p (j q) -> p j q", q=128), SC_C,
                  btile[:, 0:n0, :], ALU.mult, ALU.add, r=[C.tb[sb0], tbt], w=[tss[it]])
            if nk > 4:
                P.stt("dve", ssb[it][:, 4:nk, :], C.bank[sb0 + 1][:, 0:(nk - 4) * 128].rearrange("p (j q) -> p j q", q=128),
                      SC_C, btile[:, 4:nk, :], ALU.mult, ALU.add, r=[C.tb[sb0 + 1], tbt], w=[tss[it]])
            P.act(pT[it][:, 0:nk, :], ssb[it][:, 0:nk, :], AF.Exp, r=[tss[it]], w=[tpT[it]])
            return (i, it, kts)

        def na_stage2(st_):
            nonlocal nsm
            i, it, kts = st_
            nk = len(kts)
            bo = 4 + it
            for jj, kt in enumerate(kts):
                P.mm(C.bank[bo][:, 0:129], pT[it][:, jj, :], vca[hb][:, kt, :], jj == 0, jj == nk - 1,
                     r=[tpT[it], thd[hb]], w=[C.tb[bo]])
            sm = nsm % 2
            nsm += 1
            P.add("dve", lambda e, sm=sm, bo=bo: e.reciprocal(st[sm][:, 0:1], C.bank[bo][:, 128:129]),
                  [C.tb[bo]], [tst[sm]])
            P.ts("dve", ot[sm][:, :], C.bank[bo][:, 0:128], st[sm][:, 0:1], None, ALU.mult,
                 r=[C.tb[bo], tst[sm]], w=[tot[sm]])
            finish_tile(ot[sm][:, :], yb_, i * 128, tot[sm])
        prev_st = None
        for i in range(16):
            cur_st = na_stage1(i)
            if prev_st is not None:
                na_stage2(prev_st)
            prev_st = cur_st
        na_stage2(prev_st)
        P.dma("sp", yT[h * 128:(h + 1) * 128, :], ysb[yb_][:, :], r=[tys[yb_]], w=[tout])

    if mid_hook is not None:
        tin_d = mid_hook()
    SC_D = 64.0 ** -0.5
    kd = [P.sb([128, L], BF16, name=f"kd{i}") for i in range(2)]
    qd = [[P.sb([128, NT], BF16, name=f"qd{m}_{i}") for i in range(2)] for m in range(2)]
    vdh = [P.sb([128, L // 128, 128], BF16, name=f"vdh{i}") for i in range(2)]
    thd2 = [Tok() for _ in range(2)]
    for i in range(2):
        P.add("pool", lambda e, i=i: e.memset(qd[0][i][64:128, :], 0.0), (), [thd2[i]])
        P.add("pool", lambda e, i=i: e.memset(qd[1][i][0:64, :], 0.0), (), [thd2[i]])
    onesb = P.sb([128, 128], BF16, name="onesb")
    onesf = P.sb([128, 128], F32, name="onesf")
    tones = Tok()
    P.add("pool", lambda e: e.memset(onesb[:, :], 1.0), (), [tones])
    P.add("pool", lambda e: e.memset(onesf[:, :], 1.0), (), [tones])
    slc = P.sb([128, 1], F32, name="slc")
    P.dma("sp", slc[:, :], subln.rearrange("(p o) -> p o", o=1), r=tin, w=[tlp])
    P.ts("dve", slc[:, :], slc[:, :], lc[:, 1:2], None, ALU.mult, r=[tlp], w=[tlp])
    NPD = 4
    pD = [P.sb([128, 512], BF16, name=f"pD{i}") for i in range(NPD)]
    tpD = [Tok() for _ in range(NPD)]
    wa = P.sb([128, 512], F32, name="wa")
    wb_ = P.sb([128, 512], F32, name="wb")
    wc = P.sb([128, 512], F32, name="wc")
    twa, twb, twc = Tok(), Tok(), Tok()
    npd = 0
    nsb = 0
    NKT = L // 128
    for h in range(8):
        hb = h % 2
        P.dma("sp", kd[hb][:, :], kdT[h * 128:(h + 1) * 128, :], r=tin_d, w=[thd2[hb]])
        P.dma("sp", qd[0][hb][0:64, :], qdT[h * 128:h * 128 + 64, :], r=tin_d, w=[thd2[hb]])
        P.dma("sp", qd[1][hb][64:128, :], qdT[h * 128 + 64:(h + 1) * 128, :], r=tin_d, w=[thd2[hb]])
        P.dma("sp", vdh[hb][:, :, :], vdF[:, h * 128:(h + 1) * 128].rearrange("(t p) d -> p t d", p=128),
              r=tin_d, w=[thd2[hb]])
        yb_ = nys % 2
        nys += 1

        def d_stage1(item):
            nonlocal nsb, npd
            qb, m, kt = item
            bs = 4 + nsb % 3
            nsb += 1
            P.mm(C.bank[bs][:, :], kd[hb][:, kt * 128:(kt + 1) * 128],
                 qd[m][hb][:, qb * 512:(qb + 1) * 512], True, True,
                 r=[thd2[hb]], w=[C.tb[bs]])
            pi = npd % NPD
            npd += 1
            P.act(pD[pi][:, :], C.bank[bs][:, :], AF.Exp, r=[C.tb[bs]], w=[tpD[pi]], scale=SC_D)
            return (item, pi)

        def d_stage2(st_):
            (qb, m, kt), pi = st_
            bo, bz = 2 * m, 2 * m + 1
            P.mm(C.bank[bo][:, :], vdh[hb][:, kt, :], pD[pi][:, :], kt == 0, kt == NKT - 1,
                 r=[tpD[pi], thd2[hb]], w=[C.tb[bo]])
            P.mm(C.bank[bz][:, :], onesb[:, :], pD[pi][:, :], kt == 0, kt == NKT - 1,
                 r=[tpD[pi], tones], w=[C.tb[bz]])
            if kt != NKT - 1 or m == 0:
                return
            P.add("dve", lambda e: e.reciprocal(wa[:, :], C.bank[1][:, :]), [C.tb[1]], [twa])
            P.tt("dve", wa[:, :], C.bank[0][:, :], wa[:, :], ALU.mult, r=[C.tb[0], twa], w=[twa])
            P.add("dve", lambda e: e.reciprocal(wb_[:, :], C.bank[3][:, :]), [C.tb[3]], [twb])
            P.tt("dve", wb_[:, :], C.bank[2][:, :], wb_[:, :], ALU.mult, r=[C.tb[2], twb], w=[twb])
            P.stt("dve", wa[:, :], wb_[:, :], neglam, wa[:, :], ALU.mult, ALU.add, r=[twb, twa, tlp], w=[twa])
            P.tt("pool", wc[:, :], wa[:, :], wa[:, :], ALU.mult, r=[twa], w=[twc])
            P.mm(C.bank[7][:, :], onesf[:, :], wc[:, :], True, True, r=[twc, tones], w=[C.tb[7]])
            P.ts("dve", wb_[:, :], C.bank[7][:, :], 1.0 / 128, EPS, ALU.mult, ALU.add, r=[C.tb[7]], w=[twb])
            P.act(wb_[:, :], wb_[:, :], AF.Sqrt, r=[twb], w=[twb])
            P.add("dve", lambda e: e.reciprocal(wb_[:, :], wb_[:, :]), [twb], [twb])
            P.tt("dve", wa[:, :], wa[:, :], wb_[:, :], ALU.mult, r=[twa, twb], w=[twa])
            P.ts("dve", ysb[yb_][:, qb * 512:(qb + 1) * 512], wa[:, :], slc[:, 0:1], None, ALU.mult,
                 r=[twa, tlp], w=[tys[yb_]])
        items = [(qb, m, kt) for qb in range(NT // 512) for m in range(2) for kt in range(NKT)]
        pend = []
        for item in items:
            pend.append(d_stage1(item))
            if len(pend) > 2:
                d_stage2(pend.pop(0))
        while pend:
            d_stage2(pend.pop(0))
        P.dma("sp", yT[1024 + h * 128:1024 + (h + 1) * 128, :], ysb[yb_][:, :], r=[tys[yb_]], w=[tout])


def rope_tables(hf):
    d = np.arange(128) % 64 % 32
    inv = (1.0 / (10000.0 ** (d.astype(np.float32) * 2.0 / 64.0))).astype(np.float32)
    pos = (hf * NT + np.arange(NT)).astype(np.float32)
    ang = pos[None, :] * inv[:, None]
    return np.cos(ang).astype(np.float32), np.sin(ang).astype(np.float32)


def rot_matrix():
    rm = np.zeros((128, 128), np.float32)
    for pp in range(128):
        if pp % 64 < 32:
            rm[pp + 32, pp] = -1.0
        else:
            rm[pp - 32, pp] = 1.0
    return rm.astype(NPBF)


def na_bias_tables(rpb, hf):
    out = np.full((8, 5, 128, 6, 128), -30000.0, np.float32)
    p = np.arange(128)
    for ci, i in enumerate([0, 1, 2, 14, 15]):
        kts = na_kts(i)
        q = np.arange(128)
        r = 32 * hf + 2 * i + q // 64
        c = q % 64
        rs = np.clip(r - 4, 0, 56)
        cs = np.clip(c - 8, 0, 48)
        for jj, kt in enumerate(kts):
            gr = 32 * hf - 4 + 2 * kt + p // 64
            cp = p % 64
            valid = ((gr[:, None] >= 0) & (gr[:, None] < 64) & (gr[:, None] >= rs[None, :]) & (gr[:, None] < rs[None, :] + 8)
                     & (cp[:, None] >= cs[None, :]) & (cp[:, None] < cs[None, :] + 16))
            rr = np.clip(gr[:, None] - r[None, :] + 7, 0, 14)
            rc = np.clip(cp[:, None] - c[None, :] + 15, 0, 30)
            vals = rpb[:, rr, rc]
            out[:, ci, :, jj, :] = np.where(valid[None], vals, -30000.0)
    return out


def ext_keys(full_tm, hf):
    out = np.zeros((NKEXT, full_tm.shape[1]), full_tm.dtype)
    lo = hf * NT - 256
    a, b = max(lo, 0), min(lo + NKEXT, L)
    out[a - lo:b - lo] = full_tm[a:b]
    return out


class Arena:
    def __init__(self, P, nbytes, name="arena"):
        self.t = P.sb([128, nbytes // 4], F32, name=name)
        self.nbytes = nbytes

    def view(self, off, shape, dt):
        esz = 2 if dt == BF16 else 4
        n = 1
        for s in shape:
            n *= s
        assert off % 4 == 0 and (n * esz) % 4 == 0 and off + n * esz <= self.nbytes, (off, shape, self.nbytes)
        ap = self.t[:, off // 4: off // 4 + (n * esz) // 4]
        if dt != F32:
            ap = ap.bitcast(dt)
        if len(shape) == 2:
            ap = ap.rearrange("p (a b) -> p a b", b=shape[1])
        elif len(shape) == 3:
            ap = ap.rearrange("p (a b c) -> p a b c", b=shape[1], c=shape[2])
        return ap


def barrier(P, scratch, toks):
    P.add("dve", lambda e: e.memset(scratch[0:1, 0:1], 0.0), toks, toks)


def stage_ab_in(nc, P, C, xext, g_mix, w_in, vnorm, a_ws, a_bs, bcw, bcb, yaT, v_tm, x1_tm, x2T, tin, tout):
    KT = D // 128
    NW = NT + 2
    W5 = NW // 5
    assert W5 * 5 == NW
    scr = P.sb([128, 1], F32, name="scr")
    h_sb = P.sb([128, KT, NW], BF16, name="h_sb")
    th = Tok("h_sb")
    A = Arena(P, 64 * 1024 + 1024, "arenaE")
    gbc = A.view(0, [D], F32)
    xt = A.view(8192, [D], F32)
    xs = A.view(16384, [D], F32)
    tg, tx, txs, tstat = Tok(), Tok(), Tok(), Tok()
    load_bcast(P, gbc, g_mix, D, tg)
    nss = P.sb([128, 1], F32, name="nss")
    nrs = P.sb([128, 1], F32, name="nrs")
    nbuf = (xt, tx, xs, txs, nss, nrs, tstat, xs, txs)
    segs = [(0, 1, 0)] + [(1 + i * 128, 128, 1 + i * 128) for i in range(NT // 128)] + [(NT + 1, 1, NT + 1)]
    for (r0, rows, col) in segs:
        norm_transpose_tile(P, C, xext[r0:r0 + rows, :], tin, rows, gbc, tg, h_sb, col, th, nbuf, [6, 7])
    cw = P.sb([128, 4, 24], F32, name="cwE")
    tcw = Tok("cwE")
    tmpv = P.sb([24, 128], F32, name="tmpvE")
    ttmpv = Tok()
    for i in range(3):
        load_cols(P, C, cw[:, i, :], bcw[i, :], 24, tmpv, ttmpv, tcw, 7)
    load_cols(P, C, cw[:, 3, :], bcb, 24, tmpv, ttmpv, tcw, 7)
    wv = w_in.rearrange("(k p) n -> p k n", p=128)
    units = [(wv, 2048 + u * 256, 256) for u in range(12)] + [(wv, u * 256, 256) for u in range(4)]
    ws = WStream(P, KT, units, 3)
    a_sb = [A.view(i * 8200, [NW], F32) for i in range(2)]
    cb_ = [A.view(16400 + i * 8192, [NT], F32) for i in range(2)]
    ta = [Tok() for _ in range(2)]
    tc = [Tok() for _ in range(2)]
    barrier(P, scr, [tg, tx, txs] + ta + tc)
    otr = [P.sb([128, 512], F32, name=f"otr{i}") for i in range(2)]
    totr = [Tok() for _ in range(2)]
    otb = [P.sb([128, 512], BF16, name=f"otb{i}") for i in range(2)]
    totb = [Tok() for _ in range(2)]
    ntr = 0
    for u in range(12):
        wb, tw = ws.get(u)
        for m in range(2):
            ch = u * 2 + m
            s = ch % 2
            for c5 in range(5):
                for k in range(KT):
                    P.mm(C.bank[c5][:, 0:W5], wb[:, k, m * 128:(m + 1) * 128], h_sb[:, k, c5 * W5:(c5 + 1) * W5],
                         k == 0, k == KT - 1, r=[tw, th], w=[C.tb[c5]])
            for c5 in range(5):
                P.cp("act", a_sb[s][:, c5 * W5:(c5 + 1) * W5], C.bank[c5][:, 0:W5], r=[C.tb[c5]], w=[ta[s]])
            a = a_sb[s]
            c = cb_[s]
            P.ts("dve", c[:, :], a[:, 1:NT + 1], cw[:, 1, ch:ch + 1], cw[:, 3, ch:ch + 1], ALU.mult, ALU.add,
                 r=[ta[s], tcw], w=[tc[s]])
            P.stt("dve", c[:, :], a[:, 0:NT], cw[:, 0, ch:ch + 1], c[:, :], ALU.mult, ALU.add, r=[ta[s], tcw], w=[tc[s]])
            P.stt("dve", c[:, :], a[:, 2:NT + 2], cw[:, 2, ch:ch + 1], c[:, :], ALU.mult, ALU.add, r=[ta[s], tcw], w=[tc[s]])
            if True:
                dst = v_tm if ch < 8 else (x1_tm if ch < 16 else x2T)
                c0 = (ch % 8) * 128
                for t4 in range(NT // 512):
                    bk = 5 + ntr % 3
                    o = ntr % 2
                    ntr += 1
                    for q in range(4):
                        tt_ = t4 * 4 + q
                        P.tr(C.bank[bk][:, q * 128:(q + 1) * 128], c[:, tt_ * 128:(tt_ + 1) * 128], C.ident[:, :],
                             r=[tc[s], C.tident], w=[C.tb[bk]])
                    src = C.bank[bk][:, :].rearrange("p (q c) -> p q c", c=128)
                    if ch < 8:
                        P.cp("act", otb[o][:, :].rearrange("p (q c) -> p q c", c=128), src, r=[C.tb[bk]], w=[totb[o]])
                        P.dma("sp", dst[t4 * 512:(t4 + 1) * 512, c0:c0 + 128].rearrange("(q p) c -> p q c", p=128),
                              otb[o][:, :].rearrange("p (q c) -> p q c", c=128), r=[totb[o]], w=[tout])
                    else:
                        P.cp("act", otr[o][:, :].rearrange("p (q c) -> p q c", c=128), src, r=[C.tb[bk]], w=[totr[o]])
                        P.dma("sp", dst[t4 * 512:(t4 + 1) * 512, c0:c0 + 128].rearrange("(q p) c -> p q c", p=128),
                              otr[o][:, :].rearrange("p (q c) -> p q c", c=128), r=[totr[o]], w=[tout])
    wvb = A.view(0, [KT, 1024], BF16)
    u_sb = A.view(32768, [8, NT], BF16)
    twv, tu = Tok(), Tok()
    barrier(P, scr, ta + tc + [twv, tu])
    for u in range(4):
        P.dma("pool", wvb[:, :, u * 256:(u + 1) * 256], wv[:, :, 1024 + u * 256:1024 + (u + 1) * 256], w=[twv])
    bsb = P.sb([128, 8, 128], F32, name="bsb")
    tbs = Tok()
    load_bcast(P, bsb[:, :, :].rearrange("p g q -> p (g q)"), a_bs.rearrange("g q -> (g q)"), 1024, tbs)
    vgb = P.sb([128, 1024], F32, name="vgb")
    load_bcast(P, vgb[:, :], vnorm, 1024, tbs)
    wsT = P.sb([128, 8, 128], BF16, name="wsT")
    wtmp = P.sb([128, 128], F32, name="wtmp")
    twt = Tok()
    for g in range(8):
        P.dma("sp", wtmp[:, :], a_ws[g, :, :], w=[twt])
        P.tr(C.bank[7][:, 0:128], wtmp[:, :], C.ident[:, :], r=[twt, C.tident], w=[C.tb[7]])
        P.cp("dve", wsT[:, g, :], C.bank[7][:, 0:128], r=[C.tb[7]], w=[tbs])
    nb = 0
    for u in range(4):
        wb, tw = ws.get(12 + u)
        for m in range(2):
            fc = u * 2 + m
            for tb in range(NT // 512):
                bk = nb % 4
                nb += 1
                for k in range(KT):
                    P.mm(C.bank[bk][:, :], wb[:, k, m * 128:(m + 1) * 128], h_sb[:, k, 1 + tb * 512:1 + (tb + 1) * 512],
                         k == 0, k == KT - 1, r=[tw, th], w=[C.tb[bk]])
                P.act(u_sb[:, fc, tb * 512:(tb + 1) * 512], C.bank[bk][:, :], AF.Gelu, r=[C.tb[bk]], w=[tu])
    vt = [P.sb([128, 1024], F32, name=f"vt{i}") for i in range(2)]
    tvt = [Tok() for _ in range(2)]
    vn = [P.sb([128, 1024], BF16, name=f"vn{i}") for i in range(2)]
    tvn = [Tok() for _ in range(2)]
    vj = P.sb([128, 1024], BF16, name="vj")
    tvj = Tok()
    vst = [P.sb([128, 2], F32, name=f"vst{i}") for i in range(2)]
    tvs = [Tok() for _ in range(2)]
    sg_ = [P.sb([128, 8, 128], F32, name=f"sg{i}") for i in range(2)]
    tsg = [Tok() for _ in range(2)]
    for ck in range(NT // 128):
        s = ck % 2
        for half in range(2):
            bk = (ck % 2) * 2 + half
            for k in range(KT):
                P.mm(C.bank[bk][:, :], h_sb[:, k, 1 + ck * 128:1 + (ck + 1) * 128], wvb[:, k, half * 512:(half + 1) * 512],
                     k == 0, k == KT - 1, r=[twv, th], w=[C.tb[bk]])
            P.act(vt[s][:, half * 512:(half + 1) * 512], C.bank[bk][:, :], AF.Gelu, r=[C.tb[bk]], w=[tvt[s]])
        P.act(vj[:, :], vt[s][:, :], AF.Square, r=[tvt[s]], w=[tvj, tvs[s]], accum_out=vst[s][:, 0:1])
        P.ts("dve", vst[s][:, 1:2], vst[s][:, 0:1], 1.0 / 1024, EPS, ALU.mult, ALU.add, r=[tvs[s]], w=[tvs[s]])
        P.act(vst[s][:, 1:2], vst[s][:, 1:2], AF.Sqrt, r=[tvs[s]], w=[tvs[s]])
        P.add("dve", lambda e, s=s: e.reciprocal(vst[s][:, 1:2], vst[s][:, 1:2]), [tvs[s]], [tvs[s]])
        P.stt("dve", vn[s][:, :], vt[s][:, :], vst[s][:, 1:2], vgb[:, :], ALU.mult, ALU.mult,
              r=[tvt[s], tvs[s], tbs], w=[tvn[s]])
        for g in range(8):
            bk = 4 + (ck % 2) * 2 + g // 4
            P.mm(C.bank[bk][:, (g % 4) * 128:(g % 4 + 1) * 128], vn[s][:, g * 128:(g + 1) * 128], wsT[:, g, :],
                 True, True, r=[tvn[s], tbs], w=[C.tb[bk]])
        for hh in range(2):
            bk = 4 + (ck % 2) * 2 + hh
            P.tt("dve", sg_[s][:, hh * 4:(hh + 1) * 4, :], C.bank[bk][:, :].rearrange("p (g q) -> p g q", q=128),
                 bsb[:, hh * 4:(hh + 1) * 4, :], ALU.add, r=[C.tb[bk], tbs], w=[tsg[s]])
        P.tt("pool", u_sb[:, :, ck * 128:(ck + 1) * 128], u_sb[:, :, ck * 128:(ck + 1) * 128], sg_[s][:, :, :], ALU.mult,
             r=[tsg[s]], w=[tu])
    for g in range(8):
        P.dma("sp", yaT[g * 128:(g + 1) * 128, :], u_sb[:, g, :], r=[tu], w=[tout])


NFT = 33
NFP = NFT * 128
NFFT = 2 * L


def dft_fwd_tables():
    m = (np.arange(32)[None, :, None] * 128 + np.arange(128)[:, None, None]).astype(np.int64)
    outc = np.zeros((NFT, 128, 32, 128), NPBF)
    outs = np.zeros((NFT, 128, 32, 128), NPBF)
    for ft in range(NFT):
        f = (ft * 128 + np.arange(128))[None, None, :].astype(np.int64)
        ph = ((m * f) % NFFT).astype(np.float64) * (2.0 * np.pi / NFFT)
        valid = (f <= L)
        outc[ft] = np.where(valid, np.cos(ph), 0.0).astype(NPBF)
        outs[ft] = np.where(valid, np.sin(ph), 0.0).astype(NPBF)
    return outc, outs


def dft_inv_tables(hf):
    f = (np.arange(NFT)[None, :, None] * 128 + np.arange(128)[:, None, None]).astype(np.int64)
    wf = np.where((f == 0) | (f == L), 1.0, 2.0) / NFFT
    wf = np.where(f <= L, wf, 0.0)
    outc = np.zeros((NT // 128, 128, NFT, 128), NPBF)
    outs = np.zeros((NT // 128, 128, NFT, 128), NPBF)
    for tt in range(NT // 128):
        t = (hf * NT + tt * 128 + np.arange(128))[None, None, :].astype(np.int64)
        ph = ((f * t) % NFFT).astype(np.float64) * (2.0 * np.pi / NFFT)
        outc[tt] = (wf * np.cos(ph)).astype(NPBF)
        outs[tt] = (-wf * np.sin(ph)).astype(NPBF)
    return outc, outs


def hyena_pos_features():
    t = np.linspace(0.0, 1.0, L, dtype=np.float32)[:, None]
    w = (2.0 * np.pi * np.arange(L, dtype=np.float32)[:, None] / L).astype(np.float32)
    bands = np.linspace(1e-4, 15.0, 16, dtype=np.float32)[None, :]
    z = np.concatenate([t, np.cos(w * bands), -np.sin(w * bands)], axis=-1).astype(np.float32)
    return np.ascontiguousarray(z.T)


def hyena_decay_full():
    import math
    t = np.linspace(0.0, 1.0, L, dtype=np.float32)[:, None]
    max_decay = math.log(1e-2) / 0.3
    min_decay = math.log(1e-2) / 1.5
    deltas = np.abs(np.linspace(min_decay, max_decay, 1024, dtype=np.float32))
    return np.ascontiguousarray(np.exp(-t * deltas[None, :]).astype(np.float32))


def hyena_decay(core):
    import math
    t = np.linspace(0.0, 1.0, L, dtype=np.float32)[:, None]
    max_decay = math.log(1e-2) / 0.3
    min_decay = math.log(1e-2) / 1.5
    deltas = np.abs(np.linspace(min_decay, max_decay, 1024, dtype=np.float32))[core * 128:(core + 1) * 128]
    dec = np.exp(-t * deltas[None, :]).astype(np.float32)
    return np.ascontiguousarray(np.tile(dec, (1, 4)))


def stage_filter(nc, P, C, zT, dec, fw1, fb1, fw2, fb2, fw3r, ffreq, fskipr, tabC, tabS, kf_out, tin, tout):
    scr = P.sb([128, 1], F32, name="scrF")
    A_all = P.sb([128, 32, 512], BF16, name="A_all")
    B_all = P.sb([128, 32, 512], BF16, name="B_all")
    tAB = Tok()
    acc = P.sb([128, 512], F32, name="accF")
    tacc = Tok()
    AH = Arena(P, 65536, "arenaH")
    h2 = [AH.view(32768 + i * 16384, [L], F32)[0:64] for i in range(2)]
    th2 = [Tok() for _ in range(2)]
    wsm = P.sb([64, 128], F32, name="wsm")
    w3s = [P.sb([128, 512], BF16, name=f"w3s{i}") for i in range(2)]
    tw3 = [Tok() for _ in range(2)]
    h2b = [P.sb([128, L], BF16, name=f"h2b_{i}") for i in range(2)]
    for i in range(2):
        P.add("pool", lambda e, i=i: e.memset(w3s[i][64:128, :], 0.0), (), [tw3[i]])
        P.add("pool", lambda e, i=i: e.memset(h2b[i][64:128, :], 0.0), (), [th2[i]])
    cols = P.sb([64, 4], F32, name="colsF")
    tws = Tok()
    dsb = P.sb([128, 32, 128], F32, name="dsb")
    tds = Tok()
    tmpa = [P.sb([128, 256], F32, name=f"tmpa{i}") for i in range(2)]
    ttmp = [Tok() for _ in range(2)]
    TWO_PI = 2.0 * np.pi
    qi = P.sb([64, 512], mybir.dt.int32, name="qiF")
    qf = P.sb([64, 512], F32, name="qfF")
    tqi = Tok()
    ones = P.sb([128, 128], F32, name="onesF")
    tones = Tok()
    P.add("dve", lambda e: e.memset(ones[:, :], 1.0), (), [tones])
    rn = P.sb([128, 512], F32, name="rnF")
    trn = Tok()
    skb = P.sb([128, 512], F32, name="skb")
    z_sb = AH.view(0, [L], F32)[0:33]
    h1 = AH.view(16384, [L], F32)[0:64]
    tz, th1 = Tok(), Tok()
    P.dma("sp", z_sb[:, :], zT[:, :], r=tin, w=[tz])
    for li in range(2):
        P.dma("sp", wsm[0:33, 0:64], fw1[li, :, :], r=tin, w=[tws])
        P.dma("sp", wsm[:, 64:128], fw2[li, :, :], r=tin, w=[tws])
        P.dma("sp", cols[:, 0:1], fb1[li, :].rearrange("(p o) -> p o", o=1), r=tin, w=[tws])
        P.dma("sp", cols[:, 1:2], ffreq[li, 0, :].rearrange("(p o) -> p o", o=1), r=tin, w=[tws])
        P.dma("sp", cols[:, 2:3], fb2[li, :].rearrange("(p o) -> p o", o=1), r=tin, w=[tws])
        P.dma("sp", cols[:, 3:4], ffreq[li, 1, :].rearrange("(p o) -> p o", o=1), r=tin, w=[tws])
        for (src, srows, wcol, bcol, dst, tsrc, tdst) in [(z_sb, 33, 0, 0, h1, tz, th1), (h1, 64, 64, 2, h2[li], th1, th2[li])]:
            for cb in range(L // 512):
                bk = cb % 4
                P.mm(C.bank[bk][0:64, :], wsm[0:srows, wcol:wcol + 64], src[0:srows, cb * 512:(cb + 1) * 512], True, True,
                     r=[tws, tsrc], w=[C.tb[bk]])
                dsl = dst[:, cb * 512:(cb + 1) * 512]
                P.ts("dve", dsl, C.bank[bk][0:64, :], cols[:, bcol:bcol + 1],
                     cols[:, bcol + 1:bcol + 2], ALU.add, ALU.mult, r=[C.tb[bk], tws], w=[tdst])
                P.ts("dve", qi[:, :], dsl, 1.0 / TWO_PI, None, ALU.mult, r=[tdst], w=[tqi])
                P.cp("dve", qf[:, :], qi[:, :], r=[tqi], w=[tqi])
                P.stt("dve", dsl, qf[:, :], -TWO_PI, dsl, ALU.mult, ALU.add, r=[tqi], w=[tdst])
                P.ts("dve", dsl, dsl, -3.141592, 3.141592, ALU.max, ALU.min, r=[tdst], w=[tdst])
                P.act(dsl, dsl, AF.Sin, r=[tdst], w=[tdst])
        P.cp("dve", h2b[li][0:64, :], h2[li][:, :], r=[th2[li]], w=[th2[li]])
    hbuf = AH.view(0, [32, 512], F32)
    thb = Tok()
    tC = [AH.view(i * 8192, [32, 128], BF16) for i in range(2)]
    tS = [AH.view(16384 + i * 8192, [32, 128], BF16) for i in range(2)]
    ttab = [Tok() for _ in range(2)]
    ko = [AH.view(32768 + i * 4096, [2, 512], F32) for i in range(2)]
    tko = [Tok() for _ in range(2)]
    barrier(P, scr, [tz, th1, thb] + th2 + ttab + tko)
    for cg in range(8):
        P.add("dve", lambda e: e.memset(acc[:, :], 0.0), (), [tacc])
        load_bcast(P, skb[:, :], fskipr[cg, :], 512, trn)
        P.dma("sp", dsb[:, :, :], dec[:, cg * 128:(cg + 1) * 128].rearrange("(m p) c -> p m c", p=128), r=tin, w=[tds])
        for li in range(2):
            P.dma("pool", w3s[li][0:64, :], fw3r[li, cg, :, :], r=tin, w=[tw3[li]])
            for mt in range(32):
                bk = 4 + mt % 2
                dd = dsb[:, mt, :]
                dbc = bass.AP(tensor=dd.tensor, offset=dd.offset, ap=[list(dd.ap[0]), [0, 4], list(dd.ap[-1])])
                P.mm(C.bank[bk][:, :], h2b[li][:, mt * 128:(mt + 1) * 128], w3s[li][:, :], True, True,
                     r=[th2[li], tw3[li]], w=[C.tb[bk]])
                P.tt("dve", hbuf[:, mt, :].rearrange("p (j c) -> p j c", c=128),
                     C.bank[bk][:, :].rearrange("p (j c) -> p j c", c=128), dbc, ALU.mult,
                     r=[C.tb[bk], tds], w=[thb])
            hv = hbuf.rearrange("p m (o d c) -> p m o d c", o=2, d=2)
            P.add("dve", lambda e, hv=hv: e.memset(hv[0:1, 0, :, 1, :], 0.0), [thb], [thb])
            for mt in range(32):
                ti = mt % 2
                fwd = hv[:, mt, :, 0, :]
                bwd = hv[:, mt, :, 1, :]
                Aout = A_all[:, mt, li * 256:(li + 1) * 256].rearrange("p (o c) -> p o c", o=2)
                Bout = B_all[:, mt, li * 256:(li + 1) * 256].rearrange("p (o c) -> p o c", o=2)
                P.tt("pool", Aout, fwd, bwd, ALU.add, r=[thb], w=[tAB])
                P.tt("pool", Bout, bwd, fwd, ALU.subtract, r=[thb], w=[tAB])
                t3 = tmpa[ti][:, :].rearrange("p (o c) -> p o c", o=2)
                P.act(t3, fwd, AF.Abs, r=[thb], w=[ttmp[ti]])
                P.tt("dve", acc[:, li * 256:(li + 1) * 256], acc[:, li * 256:(li + 1) * 256], tmpa[ti][:, :], ALU.add,
                     r=[ttmp[ti]], w=[tacc])
                P.act(t3, bwd, AF.Abs, r=[thb], w=[ttmp[ti]])
                P.tt("dve", acc[:, li * 256:(li + 1) * 256], acc[:, li * 256:(li + 1) * 256], tmpa[ti][:, :], ALU.add,
                     r=[ttmp[ti]], w=[tacc])
        P.mm(C.bank[6][:, :], ones[:, :], acc[:, :], True, True, r=[tones, tacc], w=[C.tb[6]])
        P.add("dve", lambda e: e.reciprocal(rn[:, :], C.bank[6][:, :]), [C.tb[6]], [trn])
        barrier(P, scr, [thb] + ttab + tko)

        def load_tab(ft):
            b = ft % 2
            P.dma("sp", tC[b][:, :, :], tabC[ft, :, :, :], r=tin, w=[ttab[b]])
            P.dma("sp", tS[b][:, :, :], tabS[ft, :, :, :], r=tin, w=[ttab[b]])
        load_tab(0)
        for ft in range(NFT):
            if ft + 1 < NFT:
                load_tab(ft + 1)
            b = ft % 2
            rows = 128 if ft < NFT - 1 else 1
            br, bi = (ft % 2) * 2, (ft % 2) * 2 + 1
            for mt in range(32):
                P.mm(C.bank[br][0:rows, :], tC[b][:, mt, 0:rows], A_all[:, mt, :], mt == 0, mt == 31, r=[ttab[b], tAB], w=[C.tb[br]])
            for mt in range(32):
                P.mm(C.bank[bi][0:rows, :], tS[b][:, mt, 0:rows], B_all[:, mt, :], mt == 0, mt == 31, r=[ttab[b], tAB], w=[C.tb[bi]])
            P.tt("dve", ko[b][0:rows, 0, :], C.bank[br][0:rows, :], rn[0:rows, :], ALU.mult, r=[C.tb[br], trn], w=[tko[b]])
            P.tt("dve", ko[b][0:rows, 0, :], ko[b][0:rows, 0, :], skb[0:rows, :], ALU.add, r=[trn], w=[tko[b]])
            P.tt("dve", ko[b][0:rows, 1, :], C.bank[bi][0:rows, :], rn[0:rows, :], ALU.mult, r=[C.tb[bi], trn], w=[tko[b]])
            P.dma("sp", kf_out[cg, ft, 0:rows, :, :], ko[b][0:rows, :, :], r=[tko[b]], w=[tout])
        barrier(P, scr, [thb] + ttab + tko)


def stage_conv(nc, P, C, u_full, kf, order, mul_tm, tabC, tabS, invC, invS, out, mode, tin, tout):
    v_sb = P.sb([128, 32, 512], BF16, name="v_sb")
    tv = Tok()
    tC = [P.sb([128, 32, 128], BF16, name=f"tC{i}") for i in range(2)]
    tS = [P.sb([128, 32, 128], BF16, name=f"tS{i}") for i in range(2)]
    ttab = [Tok() for _ in range(2)]
    kb = [P.sb([128, 2, 512], F32, name=f"kb{i}") for i in range(2)]
    tkb = [Tok() for _ in range(2)]
    Y = P.sb([128, NFT, 2, 512], BF16, name="Ysb")
    tY = Tok()
    t1 = [P.sb([128, 512], F32, name=f"cv1_{i}") for i in range(2)]
    t2 = [P.sb([128, 512], F32, name=f"cv2_{i}") for i in range(2)]
    tt1 = [Tok() for _ in range(2)]
    tt2 = [Tok() for _ in range(2)]
    gC = [P.sb([128, NFT, 128], BF16, name=f"gC{i}") for i in range(2)]
    gS = [P.sb([128, NFT, 128], BF16, name=f"gS{i}") for i in range(2)]
    tg = [Tok() for _ in range(2)]
    xm = [P.sb([128, 512], F32, name=f"xm{i}") for i in range(2)]
    txm = [Tok() for _ in range(2)]
    ob = [P.sb([128, 512], BF16, name=f"cob{i}") for i in range(2)]
    tob = [Tok() for _ in range(2)]
    of = [P.sb([128, 512], F32, name=f"cof{i}") for i in range(2)]
    tof = [Tok() for _ in range(2)]
    ntr = 0
    for ch2 in range(2):
        c0 = ch2 * 512
        P.dma("sp", v_sb[:, :, :], u_full[:, c0:c0 + 512].rearrange("(m p) c -> p m c", p=128), r=tin, w=[tv])

        def load_f(ft):
            b = ft % 2
            P.dma("sp", tC[b][:, :, :], tabC[ft, :, :, :], r=tin, w=[ttab[b]])
            P.dma("sp", tS[b][:, :, :], tabS[ft, :, :, :], r=tin, w=[ttab[b]])
            rows = 128 if ft < NFT - 1 else 1
            for ri in range(2):
                P.dma("sp", kb[b][0:rows, ri, :].rearrange("f (k c) -> f k c", c=128),
                      kf[ch2 * 4:(ch2 + 1) * 4, ft, 0:rows, ri, order * 128:(order + 1) * 128].rearrange("k f c -> f k c"),
                      r=tin, w=[tkb[b]])
        load_f(0)
        for ft in range(NFT):
            if ft + 1 < NFT:
                load_f(ft + 1)
            b = ft % 2
            rows = 128 if ft < NFT - 1 else 1
            br, bi = (ft % 2) * 2, (ft % 2) * 2 + 1
            for mt in range(32):
                P.mm(C.bank[br][0:rows, :], tC[b][:, mt, 0:rows], v_sb[:, mt, :], mt == 0, mt == 31, r=[ttab[b], tv], w=[C.tb[br]])
            for mt in range(32):
                P.mm(C.bank[bi][0:rows, :], tS[b][:, mt, 0:rows], v_sb[:, mt, :], mt == 0, mt == 31, r=[ttab[b], tv], w=[C.tb[bi]])
            Vr, Vi = C.bank[br][0:rows, :], C.bank[bi][0:rows, :]
            Kr, Ki = kb[b][0:rows, 0, :], kb[b][0:rows, 1, :]
            P.tt("dve", t1[0][0:rows, :], Vr, Kr, ALU.mult, r=[C.tb[br], tkb[b]], w=[tt1[0]])
            P.tt("dve", t2[0][0:rows, :], Vi, Ki, ALU.mult, r=[C.tb[bi], tkb[b]], w=[tt2[0]])
            P.tt("pool", Y[0:rows, ft, 0, :], t1[0][0:rows, :], t2[0][0:rows, :], ALU.add, r=[tt1[0], tt2[0]], w=[tY])
            P.tt("dve", t1[1][0:rows, :], Vr, Ki, ALU.mult, r=[C.tb[br], tkb[b]], w=[tt1[1]])
            P.tt("dve", t2[1][0:rows, :], Vi, Kr, ALU.mult, r=[C.tb[bi], tkb[b]], w=[tt2[1]])
            P.tt("pool", Y[0:rows, ft, 1, :], t1[1][0:rows, :], t2[1][0:rows, :], ALU.subtract, r=[tt1[1], tt2[1]], w=[tY])

        def load_g(tt_):
            b = tt_ % 2
            P.dma("sp", gC[b][:, :, :], invC[tt_, :, :, :], r=tin, w=[tg[b]])
            P.dma("sp", gS[b][:, :, :], invS[tt_, :, :, :], r=tin, w=[tg[b]])
            P.dma("sp", xm[b][:, :], mul_tm[tt_ * 128:(tt_ + 1) * 128, c0:c0 + 512], r=tin, w=[txm[b]])
        load_g(0)
        for tt_ in range(NT // 128):
            if tt_ + 1 < NT // 128:
                load_g(tt_ + 1)
            b = tt_ % 2
            bo = 4 + tt_ % 2
            for ft in range(NFT):
                rows = 128 if ft < NFT - 1 else 1
                P.mm(C.bank[bo][:, :], gC[b][0:rows, ft, :], Y[0:rows, ft, 0, :], ft == 0, False, r=[tg[b], tY], w=[C.tb[bo]])
                P.mm(C.bank[bo][:, :], gS[b][0:rows, ft, :], Y[0:rows, ft, 1, :], False, ft == NFT - 1, r=[tg[b], tY], w=[C.tb[bo]])
            if mode == "z":
                P.tt("dve", ob[b][:, :], C.bank[bo][:, :], xm[b][:, :], ALU.mult, r=[C.tb[bo], txm[b]], w=[tob[b]])
                P.dma("sp", out[tt_ * 128:(tt_ + 1) * 128, c0:c0 + 512], ob[b][:, :], r=[tob[b]], w=[tout])
            else:
                P.tt("dve", of[b][:, :], C.bank[bo][:, :], xm[b][:, :], ALU.mult, r=[C.tb[bo], txm[b]], w=[tof[b]])
                bk = 6 + ntr % 2
                ntr += 1
                for q in range(4):
                    P.tr(C.bank[bk][:, q * 128:(q + 1) * 128], of[b][:, q * 128:(q + 1) * 128], C.ident[:, :],
                         r=[tof[b], C.tident], w=[C.tb[bk]])
                P.cp("act", ob[b][:, :], C.bank[bk][:, :], r=[C.tb[bk]], w=[tob[b]])
                P.dma("sp", out[c0:c0 + 512, tt_ * 128:(tt_ + 1) * 128].rearrange("(q p) t -> p q t", p=128),
                      ob[b][:, :].rearrange("p (q t) -> p q t", t=128), r=[tob[b]], w=[tout])


def stage_final_norm(nc, P, C, x, g_vec, out, tin, tout):
    gbc = P.sb([128, D], F32, name="gbcN")
    tg = Tok()
    load_bcast(P, gbc[:, :], g_vec, D, tg)
    xt = [P.sb([128, D], F32, name=f"fx{i}") for i in range(2)]
    xs = [P.sb([128, D], F32, name=f"fs{i}") for i in range(2)]
    ss = [P.sb([128, 1], F32, name=f"fss{i}") for i in range(2)]
    rs = [P.sb([128, 1], F32, name=f"frs{i}") for i in range(2)]
    tx = [Tok() for _ in range(2)]
    txs = [Tok() for _ in range(2)]
    tst = [Tok() for _ in range(2)]
    for i in range(NT // 128):
        b = i % 2
        P.dma("sp", xt[b][:, :], x[i * 128:(i + 1) * 128, :], r=tin, w=[tx[b]])
        rms_rows(P, C, xt[b], 128, ss[b], rs[b], xs[b], tx[b], txs[b], tst[b])
        P.stt("dve", xs[b][:, :], xt[b][:, :], rs[b][:, 0:1], gbc[:, :], ALU.mult, ALU.mult, r=[tx[b], tst[b], tg], w=[txs[b]])
        P.dma("sp", out[i * 128:(i + 1) * 128, :], xs[b][:, :], r=[txs[b]], w=[tout])


def _new():
    nc = bass.Bass("TRN2", target_bir_lowering=False)

    def dt(n, s, t, k="ExternalInput"):
        return nc.dram_tensor(n, s, t, kind=k).ap()
    return nc, dt


def build_filter():
    nc, dt = _new()
    identd = dt("ident", [128, 128], F32)
    zT = dt("zT", [33, L], F32)
    dec4 = dt("dec4", [L, 512], F32)
    fw1 = dt("fw1", [2, 33, 64], F32)
    fb1 = dt("fb1", [2, 64], F32)
    fw2 = dt("fw2", [2, 64, 64], F32)
    fb2 = dt("fb2", [2, 64], F32)
    fw3c = dt("fw3c", [2, 64, 512], F32)
    ffreq = dt("ffreq", [2, 2, 64], F32)
    fskip = dt("fskip", [512], F32)
    tabC = dt("tabC", [NFT, 128, 32, 128], BF16)
    tabS = dt("tabS", [NFT, 128, 32, 128], BF16)
    kf_out = dt("kf_out", [2, NFP, 512], F32, "ExternalOutput")
    P = Prog(nc)
    C = Ctx(P, identd)
    tout = Tok()
    stage_filter(nc, P, C, zT, dec4, fw1, fb1, fw2, fb2, fw3c, ffreq, fskip, tabC, tabS, kf_out, [], tout)
    P.finalize(final_reads=[tout])
    return nc


def build_ab_in():
    nc, dt = _new()
    identd = dt("ident", [128, 128], F32)
    xext = dt("xext", [NT + 2, D], F32)
    g = dt("g", [D], F32)
    w = dt("w", [D, 5120], F32)
    vnorm = dt("vnorm", [1024], F32)
    a_ws = dt("a_ws", [8, 128, 128], F32)
    a_bs = dt("a_bs", [8, 128], F32)
    bcw = dt("bcw", [3, 3072], F32)
    bcb = dt("bcb", [3072], F32)
    yaT = dt("yaT", [1024, NT], BF16, "ExternalOutput")
    v_tm = dt("v_tm", [NT, 1024], BF16, "ExternalOutput")
    x1_tm = dt("x1_tm", [NT, 1024], F32, "ExternalOutput")
    x2_tm = dt("x2_tm", [NT, 1024], F32, "ExternalOutput")
    P = Prog(nc)
    C = Ctx(P, identd)
    tout = Tok()
    stage_ab_in(nc, P, C, xext, g, w, vnorm, a_ws, a_bs, bcw, bcb, yaT, v_tm, x1_tm, x2_tm, [], tout)
    P.finalize(final_reads=[tout])
    return nc


def build_conv(order, mode):
    nc, dt = _new()
    identd = dt("ident", [128, 128], F32)
    u = dt("u", [L, 1024], BF16)
    kf = dt("kf", [2, NFP, 2, 1024], F32)
    mul = dt("mul", [NT, 1024], F32)
    tabC = dt("tabC", [NFT, 128, 32, 128], BF16)
    tabS = dt("tabS", [NFT, 128, 32, 128], BF16)
    invC = dt("invC", [16, 128, NFT, 128], BF16)
    invS = dt("invS", [16, 128, NFT, 128], BF16)
    out = dt("out", [NT, 1024] if mode == "z" else [1024, NT], BF16, "ExternalOutput")
    P = Prog(nc)
    C = Ctx(P, identd)
    tout = Tok()
    stage_conv(nc, P, C, u, kf, order, mul, tabC, tabS, invC, invS, out, mode, [], tout)
    P.finalize(final_reads=[tout])
    return nc


def build_cd_in():
    nc, dt = _new()
    identd = dt("ident", [128, 128], F32)
    x = dt("x", [NT, D], F32)
    g = dt("g", [D], F32)
    w = dt("w", [D, 6144], F32)
    cs = dt("cs", [128, NT], F32)
    sn = dt("sn", [128, NT], F32)
    rm = dt("rm", [128, 128], BF16)
    o = {n: dt(n, s, BF16, "ExternalOutput") for n, s in [("qcT", [1024, NT]), ("kcT", [1024, NT]), ("vc", [NT, 1024]),
                                                          ("qdT", [1024, NT]), ("kdT", [1024, NT]), ("vd", [NT, 1024])]}
    P = Prog(nc)
    C = Ctx(P, identd)
    tout = Tok()
    stage_cd_in(nc, P, C, x, g, w, cs, sn, rm, o["qcT"], o["kcT"], o["vc"], o["qdT"], o["kdT"], o["vd"], [], tout)
    P.finalize(final_reads=[tout])
    return nc


def build_attn():
    nc, dt = _new()
    identd = dt("ident", [128, 128], F32)
    qcT = dt("qcT", [1024, NT], BF16)
    kcx = dt("kcx", [1024, NKEXT], BF16)
    vcx = dt("vcx", [NKEXT, 1024], BF16)
    btab = dt("btab", [8, 5, 128, 6, 128], F32)
    qdT = dt("qdT", [1024, NT], BF16)
    kdT = dt("kdTf", [1024, L], BF16)
    vdF = dt("vdF", [L, 1024], BF16)
    dlam = dt("dlam", [4, 64], F32)
    lamc = dt("lamc", [2], F32)
    subln = dt("subln", [128], F32)
    yT = dt("yT", [D, NT], BF16, "ExternalOutput")
    P = Prog(nc)
    C = Ctx(P, identd)
    tout = Tok()
    stage_attn(nc, P, C, qcT, kcx, vcx, btab, qdT, kdT, vdF, dlam, lamc, subln, yT, [], tout)
    P.finalize(final_reads=[tout])
    return nc


def build_ffn():
    nc, dt = _new()
    identd = dt("ident", [128, 128], F32)
    yT = dt("yT", [D, NT + 2], BF16)
    xext = dt("xext", [NT + 2, D], F32)
    w_out = dt("w_out", [D, D], F32)
    g = dt("g", [D], F32)
    w_up = dt("w_up", [D, 2 * FF], F32)
    cw = dt("cw", [3, 2 * FF], F32)
    cb = dt("cb", [2 * FF], F32)
    w_down = dt("w_down", [FF, D], F32)
    xmid = dt("xmid", [NT + 2, D], F32, "Internal")
    xout = dt("xout", [NT, D], F32, "ExternalOutput")
    P = Prog(nc)
    C = Ctx(P, identd)
    tout = Tok()
    stage_ffn(nc, P, C, yT, xext, w_out, g, w_up, cw, cb, w_down, xmid, xout, [], tout)
    P.finalize(final_reads=[tout])
    return nc


def build_final():
    nc, dt = _new()
    identd = dt("ident", [128, 128], F32)
    x = dt("x", [NT, D], F32)
    g = dt("g", [D], F32)
    out = dt("out", [NT, D], F32, "ExternalOutput")
    P = Prog(nc)
    C = Ctx(P, identd)
    tout = Tok()
    stage_final_norm(nc, P, C, x, g, out, [], tout)
    P.finalize(final_reads=[tout])
    return nc


PAIRS = [[0, 1], [2, 3], [4, 5], [6, 7]]
QUADS = [[0, 1, 2, 3], [4, 5, 6, 7]]
CROSS = [[0, 4], [1, 5], [2, 6], [3, 7]]
ARENA_BYTES = 206 * 1024


def build_fused(nlayers=4, final=True, do_filter=True, dbg_kf=False):
    global _LIVE_TOKS
    _LIVE_TOKS = []
    nc, dt = _new()
    it = lambda n, s, t: dt(n, s, t, "Internal")
    used = []

    def ein(n, s, t):
        used.append(n)
        return dt(n, s, t)
    identd = ein("ident", [128, 128], F32)
    x = ein("x", [NT, D], F32)
    LW = lambda nm, s: [ein(f"{nm}_L{l}", s, F32) if l < nlayers else None for l in range(4)]
    EW = lambda nm, s: [ein(f"{nm}_L{i}", s, F32) if 2 * i < nlayers else None for i in range(2)]
    OW = lambda nm, s: [ein(f"{nm}_L{i}", s, F32) if 2 * i + 1 < nlayers else None for i in range(2)]
    norm_mix = LW("norm_mix", [D])
    norm_ffn = LW("norm_ffn", [D])
    w_out = LW("w_out", [D, D])
    ffn_up = LW("ffn_up", [D, 2 * FF])
    ffn_cw = LW("ffn_conv_w", [3, 2 * FF])
    ffn_cb = LW("ffn_conv_b", [2 * FF])
    ffn_down = LW("ffn_down", [FF, D])
    final_norm = ein("final_norm", [D], F32) if final else None
    ab_w_in = EW("ab_w_in", [D, 5120])
    a_vnorm = EW("a_vnorm", [1024])
    a_ws = EW("a_ws", [8, 128, 128])
    a_bs = EW("a_bs", [8, 128])
    b_conv_w = EW("b_conv_w", [3, 3072])
    b_conv_b = EW("b_conv_b", [3072])
    cd_w_in = OW("cd_w_in", [D, 6144])
    d_lambda = OW("d_lambda", [4, 64])
    d_subln = OW("d_subln", [128])
    btab = OW("btab", [8, 5, 128, 6, 128])
    lamc = OW("lamc", [2])
    if do_filter:
        fw1 = ein("b_filt_w1", [2, 33, 64], F32)
        fb1 = ein("b_filt_b1", [2, 64], F32)
        fw2 = ein("b_filt_w2", [2, 64, 64], F32)
        fb2 = ein("b_filt_b2", [2, 64], F32)
        ffreq = ein("b_filt_freq", [2, 2, 64], F32)
        fw3r = ein("fw3r", [2, 8, 64, 512], F32)
        fskipr = ein("fskipr", [8, 512], F32)
        dec = ein("dec", [L, 1024], F32)
        zT = ein("zT", [33, L], F32)
    if do_filter or nlayers > 0:
        tabC = ein("tabC", [NFT, 128, 32, 128], BF16)
        tabS = ein("tabS", [NFT, 128, 32, 128], BF16)
    if nlayers > 0:
        invC = ein("invC", [16, 128, NFT, 128], BF16)
        invS = ein("invS", [16, 128, NFT, 128], BF16)
    if nlayers > 1:
        cs = ein("cs", [128, NT], F32)
        sn = ein("sn", [128, NT], F32)
        rm = ein("rm", [128, 128], BF16)
    mskd = ein("msk", [128, 2], F32)
    out = dt("out", [NT, D], F32, "ExternalOutput")
    nc._used_inputs = used
    XE = [it("XEa", [NT + 2, D], F32), it("XEb", [NT + 2, D], F32)]
    xmid = it("xmid", [NT + 2, D], F32)
    yT = it("yT", [D, NT], BF16)
    v_tm = it("v_tm", [NT, 1024], BF16)
    v_full = it("v_full", [2 * NT, 1024], BF16)
    z_tm = it("z_tm", [NT, 1024], BF16)
    z_full = it("z_full", [2 * NT, 1024], BF16)
    x1_tm = it("x1_tm", [NT, 1024], F32)
    x2_tm = it("x2_tm", [NT, 1024], F32)
    qcT = it("qcT", [1024, NT], BF16)
    kcT = it("kcT", [1024, NT], BF16)
    qdT = it("qdT", [1024, NT], BF16)
    kdT = it("kdT", [1024, NT], BF16)
    vc = it("vc", [NT, 1024], BF16)
    vd = it("vd", [NT, 1024], BF16)
    kcb = it("kcb", [1024, 512], BF16)
    kcbg = it("kcbg", [2048, 512], BF16)
    vcb = it("vcb", [512, 1024], BF16)
    vcbg = it("vcbg", [1024, 1024], BF16)
    kdTg = it("kdTg", [2048, NT], BF16)
    vdg = it("vdg", [2 * NT, 1024], BF16)
    kcx = it("kcx", [1024, NKEXT], BF16)
    vcx = it("vcx", [NKEXT, 1024], BF16)
    kdTf = it("kdTf", [1024, L], BF16)
    xb = it("xb", [2, D], F32)
    xbg = it("xbg", [4, D], F32)
    kfall = it("kfall", [8, NFT, 128, 2, 512], F32)
    gch = [it(f"gch{i}", [2048, 1024], BF16) for i in range(2)]
    gchT = [it(f"gchT{i}", [1024, NT], BF16) for i in range(2)]
    vdF = it("vdF", [L, 1024], BF16)

    P = Prog(nc)
    C = Ctx(P, identd)
    msk = P.sb([128, 2], F32, name="msk")
    tmsk = Tok("msk", persist=True)
    P.dma("sp", msk[:, :], mskd[:, :], w=[tmsk])
    P.use_arena(ARENA_BYTES)
    txb = Tok("xb", persist=True)
    txbg = Tok("xbg", persist=True)

    def hx(X, tX):
        thb = Tok()
        P.dma("sp", xb[0:1, :], X[1:2, :], r=[tX], w=[txb])
        P.dma("sp", xb[1:2, :], X[NT:NT + 1, :], r=[tX], w=[txb])
        P.allgather(PAIRS, xb, xbg, r=[txb], w=[txbg])
        hb = P.sb([128, 2, 16], F32, name="hb")
        spread = lambda row: row.rearrange("o (p f) -> (o p) f", p=128)
        P.dma("sp", hb[:, 0, :], spread(xbg[1:2, :]), r=[txbg], w=[thb])
        P.dma("sp", hb[:, 1, :], spread(xbg[2:3, :]), r=[txbg], w=[thb])
        P.ts("dve", hb[:, 0, :], hb[:, 0, :], msk[:, 0:1], None, ALU.mult, r=[thb, tmsk], w=[thb])
        P.ts("dve", hb[:, 1, :], hb[:, 1, :], msk[:, 1:2], None, ALU.mult, r=[thb, tmsk], w=[thb])
        P.dma("sp", spread(X[0:1, :]), hb[:, 0, :], r=[thb], w=[tX])
        P.dma("sp", spread(X[NT + 1:NT + 2, :]), hb[:, 1, :], r=[thb], w=[tX])

    tX = Tok("XE", persist=True)
    for i in range(16):
        P.dma("sp", XE[0][1 + i * 128:1 + (i + 1) * 128, :], x[i * 128:(i + 1) * 128, :], w=[tX])
    hx(XE[0], tX)
    P.stage_end()
    tkl = Tok("kfl")
    if do_filter:
        stage_filter(nc, P, C, zT, dec, fw1, fb1, fw2, fb2, fw3r, ffreq, fskipr, tabC, tabS, kfall, [], tkl)
        P.stage_end()

    def gather_tm(src, dst):
        t1, t2 = Tok(), Tok()
        for ch in range(2):
            P.allgather(PAIRS, src[ch * 1024:(ch + 1) * 1024, :], gch[ch], r=[], w=[t1])
        for ch in range(2):
            for r_ in range(2):
                P.dma("sp", dst[r_ * NT + ch * 1024:r_ * NT + (ch + 1) * 1024, :], gch[ch][r_ * 1024:(r_ + 1) * 1024, :],
                      r=[t1], w=[t2])
        return t2

    def gather_fm(src, dst):
        t1, t2 = Tok(), Tok()
        for ch in range(2):
            P.allgather(PAIRS, src[ch * 512:(ch + 1) * 512, :], gchT[ch], r=[], w=[t1])
        for ch in range(2):
            for r_ in range(2):
                P.dma("sp", dst[ch * 512:(ch + 1) * 512, r_ * NT:(r_ + 1) * NT], gchT[ch][r_ * 512:(r_ + 1) * 512, :],
                      r=[t1], w=[t2])
        return t2
    cur = 0
    for l in range(nlayers):
        i = l // 2
        Xc, Xn = XE[cur], XE[1 - cur]
        ty = Tok("yT")
        if l % 2 == 0:
            stage_ab_in(nc, P, C, Xc, norm_mix[l], ab_w_in[i], a_vnorm[i], a_ws[i], a_bs[i], b_conv_w[i], b_conv_b[i],
                        yT[0:1024, :], v_tm, x1_tm, x2_tm, [], ty)
            P.stage_end()
            tvf = gather_tm(v_tm, v_full)
            stage_conv(nc, P, C, v_full, kfall, i * 2 + 0, x1_tm, tabC, tabS, invC, invS, z_tm, "z", [tvf], ty)
            P.stage_end()
            tzf = gather_tm(z_tm, z_full)
            stage_conv(nc, P, C, z_full, kfall, i * 2 + 1, x2_tm, tabC, tabS, invC, invS, yT[1024:2048, :], "yb", [tzf], ty)
            P.stage_end()
        else:
            stage_cd_in(nc, P, C, Xc[1:NT + 1, :], norm_mix[l], cd_w_in[i], cs, sn, rm, qcT, kcT, vc, qdT, kdT, vd, [], ty)
            P.stage_end()
            tg_ = Tok("gath")
            P.dma("sp", kcb[:, 0:256], kcT[:, 0:256], w=[tg_])
            P.dma("sp", kcb[:, 256:512], kcT[:, NT - 256:NT], w=[tg_])
            P.dma("sp", vcb[0:256, :], vc[0:256, :], w=[tg_])
            P.dma("sp", vcb[256:512, :], vc[NT - 256:NT, :], w=[tg_])
            tg2 = Tok("gath2")
            P.allgather(PAIRS, kcb, kcbg, r=[tg_], w=[tg2])
            P.allgather(PAIRS, vcb, vcbg, r=[tg_], w=[tg2])
            tg3 = Tok("gath3")
            P.dma("sp", kcx[:, 0:256], kcbg[0:1024, 256:512], r=[tg2], w=[tg3])
            P.dma("sp", kcx[:, 256:256 + NT], kcT[:, :], r=[tg2], w=[tg3])
            P.dma("sp", kcx[:, 256 + NT:NKEXT], kcbg[1024:2048, 0:256], r=[tg2], w=[tg3])
            P.dma("sp", vcx[0:256, :], vcbg[256:512, :], r=[tg2], w=[tg3])
            P.dma("sp", vcx[256:256 + NT, :], vc[:, :], r=[tg2], w=[tg3])
            P.dma("sp", vcx[256 + NT:NKEXT, :], vcbg[512:768, :], r=[tg2], w=[tg3])
            stage_attn(nc, P, C, qcT, kcx, vcx, btab[i], qdT, kdTf, vdF, d_lambda[i], lamc[i], d_subln[i], yT, [tg3], ty,
                       mid_hook=lambda: [gather_fm(kdT, kdTf), gather_tm(vd, vdF)])
            P.stage_end()
        last = (l == nlayers - 1)
        stage_ffn(nc, P, C, yT, Xc, w_out[l], norm_ffn[l], ffn_up[l], ffn_cw[l], ffn_cb[l], ffn_down[l], xmid, Xn,
                  [ty], tX, hx, hx_out=(not last and (l + 1) % 2 == 0))
        P.stage_end()
        cur = 1 - cur
    tout = Tok("out")
    if final:
        stage_final_norm(nc, P, C, XE[cur][1:NT + 1, :], final_norm, out, [tX], tout)
    else:
        for i in range(16):
            P.dma("sp", out[i * 128:(i + 1) * 128, :], XE[cur][1 + i * 128:1 + (i + 1) * 128, :], r=[tX], w=[tout])
    P.finalize(final_reads=[tout])
    return nc


def host_inputs(inp, used=None):
    import math
    f32 = lambda a: np.ascontiguousarray(np.asarray(a, dtype=np.float32))
    inp = {k: f32(v) for k, v in inp.items()}
    need = lambda n: used is None or n in used
    shared = {"ident": np.eye(128, dtype=np.float32)}
    if need("tabC"):
        shared["tabC"], shared["tabS"] = dft_fwd_tables()
    inv = [dft_inv_tables(hf) for hf in range(2)] if need("invC") else None
    ropes = [rope_tables(hf) for hf in range(2)]
    shared["rm"] = rot_matrix()
    shared["zT"] = hyena_pos_features()
    shared["fw3r"] = np.ascontiguousarray(inp["b_filt_w3"].reshape(2, 64, 2, 2, 8, 128).transpose(0, 4, 1, 2, 3, 5).reshape(2, 8, 64, 512))
    shared["fskipr"] = np.ascontiguousarray(inp["b_skip"].reshape(2, 2, 8, 128).transpose(2, 0, 1, 3).reshape(8, 512))
    if need("dec"):
        shared["dec"] = hyena_decay_full()
    for k in ["final_norm", "b_filt_w1", "b_filt_b1", "b_filt_w2", "b_filt_b2", "b_filt_freq"]:
        shared[k] = inp[k]
    for k in ["norm_mix", "norm_ffn", "w_out", "ffn_up", "ffn_conv_w", "ffn_conv_b", "ffn_down"]:
        for l in range(4):
            if need(f"{k}_L{l}"):
                shared[f"{k}_L{l}"] = np.ascontiguousarray(inp[k][l])
    for k in ["ab_w_in", "a_vnorm", "a_ws", "a_bs", "b_conv_w", "b_conv_b", "cd_w_in", "d_lambda", "d_subln"]:
        for i in range(2):
            if need(f"{k}_L{i}"):
                shared[f"{k}_L{i}"] = np.ascontiguousarray(inp[k][i])
    for i, l in enumerate((1, 3)):
        li = 0.8 - 0.6 * math.exp(-0.3 * l)
        shared[f"lamc_L{i}"] = np.array([li, 1.0 - li], np.float32)
    btabs = [[na_bias_tables(inp["c_rpb"][i], hf) if need(f"btab_L{i}") else None for i in range(2)] for hf in range(2)]
    maps = []
    for c in range(NCORE):
        b, hf = c // 2, c % 2
        m = dict(shared)
        m["x"] = np.ascontiguousarray(inp["x"][b, hf * NT:(hf + 1) * NT])
        if inv is not None:
            m["invC"], m["invS"] = inv[hf]
        for i in range(2):
            if btabs[hf][i] is not None:
                m[f"btab_L{i}"] = btabs[hf][i]
        m["cs"], m["sn"] = ropes[hf]
        mk = np.zeros((128, 2), np.float32)
        mk[:, 0] = 1.0 if hf == 1 else 0.0
        mk[:, 1] = 1.0 if hf == 0 else 0.0
        m["msk"] = mk
        if used is not None:
            m = {k: v for k, v in m.items() if k in used}
        maps.append(m)
    return maps


def kernel(**inp):
    nc = build_fused()
    maps = host_inputs(inp, set(nc._used_inputs))
    res = run_bass_kernel_spmd(nc, maps, core_ids=list(range(NCORE))).results
    out = np.zeros((4, L, D), np.float32)
    for c in range(NCORE):
        out[c // 2, (c % 2) * NT:(c % 2 + 1) * NT] = res[c]["out"]
    return out
```
